# Optimizing a Trainium2 kernel written in Bass

```python
import jax
import jax.numpy as jnp
from jax import lax
import numpy as np

D_MODEL = 2048
BATCH = 1
SEQ = 16384
DEPTH = 2
DEC_BATCH = 2
DEC_SEQ = 8192
PAST_LEN = 128

N_MIXERS = 2
N_CONV_LAYERS = (DEPTH + 1) // N_MIXERS
N_MLSTM_LAYERS = DEPTH // N_MIXERS
D_FF = -(-8 * D_MODEL // (3 * 256)) * 256
CONV_WIDTH = 3
MLSTM_HEADS = 4
MLSTM_DQK = D_MODEL // (2 * MLSTM_HEADS)
MLSTM_DV = D_MODEL // MLSTM_HEADS
MLSTM_QK = MLSTM_HEADS * MLSTM_DQK
MLSTM_IN = 2 * MLSTM_QK + 2 * D_MODEL + 4 * MLSTM_HEADS
CHUNK = 64
GATE_CAP = 15.0
EPS = 1e-6

kernel_name = 'hybrid_shortconv_mlstm_adaln_encoder'


def rmsnorm(x, g):
    xf = x.astype(jnp.float32)
    y = xf * lax.rsqrt(jnp.mean(xf * xf, axis=-1, keepdims=True) + EPS)
    return (y * g.astype(jnp.float32)).astype(x.dtype)


def modulate(z, shift, scale):
    return z * (1 + scale) + shift


def softcap(x):
    return GATE_CAP * jnp.tanh(x / GATE_CAP)


def swiglu_ffn(z, w_in, w_out):
    gate, up = jnp.split(z @ w_in, 2, axis=-1)
    return (jax.nn.silu(gate) * up) @ w_out


def conv_mixer(z, w_in, w_conv, w_out):
    d = z.shape[-1]
    bg, cg, h = jnp.split(z @ w_in, 3, axis=-1)
    u = cg * h
    u = lax.conv_general_dilated(
        u, w_conv[:, None, :].astype(u.dtype), window_strides=(1,),
        padding=[((CONV_WIDTH - 1) // 2, (CONV_WIDTH - 1) // 2)],
        dimension_numbers=('NWC', 'WIO', 'NWC'), feature_group_count=d)
    return (bg * u) @ w_out


def _mlstm_chunk_step(carry, xs):
    c_mem, n_mem, m_mem = carry
    q, k, v, ig, b = xs
    L = q.shape[-2]
    causal = jnp.tril(jnp.ones((L, L), dtype=bool))
    g = b[..., -1]
    d_log = b[..., :, None] - b[..., None, :] + ig[..., None, :]
    d_log = jnp.where(causal, d_log, -jnp.inf)
    inter_log = b + m_mem[..., None]
    m_t = jnp.maximum(inter_log, jnp.max(d_log, axis=-1))
    w_inter = jnp.exp(inter_log - m_t)
    s = jnp.einsum('bhtd,bhsd->bhts', q, k) * jnp.exp(d_log - m_t[..., None])
    num = w_inter[..., None] * jnp.einsum('bhtd,bhde->bhte', q, c_mem) + jnp.einsum('bhts,bhse->bhte', s, v)
    den = w_inter * jnp.einsum('bhtd,bhd->bht', q, n_mem) + jnp.sum(s, axis=-1)
    h = num / jnp.maximum(jnp.abs(den), jnp.exp(-m_t))[..., None]
    a = g[..., None] - b + ig
    m_new = jnp.maximum(g + m_mem, jnp.max(a, axis=-1))
    decay = jnp.exp(g + m_mem - m_new)
    wk = jnp.exp(a - m_new[..., None])
    c_new = decay[..., None, None] * c_mem + jnp.einsum('bhs,bhsd,bhse->bhde', wk, k, v)
    n_new = decay[..., None] * n_mem + jnp.einsum('bhs,bhsd->bhd', wk, k)
    return (c_new, n_new, m_new), h


def mlstm_scan(q, k, v, ig, fg):
    bsz, nh, s, dk = q.shape
    dv = v.shape[-1]
    nc = s // CHUNK
    logf = jax.nn.log_sigmoid(fg)
    b = jnp.cumsum(logf.reshape(bsz, nh, nc, CHUNK), axis=-1)

    def chunks(t):
        return jnp.moveaxis(t.reshape(bsz, nh, nc, CHUNK, *t.shape[3:]), 2, 0)

    xs = (chunks(q), chunks(k), chunks(v), chunks(ig), jnp.moveaxis(b, 2, 0))
    init = (jnp.zeros((bsz, nh, dk, dv), jnp.float32),
            jnp.zeros((bsz, nh, dk), jnp.float32),
            jnp.zeros((bsz, nh), jnp.float32))
    _, h = lax.scan(_mlstm_chunk_step, init, xs)
    return jnp.moveaxis(h, 0, 2).reshape(bsz, nh, s, dv)


def mlstm_mixer(z, w_in, b_gate, norm_w, w_out):
    bsz, s, d = z.shape
    H = MLSTM_HEADS
    q, k, v, o, gates = jnp.split(
        z @ w_in, [MLSTM_QK, 2 * MLSTM_QK, 2 * MLSTM_QK + d, 2 * MLSTM_QK + 2 * d], axis=-1)

    def heads(t, hd):
        return t.reshape(bsz, s, H, hd).transpose(0, 2, 1, 3).astype(jnp.float32)

    q = heads(q, MLSTM_DQK) * (MLSTM_DQK ** -0.5)
    k = heads(k, MLSTM_DQK)
    v = heads(v, MLSTM_DV)
    gates = softcap((gates + b_gate).astype(jnp.float32))
    gates = gates.reshape(bsz, s, 4, H).transpose(2, 0, 3, 1)
    i_fwd, f_fwd, i_bwd, f_bwd = gates[0], gates[1], gates[2], gates[3]
    h_fwd = mlstm_scan(q, k, v, i_fwd, f_fwd)
    flip = lambda t: jnp.flip(t, axis=2)
    h_bwd = flip(mlstm_scan(flip(q), flip(k), flip(v), flip(i_bwd), flip(f_bwd)))
    h = h_fwd + h_bwd
    h = h * lax.rsqrt(jnp.mean(h * h, axis=-1, keepdims=True) + EPS)
    h = h * norm_w.astype(jnp.float32).reshape(H, 1, MLSTM_DV)
    h = h.transpose(0, 2, 1, 3).reshape(bsz, s, d).astype(z.dtype)
    return (jax.nn.sigmoid(o) * h) @ w_out


def trunk(x, c, norm_g, ada_w, ada_b, ffn_w_in, ffn_w_out, conv_w_in, conv_w, conv_w_out,
          mlstm_w_in, mlstm_b_gate, mlstm_norm, mlstm_w_out, final_g, final_ada_w, final_ada_b):
    cs = jax.nn.silu(c)
    for layer in range(DEPTH):
        mod = (cs @ ada_w[layer] + ada_b[layer])[:, None, :]
        sh1, sc1, g1, sh2, sc2, g2 = jnp.split(mod, 6, axis=-1)
        z = modulate(rmsnorm(x, norm_g[layer, 0]), sh1, sc1)
        j = layer // N_MIXERS
        if layer % N_MIXERS == 0:
            mix = conv_mixer(z, conv_w_in[j], conv_w[j], conv_w_out[j])
        else:
            mix = mlstm_mixer(z, mlstm_w_in[j], mlstm_b_gate[j], mlstm_norm[j], mlstm_w_out[j])
        x = x + g1 * mix
        z = modulate(rmsnorm(x, norm_g[layer, 1]), sh2, sc2)
        x = x + g2 * swiglu_ffn(z, ffn_w_in[layer], ffn_w_out[layer])
    fmod = (cs @ final_ada_w + final_ada_b)[:, None, :]
    fsh, fsc = jnp.split(fmod, 2, axis=-1)
    return modulate(rmsnorm(x, final_g), fsh, fsc)


def setup_inputs(seed: int = 0) -> dict:
    key = jax.random.key(seed)
    ks = jax.random.split(key, 20)
    D, F, H = D_MODEL, D_FF, MLSTM_HEADS
    NCL, NML = N_CONV_LAYERS, N_MLSTM_LAYERS

    def nrm(k, shape, s):
        return jax.random.normal(k, shape, jnp.float32) * s

    b_i = nrm(ks[14], (NML, 2, H), 0.1)
    b_f = jax.random.uniform(ks[15], (NML, 2, H), jnp.float32, 3.0, 6.0)
    mlstm_b_gate = jnp.stack([b_i, b_f], axis=2).reshape(NML, 4 * H)
    return {
        'x_prompt': nrm(ks[0], (BATCH, SEQ, D), 1.0),
        'x_sample': nrm(ks[1], (DEC_BATCH, DEC_SEQ, D), 1.0),
        'c_prompt': nrm(ks[2], (BATCH, D), 1.0),
        'c_sample': nrm(ks[3], (DEC_BATCH, D), 1.0),
        'norm_g': 1.0 + nrm(ks[4], (DEPTH, 2, D), 0.1),
        'ada_w': nrm(ks[5], (DEPTH, D, 6 * D), 0.5 * D ** -0.5),
        'ada_b': nrm(ks[6], (DEPTH, 6 * D), 0.02),
        'ffn_w_in': nrm(ks[7], (DEPTH, D, 2 * F), D ** -0.5),
        'ffn_w_out': nrm(ks[8], (DEPTH, F, D), F ** -0.5),
        'conv_w_in': nrm(ks[9], (NCL, D, 3 * D), D ** -0.5),
        'conv_w': nrm(ks[10], (NCL, CONV_WIDTH, D), CONV_WIDTH ** -0.5),
        'conv_w_out': nrm(ks[11], (NCL, D, D), D ** -0.5),
        'mlstm_w_in': nrm(ks[12], (NML, D, MLSTM_IN), D ** -0.5),
        'mlstm_b_gate': mlstm_b_gate,
        'mlstm_norm': 1.0 + nrm(ks[13], (NML, D), 0.1),
        'mlstm_w_out': nrm(ks[16], (NML, D, D), D ** -0.5),
        'final_g': 1.0 + nrm(ks[17], (D,), 0.1),
        'final_ada_w': nrm(ks[18], (D, 2 * D), 0.5 * D ** -0.5),
        'final_ada_b': nrm(ks[19], (2 * D,), 0.02),
    }


def reference(x_prompt, x_sample, c_prompt, c_sample, norm_g, ada_w, ada_b, ffn_w_in, ffn_w_out,
              conv_w_in, conv_w, conv_w_out, mlstm_w_in, mlstm_b_gate, mlstm_norm, mlstm_w_out,
              final_g, final_ada_w, final_ada_b):
    y_prompt = trunk(x_prompt, c_prompt, norm_g, ada_w, ada_b, ffn_w_in, ffn_w_out,
                     conv_w_in, conv_w, conv_w_out, mlstm_w_in, mlstm_b_gate, mlstm_norm,
                     mlstm_w_out, final_g, final_ada_w, final_ada_b)
    y_sample = trunk(x_sample, c_sample, norm_g, ada_w, ada_b, ffn_w_in, ffn_w_out,
                     conv_w_in, conv_w, conv_w_out, mlstm_w_in, mlstm_b_gate, mlstm_norm,
                     mlstm_w_out, final_g, final_ada_w, final_ada_b)
    return (y_prompt, y_sample)
```

```python
import numpy as np
import concourse.bass as bass
import concourse.mybir as mybir
from concourse.bass_utils import run_bass_kernel_spmd

F32 = mybir.dt.float32
BF16 = mybir.dt.bfloat16
AF = mybir.ActivationFunctionType
ALU = mybir.AluOpType

D = 2048
KC = 16
FF = 5632
FC = 44
T = 512
NTOK = 4096
NTILE = NTOK // T
NBLK = NTOK // 128
NCORE = 8
H = 4
DQK = 256
DV = 512
EPS = 1e-6
MIN_ = 6160
NEG = -30000.0
_OLEV = 4
_SUB = 4


class Op:
    __slots__ = ("eng", "fn", "deps", "signal", "ticket", "dma_sem", "ndma", "idx", "inc")


class Sched:
    ENGS = ("pe", "act", "dve", "pool", "sp")

    def __init__(self):
        self.ops = {e: [] for e in self.ENGS}
        self.last_w = {}
        self.readers = {}
        self.dma_sems = []

    def add(self, eng, fn, reads=(), writes=(), dma_sem=None, ndma=1, inc=16):
        op = Op()
        op.inc = inc
        op.eng = eng
        op.fn = fn
        op.signal = False
        op.ticket = None
        op.dma_sem = dma_sem
        op.ndma = ndma
        if dma_sem is not None and dma_sem not in self.dma_sems:
            self.dma_sems.append(dma_sem)
        deps = []
        for r in reads:
            w = self.last_w.get(r)
            if w is not None:
                deps.append(w)
        for w_ in writes:
            w = self.last_w.get(w_)
            if w is not None:
                deps.append(w)
            rd = self.readers.get(w_)
            if rd:
                deps.extend(rd.values())
        op.deps = deps
        for d in deps:
            d.signal = True
        for r in reads:
            rd = self.readers.setdefault(r, {})
            key = eng if dma_sem is None else ("dma", dma_sem)
            rd[key] = op
        for w_ in writes:
            self.last_w[w_] = op
            self.readers[w_] = {}
        op.idx = len(self.ops[eng])
        self.ops[eng].append(op)
        return op

    def finalize(self):
        cnt = {e: 0 for e in self.ENGS}
        dcnt = {}
        for e in self.ENGS:
            lst = self.ops[e]
            if lst and lst[-1].dma_sem is None:
                lst[-1].signal = True
            for op in lst:
                if op.dma_sem is not None:
                    dcnt[op.dma_sem] = dcnt.get(op.dma_sem, 0) + op.inc * op.ndma
                    op.ticket = dcnt[op.dma_sem]
                elif op.signal:
                    cnt[e] += 1
                    op.ticket = cnt[e]
        self.final_cnt = cnt
        self.final_dcnt = dcnt

    def emit(self, nc, blk, sems, base):
        engobj = {"pe": blk.tensor, "act": blk.scalar, "dve": blk.vector, "pool": blk.gpsimd, "sp": blk.sync}
        S = self

        def make(ename):
            def body(e):
                waited = {}
                for op in S.ops[ename]:
                    for d in op.deps:
                        if d.dma_sem is not None:
                            key = d.dma_sem
                        else:
                            key = d.eng
                            if d.eng == "pe" and ename == "pe":
                                continue
                        val = d.ticket + base.get(key, 0)
                        if waited.get(key, -1) >= val:
                            continue
                        waited[key] = val
                        e.wait_ge(sems[key], val)
                    r = op.fn(e)
                    if op.dma_sem is not None:
                        if not isinstance(r, (list, tuple)):
                            r = [r]
                        assert len(r) == op.ndma
                        for ins in r:
                            ins.then_inc(sems[op.dma_sem], op.inc)
                    elif op.signal:
                        if isinstance(r, (list, tuple)):
                            r = r[-1]
                        r.then_inc(sems[ename], 1)
                for k, v in S.final_cnt.items():
                    if v > 0 and not (k == ename):
                        e.wait_ge(sems[k], v + base.get(k, 0))
                for k, v in S.final_dcnt.items():
                    e.wait_ge(sems[k], v + base.get(k, 0))
            return body

        for ename in self.ENGS:
            engobj[ename](make(ename))
        for k, v in self.final_cnt.items():
            base[k] = base.get(k, 0) + v
        for k, v in self.final_dcnt.items():
            base[k] = base.get(k, 0) + v


class WStream:
    NF = 4
    NB = 4
    LA = 3

    def __init__(self, S, wf, wb, plan):
        self.S = S
        self.wf = wf
        self.wb = wb
        self.plan = plan
        self.issued = 0
        self.used = 0

    def _issue(self, i):
        src, nk, ncol = self.plan[i]
        sf = i % self.NF
        sb = i % self.NB
        vf = self.wf[sf][:, 0:nk * ncol].rearrange("p (k c) -> p k c", k=nk)
        vb = self.wb[sb][:, 0:nk * ncol].rearrange("p (k c) -> p k c", k=nk)
        self.S.add("sp", lambda e, vf=vf, src=src: e.dma_start(out=vf, in_=src),
                   writes=[f"wf{sf}"], dma_sem=f"wf{sf}")
        ce = ("act", "dve", "pool")[i % 3]
        if ce == "act":
            fn = lambda e, vb=vb, vf=vf: e.copy(out=vb, in_=vf)
        else:
            fn = lambda e, vb=vb, vf=vf: e.tensor_copy(out=vb, in_=vf)
        self.S.add(ce, fn, reads=[f"wf{sf}"], writes=[f"wb{sb}"])

    def next(self, nk, ncol):
        i = self.used
        assert self.plan[i][1] == nk and self.plan[i][2] == ncol, (i, self.plan[i][1:], nk, ncol)
        while self.issued < min(len(self.plan), i + self.LA + 1):
            self._issue(self.issued)
            self.issued += 1
        self.used += 1
        sb = i % self.NB
        vb = self.wb[sb][:, 0:nk * ncol].rearrange("p (k c) -> p k c", k=nk)
        return vb, f"wb{sb}"


def build_nc(ntiles=NTILE, dbg=False, do_phase2=True, nextra=3, p2stage=6):
    nc = bass.Bass("TRN2", target_bir_lowering=False)

    def din(name, shape, dt=F32):
        return nc.dram_tensor(name, list(shape), dt, kind="ExternalInput")

    xT = din("xT", [D, NTOK])
    xTe = din("xTe", [3, D, NTOK])
    xhalo_e = din("xhalo_e", [3, 128, NTILE * KC * 2])
    emask_e = din("emask_e", [3, 128, NTILE * 2])
    xhalo = din("xhalo", [128, NTILE * KC * 2])
    emask = din("emask", [128, NTILE * 2])
    cvec = din("cvec", [128, KC])
    smallv = din("smallv", [128, 64 + 192 + 32 + 16 + 48])
    mnorm = din("mnorm", [128, D])
    bgate = din("bgate", [16, 1])
    gsel = din("gsel", [16, 2])
    consts = din("consts", [128, 5 * 128])
    pred = din("pred", [128, 16])
    ada_w = din("ada_w", [2, D, 6 * D])
    final_ada_w = din("final_ada_w", [D, 2 * D])
    ffn_w_in = din("ffn_w_in", [2, D, 2 * FF])
    ffn_w_out = din("ffn_w_out", [2, FF, D])
    conv_w_in = din("conv_w_in", [D, 3 * D])
    conv_w_out = din("conv_w_out", [D, D])
    mlstm_w_in = din("mlstm_w_in", [D, MIN_])
    mlstm_w_out = din("mlstm_w_out", [D, D])

    yT = nc.dram_tensor("yT", [D, NTOK], F32, kind="ExternalOutput")

    okind = "ExternalOutput" if dbg else "Internal"
    x1s = nc.dram_tensor("x1s", [D, NTOK], F32, kind=okind)
    qs = nc.dram_tensor("qs", [128, 8, NTOK], BF16, kind=okind)
    ks = nc.dram_tensor("ks", [128, 8, NTOK], BF16, kind=okind)
    kts = nc.dram_tensor("kts", [NBLK, 128, 1024], BF16, kind=okind)
    vts = nc.dram_tensor("vts", [NBLK, 128, D], BF16, kind=okind)
    ots = nc.dram_tensor("ots", [NBLK, 128, D], BF16, kind=okind)
    gts = nc.dram_tensor("gts", [16, NTOK], F32, kind=okind)

    SW = 4096 + 16
    hbs = nc.dram_tensor("hbs", [NBLK, 128, D], F32, kind="Internal")
    gTd = nc.dram_tensor("gTd", [128, KC, NTOK], BF16, kind="Internal")
    sloc = nc.dram_tensor("sloc", [128, 2 * SW], F32, kind="Internal")
    sall = nc.dram_tensor("sall", [3 * 128, 2 * SW], F32, kind="Internal")
    xT_r = xT.ap().rearrange("(k p) t -> p k t", p=128)
    x1s_r = x1s.ap().rearrange("(k p) t -> p k t", p=128)
    yT_r = yT.ap().rearrange("(k p) t -> p k t", p=128)
    xsrcs = [xTe.ap()[e_].rearrange("(k p) t -> p k t", p=128) for e_ in range(3)] + [xT_r]
    xhsrcs = [xhalo_e.ap()[e_] for e_ in range(3)] + [xhalo.ap()]
    emsrcs = [emask_e.ap()[e_] for e_ in range(3)] + [emask.ap()]

    def wr(w):
        return w.rearrange("(k p) n -> p k n", p=128)

    adaw_r = [wr(ada_w.ap()[l]) for l in range(2)]
    fadaw_r = wr(final_ada_w.ap())
    fwin_r = [wr(ffn_w_in.ap()[l]) for l in range(2)]
    fwout_r = [wr(ffn_w_out.ap()[l]) for l in range(2)]
    cwin_r = wr(conv_w_in.ap())
    cwout_r = wr(conv_w_out.ap())
    mwin_r = wr(mlstm_w_in.ap())
    mwout_r = wr(mlstm_w_out.ap())

    def plan_ffn(l, plan):
        for j in range(FC):
            plan.append((fwin_r[l][:, :, j * 128:(j + 1) * 128], 16, 128))
            plan.append((fwin_r[l][:, :, FF + j * 128:FF + (j + 1) * 128], 16, 128))
        for m in range(KC):
            for (k0, nk) in ((0, 16), (16, 16), (32, 12)):
                plan.append((fwout_r[l][:, k0:k0 + nk, m * 128:(m + 1) * 128], nk, 128))

    def make_plan1(lite):
        plan1 = []
        for i in range(ntiles):
            for m in range(KC):
                for part in range(3):
                    plan1.append((cwin_r[:, :, part * D + m * 128: part * D + (m + 1) * 128], 16, 128))
            for m in range(KC):
                plan1.append((cwout_r[:, :, m * 128:(m + 1) * 128], 16, 128))
            plan_ffn(0, plan1)
            for c in range(8 if lite else 0, 16):
                plan1.append((mwin_r[:, :, c * 128:(c + 1) * 128], 16, 128))
            for g in range(4 if lite else 8):
                for q4 in range(4):
                    plan1.append((mwin_r[:, q4 * 4:(q4 + 1) * 4, 2048 + g * 512: 2048 + (g + 1) * 512], 4, 512))
        return plan1

    sem_names = ["pe", "act", "dve", "pool", "sp"]
    base = {}

    from contextlib import ExitStack
    with ExitStack() as es:
        def sb(name, shape, dt=F32):
            return es.enter_context(nc.sbuf_tensor(name, list(shape), dt))

        cs = sb("cs", [128, KC])
        small = sb("small", [128, 352])
        modt = sb("modt", [128, 224])
        avec = sb("avec", [128, 5 * KC])
        cst = sb("cst", [128, 640])
        identb = sb("identb", [128, 128], BF16)
        onesb = sb("onesb", [128, 128], BF16)
        emk = sb("emk", [128, NTILE * 2])
        wgb = sb("wgb", [128, KC, 16], BF16)
        wgf = sb("wgf", [128, KC, 16])
        bgt = sb("bgt", [16, 1])
        gsl = sb("gsl", [16, 2])
        graw = sb("graw", [16, NTOK])
        ps = [es.enter_context(nc.psum_tensor(f"ps{i}", [128, 512], F32)) for i in range(8)]
        sems = {}
        blkno = [0]

        def new_block_sems():
            blkno[0] += 1
            names = sem_names + ["wf0", "wf1", "wf2", "wf3"]
            for n_ in names:
                sems[n_] = es.enter_context(nc.semaphore(f"s{blkno[0]}_" + n_))
                base[n_] = 0
        dsem_pool = ["x", "xh", "misc", "st_x1", "st_qk", "st_kt", "st_v",
                     "st_o", "st_g", "mw0", "mw1", "mw2", "cc", "p2a", "p2b", "p2c", "p2d", "p2e", "p2f",
                     "st_y", "st_hb0", "st_hb1", "st_S", "st_k", "st_gT0", "st_gT1", "ld_S0", "ld_S1", "ld_m"]
        for n_ in dsem_pool:
            sems[n_] = es.enter_context(nc.semaphore("d_" + n_))

        norm_g_v = small[:, 0:64]
        ada_b_v = small[:, 64:256]
        fada_b_v = small[:, 256:288]
        final_g_v = small[:, 288:304]
        conv_w_v = small[:, 304:352]

        def make_helpers(S, W, xt, zt, hid, rstd, rstdh, tmp, tmph, sg, xh, zh):
            bank_rr = [0]

            def nbank():
                b = bank_rr[0] % 6
                bank_rr[0] += 1
                return b

            def rmsnorm_mod(avw, shv, with_halo=False, outx=False):
                for kc in range(KC):
                    S.add("act", lambda e, kc=kc: e.activation(out=zt[:, kc, :], in_=xt[:, kc, :], func=AF.Square),
                          reads=[f"x{kc}"], writes=[f"z{kc}"])
                for kc in range(KC):
                    S.add("pe", lambda e, kc=kc: e.matmul(ps[6][:, :], onesb[:], zt[:, kc, :], start=(kc == 0), stop=(kc == KC - 1)),
                          reads=[f"z{kc}", "onesb"], writes=["ps6"])
                S.add("dve", lambda e: e.tensor_scalar(out=rstd[:], in0=ps[6][:, :], scalar1=1.0 / D, scalar2=EPS,
                                                       op0=ALU.mult, op1=ALU.add), reads=["ps6"], writes=["rstd"])
                S.add("act", lambda e: e.activation(out=rstd[:], in_=rstd[:], func=AF.Sqrt), reads=["rstd"], writes=["rstd"])
                S.add("dve", lambda e: e.reciprocal(out=rstd[:], in_=rstd[:]), reads=["rstd"], writes=["rstd"])
                if with_halo:
                    S.add("act", lambda e: e.activation(out=zh[:], in_=xh[:], func=AF.Square), reads=["xh"], writes=["zh"])
                    for kc in range(KC):
                        S.add("pe", lambda e, kc=kc: e.matmul(ps[7][:, 0:2], onesb[:], zh[:, kc, :], start=(kc == 0), stop=(kc == KC - 1)),
                              reads=["zh", "onesb"], writes=["ps7"])
                    S.add("dve", lambda e: e.tensor_scalar(out=rstdh[:], in0=ps[7][:, 0:2], scalar1=1.0 / D, scalar2=EPS,
                                                           op0=ALU.mult, op1=ALU.add), reads=["ps7"], writes=["rstdh"])
                    S.add("act", lambda e: e.activation(out=rstdh[:], in_=rstdh[:], func=AF.Sqrt), reads=["rstdh"], writes=["rstdh"])
                    S.add("dve", lambda e: e.reciprocal(out=rstdh[:], in_=rstdh[:]), reads=["rstdh"], writes=["rstdh"])
                for kc in range(KC):
                    tb = kc % 2
                    S.add("dve", lambda e, kc=kc, tb=tb: e.scalar_tensor_tensor(
                        out=tmp[tb][:], in0=xt[:, kc, :], scalar=avw[:, kc:kc + 1], in1=rstd[:], op0=ALU.mult, op1=ALU.mult),
                        reads=[f"x{kc}", "rstd", "avec"], writes=[f"tmp{tb}"])
                    S.add("act", lambda e, kc=kc, tb=tb: e.activation(out=(xt if outx else zt)[:, kc, :], in_=tmp[tb][:], func=AF.Identity,
                                                                    bias=shv[:, kc:kc + 1], scale=1.0),
                          reads=[f"tmp{tb}", "modt"], writes=[(f"x{kc}" if outx else f"z{kc}")])
                    if with_halo:
                        S.add("dve", lambda e, kc=kc: e.scalar_tensor_tensor(
                            out=tmph[:], in0=xh[:, kc, :], scalar=avw[:, kc:kc + 1], in1=rstdh[:], op0=ALU.mult, op1=ALU.mult),
                            reads=["xh", "rstdh", "avec"], writes=["tmph"])
                        S.add("act", lambda e, kc=kc: e.activation(out=zh[:, kc, :], in_=tmph[:], func=AF.Identity,
                                                                 bias=shv[:, kc:kc + 1], scale=1.0),
                              reads=["tmph", "modt"], writes=["zh"])

            def mm_slab(bank, wv, wres, rhs_fn, rhs_res_fn, nk, first, last, ncols=T, k0=0):
                for k in range(nk):
                    S.add("pe", lambda e, k=k: e.matmul(ps[bank][:, 0:ncols], wv[:, k, :], rhs_fn(k0 + k),
                                                       start=(first and k == 0), stop=(last and k == nk - 1)),
                          reads=[wres, rhs_res_fn(k0 + k)], writes=[f"ps{bank}"])

            def ffn(l, gv):
                for j in range(FC):
                    ba, bb = nbank(), nbank()
                    wv, wres = W.next(16, 128)
                    mm_slab(ba, wv, wres, lambda k: zt[:, k, :], lambda k: f"z{k}", 16, True, True)
                    wv, wres = W.next(16, 128)
                    mm_slab(bb, wv, wres, lambda k: zt[:, k, :], lambda k: f"z{k}", 16, True, True)
                    si = j % 2
                    S.add("act", lambda e, ba=ba, si=si: e.activation(out=sg[si][:], in_=ps[ba][:, :], func=AF.Silu),
                          reads=[f"ps{ba}"], writes=[f"sg{si}"])
                    S.add("dve", lambda e, bb=bb, si=si, j=j: e.tensor_tensor(out=hid[:, j, :], in0=sg[si][:], in1=ps[bb][:, :], op=ALU.mult),
                          reads=[f"sg{si}", f"ps{bb}"], writes=[f"hid{j}"])
                for m in range(KC):
                    b = nbank()
                    for si_, (k0, nk) in enumerate(((0, 16), (16, 16), (32, 12))):
                        wv, wres = W.next(nk, 128)
                        mm_slab(b, wv, wres, lambda k: hid[:, k, :], lambda k: f"hid{k}", nk, si_ == 0, si_ == 2, k0=k0)
                    S.add("dve", lambda e, b=b, m=m: e.scalar_tensor_tensor(
                        out=xt[:, m, :], in0=ps[b][:, :], scalar=gv[:, m:m + 1], in1=xt[:, m, :], op0=ALU.mult, op1=ALU.add),
                        reads=[f"ps{b}", f"x{m}", "modt"], writes=[f"x{m}"])

            return rmsnorm_mod, mm_slab, ffn, nbank

        def run_block1(chunk, lite, first):
            xsrc, xhsrc, emsrc = xsrcs[chunk], xhsrcs[chunk], emsrcs[chunk]
            with ExitStack() as es1:
                new_block_sems()
                def sb1(name, shape, dt=F32):
                    return es1.enter_context(nc.sbuf_tensor(f"{name}_b{blkno[0]}", list(shape), dt))
                xt = sb1("xt", [128, KC, T])
                zt = sb1("zt", [128, KC, T], BF16)
                hid = sb1("hid", [128, FC, T], BF16)
                ostg = sb1("ostg", [128, 4, D], BF16)
                wf = [sb1(f"wf{i}", [128, 2048]) for i in range(WStream.NF)]
                wb = [sb1(f"wb{i}", [128, 2048], BF16) for i in range(WStream.NB)]
                xh = sb1("xh", [128, KC, 2])
                zh = sb1("zh", [128, KC, 2], BF16)
                rstd = sb1("rstd", [128, T])
                rstdh = sb1("rstdh", [128, 2])
                tmp = [sb1(f"tmp{i}", [128, T]) for i in range(2)]
                tmph = sb1("tmph", [128, 2])
                bgs = [sb1(f"bgs{i}", [128, T]) for i in range(2)]
                cgs = [sb1(f"cgs{i}", [128, T]) for i in range(2)]
                uext = [sb1(f"uext{i}", [128, T + 2]) for i in range(2)]
                cv = [sb1(f"cv{i}", [128, T]) for i in range(2)]
                hc = [sb1(f"hc{i}", [128, 4]) for i in range(2)]
                sg = [sb1(f"sg{i}", [128, T]) for i in range(2)]

                S = Sched()
                plan1 = make_plan1(lite)
                W = WStream(S, wf, wb, plan1)
                if first:
                    S.add("pool", lambda e: [e.dma_start(out=cs[:], in_=cvec[:, :]),
                                             e.dma_start(out=small[:], in_=smallv[:, :]),
                                             e.dma_start(out=cst[:], in_=consts[:, :]),
                                             e.dma_start(out=bgt[:], in_=bgate[:, :]),
                                             e.dma_start(out=gsl[:], in_=gsel[:, :]),
                                             e.dma_start(out=wgf[:], in_=mwin_r[:, :, 6144:6160])],
                          writes=["cs", "small", "cst", "bgt", "gsl", "wgf"], dma_sem="misc", ndma=6)
                    S.add("act", lambda e: e.activation(out=cs[:], in_=cs[:], func=AF.Silu), reads=["cs"], writes=["cs"])
                    S.add("dve", lambda e: e.tensor_copy(out=identb[:], in_=cst[:, 0:128]), reads=["cst"], writes=["identb"])
                    S.add("dve", lambda e: e.memset(onesb[:], 1.0), writes=["onesb"])
                    S.add("dve", lambda e: e.tensor_copy(out=wgb[:], in_=wgf[:]), reads=["wgf"], writes=["wgb"])

                    mod_jobs = []
                    mod_jobs.append((adaw_r[0], 24, 0, ada_b_v[:, 0:96]))
                    mod_jobs.append((adaw_r[1], 24, 96, ada_b_v[:, 96:192]))
                    mod_jobs.append((fadaw_r, 8, 192, fada_b_v))
                    piece_i = 0
                    for (w_r, ng, cbase, bview) in mod_jobs:
                        for g in range(ng):
                            for kq in range(4):
                                slot = piece_i % 4
                                piece_i += 1
                                src = w_r[:, kq * 4:(kq + 1) * 4, g * 512:(g + 1) * 512]
                                mwv = wf[slot][:, :].rearrange("p (k c) -> p k c", k=4)
                                S.add("sp", lambda e, mwv=mwv, src=src: e.dma_start(out=mwv, in_=src),
                                      writes=[f"wf{slot}"], dma_sem=f"wf{slot}")
                                for m in range(4):
                                    for kcl in range(4):
                                        kc = kq * 4 + kcl
                                        S.add("pe", lambda e, m=m, kcl=kcl, kc=kc, mwv=mwv, kq=kq: e.matmul(
                                            ps[m][:, 0:1], mwv[:, kcl, m * 128:(m + 1) * 128], cs[:, kc:kc + 1],
                                            start=(kq == 0 and kcl == 0), stop=(kq == 3 and kcl == 3)),
                                            reads=[f"wf{slot}", "cs"], writes=[f"ps{m}"])
                            for m in range(4):
                                col = cbase + g * 4 + m
                                bc = g * 4 + m
                                S.add("dve", lambda e, m=m, col=col, bc=bc, bview=bview: e.tensor_tensor(
                                    out=modt[:, col:col + 1], in0=ps[m][:, 0:1], in1=bview[:, bc:bc + 1], op=ALU.add),
                                    reads=[f"ps{m}", "small"], writes=["modt"])
                    def mk_a(idx, gview, scview):
                        S.add("dve", lambda e: e.scalar_tensor_tensor(out=avec[:, idx * 16:(idx + 1) * 16], in0=scview, scalar=1.0,
                                                                      in1=gview, op0=ALU.add, op1=ALU.mult),
                              reads=["modt", "small"], writes=["avec"])
                    mk_a(0, norm_g_v[:, 0:16], modt[:, 16:32])
                    mk_a(1, norm_g_v[:, 16:32], modt[:, 64:80])
                    mk_a(2, norm_g_v[:, 32:48], modt[:, 96 + 16:96 + 32])
                    mk_a(3, norm_g_v[:, 48:64], modt[:, 96 + 64:96 + 80])
                    mk_a(4, final_g_v, modt[:, 192 + 16:192 + 32])

                S.add("pool", lambda e: e.dma_start(out=emk[:], in_=emsrc), writes=["emk"], dma_sem="ld_m")
                rmsnorm_mod, mm_slab, ffn, nbank = make_helpers(S, W, xt, zt, hid, rstd, rstdh, tmp, tmph, sg, xh, zh)

                for i in range(ntiles):
                    c0 = i * T
                    S.add("pool", lambda e, c0=c0: e.dma_start(out=xt[:], in_=xsrc[:, :, c0:c0 + T]),
                          writes=[f"x{k}" for k in range(KC)], dma_sem="x")
                    S.add("pool", lambda e, i=i: e.dma_start(out=xh[:].rearrange("p k t -> p (k t)"),
                                                              in_=xhsrc[:, i * 32:(i + 1) * 32]),
                          writes=["xh"], dma_sem="xh")
                    rmsnorm_mod(avec[:, 0:16], modt[:, 0:16], with_halo=True)
                    for m in range(KC):
                        ba, bb, bc_ = nbank(), nbank(), nbank()
                        wv, wres = W.next(16, 128)
                        mm_slab(ba, wv, wres, lambda k: zt[:, k, :], lambda k: f"z{k}", 16, True, True)
                        wv, wres = W.next(16, 128)
                        mm_slab(bb, wv, wres, lambda k: zt[:, k, :], lambda k: f"z{k}", 16, True, True)
                        for k in range(KC):
                            S.add("pe", lambda e, k=k, wv=wv: e.matmul(ps[7][:, 0:2], wv[:, k, :], zh[:, k, :], start=(k == 0), stop=(k == KC - 1)),
                                  reads=[wres, "zh"], writes=["ps7"])
                        wv, wres = W.next(16, 128)
                        mm_slab(bc_, wv, wres, lambda k: zt[:, k, :], lambda k: f"z{k}", 16, True, True)
                        for k in range(KC):
                            S.add("pe", lambda e, k=k, wv=wv: e.matmul(ps[7][:, 2:4], wv[:, k, :], zh[:, k, :], start=(k == 0), stop=(k == KC - 1)),
                                  reads=[wres, "zh"], writes=["ps7"])
                        r2 = m % 2
                        S.add("act", lambda e, ba=ba, r2=r2: e.copy(out=bgs[r2][:], in_=ps[ba][:, :]), reads=[f"ps{ba}"], writes=[f"bgs{r2}"])
                        S.add("act", lambda e, bb=bb, r2=r2: e.copy(out=cgs[r2][:], in_=ps[bb][:, :]), reads=[f"ps{bb}"], writes=[f"cgs{r2}"])
                        S.add("act", lambda e, r2=r2: e.copy(out=hc[r2][:], in_=ps[7][:, 0:4]), reads=["ps7"], writes=[f"hc{r2}"])
                        S.add("dve", lambda e, bc_=bc_, r2=r2: e.tensor_tensor(out=uext[r2][:, 1:T + 1], in0=cgs[r2][:], in1=ps[bc_][:, :], op=ALU.mult),
                              reads=[f"cgs{r2}", f"ps{bc_}"], writes=[f"uext{r2}"])
                        S.add("dve", lambda e, r2=r2, i=i: e.scalar_tensor_tensor(
                            out=uext[r2][:, 0:1], in0=hc[r2][:, 0:1], scalar=emk[:, 2 * i:2 * i + 1], in1=hc[r2][:, 2:3], op0=ALU.mult, op1=ALU.mult),
                            reads=[f"hc{r2}", "emk", f"uext{r2}"], writes=[f"uext{r2}"])
                        S.add("dve", lambda e, r2=r2, i=i: e.scalar_tensor_tensor(
                            out=uext[r2][:, T + 1:T + 2], in0=hc[r2][:, 1:2], scalar=emk[:, 2 * i + 1:2 * i + 2], in1=hc[r2][:, 3:4], op0=ALU.mult, op1=ALU.mult),
                            reads=[f"hc{r2}", "emk", f"uext{r2}"], writes=[f"uext{r2}"])
                        S.add("dve", lambda e, r2=r2, m=m: e.tensor_scalar(out=cv[r2][:], in0=uext[r2][:, 0:T], scalar1=conv_w_v[:, m:m + 1], scalar2=None, op0=ALU.mult),
                              reads=[f"uext{r2}", "small"], writes=[f"cv{r2}"])
                        S.add("dve", lambda e, r2=r2, m=m: e.scalar_tensor_tensor(out=cv[r2][:], in0=uext[r2][:, 1:T + 1], scalar=conv_w_v[:, 16 + m:17 + m],
                                                                                  in1=cv[r2][:], op0=ALU.mult, op1=ALU.add),
                              reads=[f"uext{r2}", "small", f"cv{r2}"], writes=[f"cv{r2}"])
                        S.add("dve", lambda e, r2=r2, m=m: e.scalar_tensor_tensor(out=cv[r2][:], in0=uext[r2][:, 2:T + 2], scalar=conv_w_v[:, 32 + m:33 + m],
                                                                                  in1=cv[r2][:], op0=ALU.mult, op1=ALU.add),
                              reads=[f"uext{r2}", "small", f"cv{r2}"], writes=[f"cv{r2}"])
                        S.add("dve", lambda e, r2=r2, m=m: e.tensor_tensor(out=hid[:, m, :], in0=cv[r2][:], in1=bgs[r2][:], op=ALU.mult),
                              reads=[f"cv{r2}", f"bgs{r2}"], writes=[f"hid{m}"])
                    for m in range(KC):
                        b = nbank()
                        wv, wres = W.next(16, 128)
                        mm_slab(b, wv, wres, lambda k: hid[:, k, :], lambda k: f"hid{k}", 16, True, True)
                        S.add("dve", lambda e, b=b, m=m: e.scalar_tensor_tensor(
                            out=xt[:, m, :], in0=ps[b][:, :], scalar=modt[:, 32 + m:33 + m], in1=xt[:, m, :], op0=ALU.mult, op1=ALU.add),
                            reads=[f"ps{b}", f"x{m}", "modt"], writes=[f"x{m}"])
                    rmsnorm_mod(avec[:, 16:32], modt[:, 48:64])
                    ffn(0, modt[:, 80:96])
                    if not lite:
                      S.add("pool", lambda e, c0=c0: e.dma_start(out=x1s_r[:, :, c0:c0 + T], in_=xt[:]),
                          reads=[f"x{k}" for k in range(KC)], writes=["d_x1s"], dma_sem="st_x1")
                    rmsnorm_mod(avec[:, 32:48], modt[:, 96:112])
                    for c in range(8 if lite else 0, 16):
                        b = nbank()
                        wv, wres = W.next(16, 128)
                        mm_slab(b, wv, wres, lambda k: zt[:, k, :], lambda k: f"z{k}", 16, True, True)
                        if c < 8:
                            S.add("act", lambda e, b=b, c=c: e.activation(out=hid[:, c, :], in_=ps[b][:, :], func=AF.Copy, scale=1.0 / 16.0),
                                  reads=[f"ps{b}"], writes=[f"hid{c}"])
                        else:
                            S.add("dve", lambda e, b=b, c=c: e.tensor_copy(out=hid[:, c, :], in_=ps[b][:, :]),
                                  reads=[f"ps{b}"], writes=[f"hid{c}"])
                    if not lite:
                      S.add("pool", lambda e, c0=c0: e.dma_start(out=qs[:, :, c0:c0 + T], in_=hid[:, 0:8, :]),
                          reads=[f"hid{c}" for c in range(8)], writes=["d_qs"], dma_sem="st_qk")
                    if not lite:
                      S.add("pool", lambda e, c0=c0: e.dma_start(out=ks[:, :, c0:c0 + T], in_=hid[:, 8:16, :]),
                          reads=[f"hid{c}" for c in range(8, 16)], writes=["d_ks"], dma_sem="st_k")
                    ktv = hid[:, 16:24, :].rearrange("p (a b) c -> p a (b c)", b=2)
                    ps7b = ps[7][:].bitcast(BF16)
                    for tb in range(4):
                        for c in range(8):
                            S.add("pe", lambda e, c=c, tb=tb: e.transpose(ps7b[:, c * 128:(c + 1) * 128], hid[:, 8 + c, tb * 128:(tb + 1) * 128], identb[:]),
                                  reads=[f"hid{8 + c}", "identb"], writes=["ps7"])
                        S.add("dve", lambda e, tb=tb: e.tensor_copy(out=ktv[:, tb, :], in_=ps7b[:, :]),
                              reads=["ps7"], writes=[f"hid{16 + 2 * tb}", f"hid{17 + 2 * tb}"])
                    S.add("pool", lambda e, i=i: e.dma_start(out=kts.ap()[4 * i:4 * i + 4].rearrange("b p c -> p b c"), in_=ktv),
                          reads=[f"hid{c}" for c in range(16, 24)], writes=["d_kts"], dma_sem="st_kt")
                    vtv = hid[:, 24:40, :].rearrange("p (a b) c -> p a (b c)", b=4)
                    for g in range(4 if lite else 8):
                        for q4 in range(4):
                            wv, wres = W.next(4, 512)
                            for kcl in range(4):
                                kc = q4 * 4 + kcl
                                for tb in range(4):
                                    S.add("pe", lambda e, kc=kc, kcl=kcl, tb=tb, wv=wv, q4=q4: e.matmul(
                                        ps[tb][:, :], zt[:, kc, tb * 128:(tb + 1) * 128], wv[:, kcl, :],
                                        start=(q4 == 0 and kcl == 0), stop=(q4 == 3 and kcl == 3)),
                                        reads=[wres, f"z{kc}"], writes=[f"ps{tb}"])
                        for tb in range(4):
                            if g < 4:
                                S.add("dve" if tb % 2 else "act",
                                      (lambda e, tb=tb, g=g: e.tensor_copy(out=vtv[:, tb, g * 512:(g + 1) * 512], in_=ps[tb][:, :])) if tb % 2 else
                                      (lambda e, tb=tb, g=g: e.copy(out=vtv[:, tb, g * 512:(g + 1) * 512], in_=ps[tb][:, :])),
                                      reads=[f"ps{tb}"], writes=[f"hid{24 + 4 * tb + k}" for k in range(4)])
                            else:
                                go = g - 4
                                S.add("act", lambda e, tb=tb, go=go: e.activation(out=ostg[:, tb, go * 512:(go + 1) * 512], in_=ps[tb][:, :], func=AF.Sigmoid),
                                      reads=[f"ps{tb}"], writes=["ostg"])
                        if g == 3:
                            S.add("pool", lambda e, i=i: e.dma_start(out=vts.ap()[4 * i:4 * i + 4].rearrange("b p c -> p b c"), in_=vtv),
                                  reads=[f"hid{c}" for c in range(24, 40)], writes=["d_vts"], dma_sem="st_v")
                    if not lite:
                      S.add("pool", lambda e, i=i: e.dma_start(out=ots.ap()[4 * i:4 * i + 4].rearrange("b p c -> p b c"), in_=ostg[:]),
                          reads=["ostg"], writes=["d_ots"], dma_sem="st_o")
                    for k in range(KC):
                        S.add("pe", lambda e, k=k: e.matmul(ps[6][0:16, :], wgb[:, k, :], zt[:, k, :], start=(k == 0), stop=(k == KC - 1)),
                              reads=["wgb", f"z{k}"], writes=["ps6"])
                    S.add("act", lambda e, c0=c0: e.activation(out=graw[:, c0:c0 + T], in_=ps[6][0:16, :], func=AF.Identity, bias=bgt[:, 0:1], scale=1.0),
                          reads=["ps6", "bgt"], writes=["graw"])

                if dbg:
                    S.add("pool", lambda e: e.dma_start(out=gts[:, :], in_=graw[:]), reads=["graw"], writes=["d_gts"], dma_sem="st_g")
                assert W.used == len(plan1), (W.used, len(plan1))
                S.finalize()
                with nc.Block() as blk:
                    S.emit(nc, blk, sems, base)
        nblk = ntiles * 4

        def run_block2(mode, ex):
            with ExitStack() as es2:
                blkno[0] += 1
                def sb2(name, shape, dt=F32):
                    return es2.enter_context(nc.sbuf_tensor(f"{name}_b{blkno[0]}", list(shape), dt))
                C = sb2("C", [128, 8, 512])
                Cb = sb2("Cb", [128, 8, 512], BF16)
                nst = sb2("nst", [128, 8])
                nb = sb2("nb", [128, 8], BF16)
                qb = [sb2(f"qb{i}", [128, 8, 128], BF16) for i in range(2)]
                kb = [sb2(f"kb{i}", [128, 8, 128], BF16) for i in range(2)]
                ktb = [sb2(f"ktb{i}", [128, 1024], BF16) for i in range(2)]
                vtb = [sb2(f"vtb{i}", [128, D], BF16) for i in range(2)]
                sob = [sb2(f"sob{i}", [128, D], BF16) for i in range(2)]
                hbb = [sb2(f"hbb{i}", [128, D]) for i in range(2)]
                glf = sb2("glf", [16, NTOK])
                gtok = sb2("gtok", [128, NBLK, 16])
                cumf = sb2("cumf", [128, NBLK, 16])
                cumb = sb2("cumb", [128, NBLK, 16])
                tot = sb2("tot", [128, NBLK, 16])
                dec = sb2("dec", [128, NBLK, 16])
                biasD = sb2("biasD", [128, NBLK, 8])
                wk = sb2("wk", [128, NBLK, 8])
                gsum = sb2("gsum", [128, 16])
                gexp = sb2("gexp", [128, 16])
                coef = sb2("coef", [128, 4])
                onesf = sb2("onesf", [128, 128])
                Rt = [sb2(f"Rt{i}", [128, 128]) for i in range(2)]
                EBt = [sb2(f"EBt{i}", [128, 128]) for i in range(2)]
                DTt = [sb2(f"DTt{i}", [128, 128]) for i in range(2)]
                ptl = [sb2(f"pt{i}", [128, 128], BF16) for i in range(2)]
                qtil = [sb2(f"qtil{i}", [128, 2, 128], BF16) for i in range(2)]
                ktil = [sb2(f"ktil{i}", [128, 256], BF16) for i in range(2)]
                dab = [sb2(f"dab{i}", [128, 1]) for i in range(2)]
                rden = [sb2(f"rden{i}", [128, 1]) for i in range(2)]
                ssq = sb2("ssq", [128, 4])
                rn = sb2("rn", [128, 4])
                tmpn = [sb2(f"tmpn{i}", [128, 512]) for i in range(2)]
                junk = sb2("junk", [128, 512])
                gtb = sb2("gtb", [128, D], BF16)
                gTst = [sb2(f"gTst{i}", [128, KC, 128], BF16) for i in range(2)]
                mnt = sb2("mnt", [128, D])
                sstg = [sb2(f"sstg{i}", [128, SW]) for i in range(2)]
                prd = sb2("prd", [128, 16])

                S2 = Sched()
                A2 = S2.add
                A2("pool", lambda e: [e.dma_start(out=prd[:], in_=pred[:, :]), e.dma_start(out=mnt[:], in_=mnorm[:, :])],
                   writes=["prd", "mnt"], dma_sem="ld_m", ndma=2)
                A2("dve", lambda e: e.memset(onesf[:], 1.0), writes=["onesf"])
                identf = cst[:, 0:128]

                A2("act", lambda e: e.activation(out=graw[:], in_=graw[:], func=AF.Tanh, scale=1.0 / 15.0), reads=["graw"], writes=["graw"])
                A2("dve", lambda e: e.tensor_scalar(out=graw[:], in0=graw[:], scalar1=15.0, scalar2=None, op0=ALU.mult), reads=["graw"], writes=["graw"])
                A2("act", lambda e: e.activation(out=glf[:], in_=graw[:], func=AF.Sigmoid), reads=["graw"], writes=["glf"])
                A2("act", lambda e: e.activation(out=glf[:], in_=glf[:], func=AF.Ln), reads=["glf"], writes=["glf"])
                A2("dve", lambda e: e.tensor_scalar(out=glf[:], in0=glf[:], scalar1=gsl[:, 1:2], scalar2=None, op0=ALU.mult), reads=["glf", "gsl"], writes=["glf"])
                A2("dve", lambda e: e.scalar_tensor_tensor(out=graw[:], in0=graw[:], scalar=gsl[:, 0:1], in1=glf[:], op0=ALU.mult, op1=ALU.add),
                   reads=["graw", "glf", "gsl"], writes=["graw"])
                if dbg:
                    A2("pool", lambda e: e.dma_start(out=gts[:, :], in_=graw[:]), reads=["graw"], writes=["d_gts"], dma_sem="st_g")
                for b_ in range(nblk):
                    A2("pe", lambda e, b_=b_: e.matmul(ps[0][:, b_ * 16:(b_ + 1) * 16], graw[0:16, b_ * 128:(b_ + 1) * 128], cst[0:16, 0:16], start=True, stop=True),
                       reads=["graw", "cst"], writes=["ps0"])
                gtok2 = gtok[:].rearrange("p b j -> p (b j)")
                A2("dve", lambda e: e.tensor_copy(out=gtok2[:, 0:nblk * 16], in_=ps[0][:, 0:nblk * 16]), reads=["ps0"], writes=["gtok"])
                for (bank, lhs, dst, lres, dname) in ((1, cst[:, 128:256], cumf, "cst", "cumf"), (2, cst[:, 384:512], cumb, "cst", "cumb"), (3, onesf[:], tot, "onesf", "tot")):
                    for b_ in range(nblk):
                        A2("pe", lambda e, b_=b_, bank=bank, lhs=lhs: e.matmul(ps[bank][:, b_ * 16:(b_ + 1) * 16], lhs, gtok[:, b_, :], start=True, stop=True),
                           reads=["gtok", lres], writes=[f"ps{bank}"])
                    d2 = dst[:].rearrange("p b j -> p (b j)")
                    A2("dve", lambda e, d2=d2, bank=bank: e.tensor_copy(out=d2[:, 0:nblk * 16], in_=ps[bank][:, 0:nblk * 16]), reads=[f"ps{bank}"], writes=[dname])
                A2("dve", lambda e: e.tensor_tensor(out=biasD[:, 0:nblk, 0:4], in0=gtok[:, 0:nblk, 0:4], in1=cumf[:, 0:nblk, 4:8], op=ALU.subtract),
                   reads=["gtok", "cumf", "cumb", "tot"], writes=["biasD"])
                A2("dve", lambda e: e.tensor_tensor(out=biasD[:, 0:nblk, 4:8], in0=gtok[:, 0:nblk, 8:12], in1=cumb[:, 0:nblk, 12:16], op=ALU.subtract),
                   reads=["gtok", "cumb", "biasD"], writes=["biasD"])
                A2("dve", lambda e: e.tensor_tensor(out=wk[:, 0:nblk, 0:4], in0=tot[:, 0:nblk, 4:8], in1=biasD[:, 0:nblk, 0:4], op=ALU.add),
                   reads=["biasD", "tot"], writes=["wk"])
                A2("dve", lambda e: e.tensor_tensor(out=wk[:, 0:nblk, 4:8], in0=tot[:, 0:nblk, 12:16], in1=biasD[:, 0:nblk, 4:8], op=ALU.add),
                   reads=["biasD", "tot", "wk"], writes=["wk"])
                A2("act", lambda e: e.activation(out=wk[:, 0:nblk, :], in_=wk[:, 0:nblk, :], func=AF.Exp), reads=["wk"], writes=["wk"])
                A2("act", lambda e: e.activation(out=dec[:, 0:nblk, :], in_=tot[:, 0:nblk, :], func=AF.Exp), reads=["tot"], writes=["dec"])
                A2("dve", lambda e: e.tensor_reduce(out=gsum[:], in_=tot[:, 0:nblk, :].rearrange("p b j -> p j b"), axis=mybir.AxisListType.X, op=ALU.add),
                   reads=["tot"], writes=["gsum"])
                A2("act", lambda e: e.activation(out=gexp[:], in_=gsum[:], func=AF.Exp), reads=["gsum"], writes=["gexp"])

                C2 = C[:].rearrange("p a b -> p (a b)")
                Cb2 = Cb[:].rearrange("p a b -> p (a b)")
                Cres = [f"C{h}" for h in range(4)]
                Cbres = [f"Cb{h}" for h in range(4)]

                def zero_state():
                    A2("dve", lambda e: e.memset(C2, 0.0), writes=Cres)
                    A2("dve", lambda e: e.memset(nst[:], 0.0), writes=["nst"])

                def cast_state():
                    A2("act", lambda e: e.copy(out=Cb2, in_=C2), reads=Cres, writes=Cbres)
                    A2("dve", lambda e: e.tensor_copy(out=nb[:], in_=nst[:]), reads=["nst"], writes=["nb"])

                dsem_qk = ["p2a", "p2b"]
                dsem_ktv = ["p2c", "p2d"]
                dsem_soh = ["p2e", "p2f"]

                def load_blk(pos, blk, outputs, dirn):
                    p = pos % 2
                    c0 = blk * 128
                    A2("sp", lambda e: [e.dma_start(out=ktb[p][:], in_=kts.ap()[blk]), e.dma_start(out=vtb[p][:], in_=vts.ap()[blk])],
                       reads=["d_kts", "d_vts"], writes=[f"ktb{p}", f"vtb{p}"], dma_sem=dsem_ktv[p], ndma=2)
                    if outputs:
                        A2("sp", lambda e: [e.dma_start(out=qb[p][:], in_=qs[:, :, c0:c0 + 128]), e.dma_start(out=kb[p][:], in_=ks[:, :, c0:c0 + 128])],
                           reads=["d_qs", "d_ks"], writes=[f"qb{p}", f"kb{p}"], dma_sem=dsem_qk[p], ndma=2)
                        if dirn == 0:
                            A2("sp", lambda e: [e.dma_start(out=sob[p][:], in_=ots.ap()[blk]), e.dma_start(out=hbb[p][:], in_=hbs.ap()[blk])],
                               reads=["d_ots", f"d_hbs{blk}"], writes=[f"sob{p}", f"hbb{p}"], dma_sem=dsem_soh[p], ndma=2)

                def scan_block(pos, blk, dirn, outputs):
                    p = pos % 2
                    U_ = cst[:, 128:256] if dirn == 0 else cst[:, 384:512]
                    M_ = cst[:, 256:384] if dirn == 0 else cst[:, 512:640]
                    for pair in ((0, 1), (2, 3)):
                        def st0(h):
                            par = h % 2
                            bE = 4 * par
                            gl = 8 * dirn + 4 + h
                            bi = 4 * dirn + h
                            if outputs:
                                A2("dve", lambda e: e.tensor_scalar(out=Rt[par][:], in0=U_, scalar1=gtok[:, blk, gl:gl + 1], scalar2=None, op0=ALU.mult),
                                   reads=["cst", "gtok"], writes=[f"Rt{par}"])
                                A2("pe", lambda e: e.matmul(ps[bE][:, 0:128], onesf[:], Rt[par][:], start=True, stop=True),
                                   reads=["onesf", f"Rt{par}"], writes=[f"ps{bE}"])
                                A2("pe", lambda e: e.matmul(ps[bE][:, 128:256], onesf[:], Rt[par][:], start=True, stop=False),
                                   reads=["onesf", f"Rt{par}"], writes=[f"ps{bE}"])
                                A2("pe", lambda e: e.matmul(ps[bE][:, 128:256], identf, M_, start=False, stop=True),
                                   reads=["cst"], writes=[f"ps{bE}"])
                                for half in range(2):
                                    A2("pe", lambda e, half=half: e.matmul(ps[bE][:, 256:384], kb[p][:, 2 * h + half, :], qb[p][:, 2 * h + half, :],
                                                                          start=(half == 0), stop=(half == 1)),
                                       reads=[f"kb{p}", f"qb{p}"], writes=[f"ps{bE}"])
                            A2("dve", lambda e: e.tensor_scalar(out=ktil[par][:], in0=ktb[p][:, h * 256:(h + 1) * 256], scalar1=wk[:, blk, bi:bi + 1], scalar2=None, op0=ALU.mult),
                               reads=[f"ktb{p}", "wk"], writes=[f"ktil{par}"])

                        def st1(h):
                            par = h % 2
                            bE = 4 * par
                            bi = 4 * dirn + h
                            if outputs and _OLEV >= 2:
                                A2("act", lambda e: e.activation(out=EBt[par][:], in_=ps[bE][:, 0:128], func=AF.Exp), reads=[f"ps{bE}"], writes=[f"EBt{par}"])
                                if _SUB >= 2:
                                  A2("act", lambda e: e.activation(out=DTt[par][:], in_=ps[bE][:, 128:256], func=AF.Exp, bias=biasD[:, blk, bi:bi + 1], scale=1.0),
                                   reads=[f"ps{bE}", "biasD"], writes=[f"DTt{par}"])
                                if _SUB >= 3:
                                  A2("dve", lambda e: e.tensor_tensor(out=ptl[par][:], in0=ps[bE][:, 256:384], in1=DTt[par][:], op=ALU.mult),
                                   reads=[f"ps{bE}", f"DTt{par}"], writes=[f"pt{par}"])
                                for half in range(2 if _SUB >= 4 else 0):
                                    A2("dve", lambda e, half=half: e.tensor_tensor(out=qtil[par][:, half, :], in0=qb[p][:, 2 * h + half, :], in1=EBt[par][:], op=ALU.mult),
                                       reads=[f"qb{p}", f"EBt{par}"], writes=[f"qtil{par}"])

                        def st2(h):
                            par = h % 2
                            bE, bN, bC0, bC1 = 4 * par, 4 * par + 1, 4 * par + 2, 4 * par + 3
                            vh = vtb[p][:, h * 512:(h + 1) * 512]
                            if outputs and _OLEV >= 3:
                                A2("pe", lambda e: e.matmul(ps[bN][:, :], ptl[par][:], vh, start=True, stop=False),
                                   reads=[f"pt{par}", f"vtb{p}"], writes=[f"ps{bN}"])
                                for half in range(2):
                                    A2("pe", lambda e, half=half: e.matmul(ps[bN][:, :], qtil[par][:, half, :], Cb[:, 2 * h + half, :], start=False, stop=(half == 1)),
                                       reads=[f"qtil{par}", f"Cb{h}"], writes=[f"ps{bN}"])
                                A2("pe", lambda e: e.matmul(ps[bE][:, 384:385], ptl[par][:], onesb[:, 0:1], start=True, stop=False),
                                   reads=[f"pt{par}", "onesb"], writes=[f"ps{bE}"])
                                for half in range(2):
                                    A2("pe", lambda e, half=half: e.matmul(ps[bE][:, 384:385], qtil[par][:, half, :], nb[:, 2 * h + half:2 * h + half + 1], start=False, stop=(half == 1)),
                                       reads=[f"qtil{par}", "nb"], writes=[f"ps{bE}"])
                            for half, bC in ((0, bC0), (1, bC1)):
                                A2("pe", lambda e, half=half, bC=bC: e.matmul(ps[bC][:, :], ktil[par][:, half * 128:(half + 1) * 128], vh, start=True, stop=True),
                                   reads=[f"ktil{par}", f"vtb{p}"], writes=[f"ps{bC}"])
                                A2("pe", lambda e, half=half: e.matmul(ps[bE][:, 386 + half:387 + half], ktil[par][:, half * 128:(half + 1) * 128], onesb[:, 0:1], start=True, stop=True),
                                   reads=[f"ktil{par}", "onesb"], writes=[f"ps{bE}"])

                        def st3(h):
                            par = h % 2
                            bE, bN, bC0, bC1 = 4 * par, 4 * par + 1, 4 * par + 2, 4 * par + 3
                            gl = 8 * dirn + 4 + h
                            hc_ = slice(h * 512, (h + 1) * 512)
                            if outputs and _OLEV >= 4:
                                A2("act", lambda e: e.activation(out=dab[par][:], in_=ps[bE][:, 384:385], func=AF.Abs), reads=[f"ps{bE}"], writes=[f"dab{par}"])
                                A2("dve", lambda e: e.tensor_scalar_max(out=dab[par][:], in0=dab[par][:], scalar1=1.0), reads=[f"dab{par}"], writes=[f"dab{par}"])
                                A2("dve", lambda e: e.reciprocal(out=rden[par][:], in_=dab[par][:]), reads=[f"dab{par}"], writes=[f"rden{par}"])
                                if dirn == 1:
                                    A2("act", lambda e: e.activation(out=hbb[p][:, hc_], in_=ps[bN][:, :], func=AF.Copy, scale=rden[par][:, 0:1]),
                                       reads=[f"ps{bN}", f"rden{par}"], writes=[f"hbb{p}"])
                                else:
                                    A2("dve", lambda e: e.scalar_tensor_tensor(out=hbb[p][:, hc_], in0=ps[bN][:, :], scalar=rden[par][:, 0:1], in1=hbb[p][:, hc_],
                                                                              op0=ALU.mult, op1=ALU.add),
                                       reads=[f"ps{bN}", f"rden{par}", f"hbb{p}"], writes=[f"hbb{p}"])
                            for half, bC in ((0, bC0), (1, bC1)):
                                A2("dve", lambda e, half=half, bC=bC: e.scalar_tensor_tensor(out=C[:, 2 * h + half, :], in0=C[:, 2 * h + half, :], scalar=dec[:, blk, gl:gl + 1],
                                                                                          in1=ps[bC][:, :], op0=ALU.mult, op1=ALU.add),
                                   reads=[f"C{h}", "dec", f"ps{bC}"], writes=[f"C{h}"])
                            A2("dve", lambda e: e.scalar_tensor_tensor(out=nst[:, 2 * h:2 * h + 2], in0=nst[:, 2 * h:2 * h + 2], scalar=dec[:, blk, gl:gl + 1],
                                                                      in1=ps[bE][:, 386:388], op0=ALU.mult, op1=ALU.add),
                               reads=["nst", "dec", f"ps{bE}"], writes=["nst"])
                            A2("act", lambda e: e.copy(out=Cb[:, 2 * h, :], in_=C[:, 2 * h, :]), reads=[f"C{h}"], writes=[f"Cb{h}"])
                            A2("pool", lambda e: e.tensor_copy(out=Cb[:, 2 * h + 1, :], in_=C[:, 2 * h + 1, :]), reads=[f"C{h}"], writes=[f"Cb{h}"])
                            A2("pool", lambda e: e.tensor_copy(out=nb[:, 2 * h:2 * h + 2], in_=nst[:, 2 * h:2 * h + 2]), reads=["nst"], writes=["nb"])

                        for st in (st0, st1, st2, st3):
                            for h in pair:
                                st(h)

                def sweep(dirn, outputs, post=None):
                    order = list(range(nblk)) if dirn == 0 else list(range(nblk - 1, -1, -1))
                    load_blk(0, order[0], outputs, dirn)
                    for pos, blk in enumerate(order):
                        if pos + 1 < len(order):
                            load_blk(pos + 1, order[pos + 1], outputs, dirn)
                        scan_block(pos, blk, dirn, outputs)
                        if post is not None:
                            post(pos, blk)

                def store_state(dirn):
                    o = dirn * SW
                    A2("pool", lambda e: [e.dma_start(out=sall[ex * 128:(ex + 1) * 128, o:o + 4096], in_=C2),
                                          e.dma_start(out=sall[ex * 128:(ex + 1) * 128, o + 4096:o + 4104], in_=nst[:]),
                                          e.dma_start(out=sall[ex * 128:(ex + 1) * 128, o + 4104:o + 4112], in_=gexp[:, dirn * 8:dirn * 8 + 8])],
                       reads=Cres + ["nst", "gexp"], writes=["d_sloc"], dma_sem="st_S", ndma=3)

                def combine(dirn):
                    zero_state()
                    order = list(range(nextra)) if dirn == 0 else list(range(nextra - 1, -1, -1))
                    o = dirn * SW
                    for n_, i in enumerate(order):
                        pp = n_ % 2
                        A2("sp", lambda e, i=i, pp=pp: e.dma_start(out=sstg[pp][:], in_=sall[i * 128:(i + 1) * 128, o:o + SW]),
                           reads=["d_sall"], writes=[f"sstg{pp}"], dma_sem=f"ld_S{pp}")
                        a_ = prd[:, dirn * 8 + i:dirn * 8 + i + 1]
                        A2("dve", lambda e, pp=pp, a_=a_: e.tensor_scalar(out=coef[:], in0=sstg[pp][:, 4108:4112], scalar1=-1.0, scalar2=a_, op0=ALU.add, op1=ALU.mult),
                           reads=[f"sstg{pp}", "prd"], writes=["coef"])
                        A2("dve", lambda e: e.tensor_scalar_add(out=coef[:], in0=coef[:], scalar1=1.0), reads=["coef"], writes=["coef"])
                        for hh in range(8):
                            h = hh // 2
                            A2("dve", lambda e, hh=hh, h=h: e.tensor_scalar(out=C[:, hh, :], in0=C[:, hh, :], scalar1=coef[:, h:h + 1], scalar2=None, op0=ALU.mult),
                               reads=[f"C{h}", "coef"], writes=[f"C{h}"])
                            A2("dve", lambda e, hh=hh, h=h, pp=pp, a_=a_: e.scalar_tensor_tensor(out=C[:, hh, :], in0=sstg[pp][:, hh * 512:(hh + 1) * 512], scalar=a_, in1=C[:, hh, :],
                                                                                             op0=ALU.mult, op1=ALU.add),
                               reads=[f"C{h}", f"sstg{pp}", "prd"], writes=[f"C{h}"])
                        for h in range(4):
                            A2("dve", lambda e, h=h: e.tensor_scalar(out=nst[:, 2 * h:2 * h + 2], in0=nst[:, 2 * h:2 * h + 2], scalar1=coef[:, h:h + 1], scalar2=None, op0=ALU.mult),
                               reads=["nst", "coef"], writes=["nst"])
                        A2("dve", lambda e, pp=pp, a_=a_: e.scalar_tensor_tensor(out=nst[:], in0=sstg[pp][:, 4096:4104], scalar=a_, in1=nst[:], op0=ALU.mult, op1=ALU.add),
                           reads=["nst", f"sstg{pp}", "prd"], writes=["nst"])
                    cast_state()

                if mode == "local":
                    zero_state()
                    cast_state()
                    sweep(0, False)
                    store_state(0)
                    zero_state()
                    cast_state()
                    sweep(1, False)
                    store_state(1)
                else:
                    combine(1)

                def post_b(pos, blk):
                    p = pos % 2
                    A2("pool", lambda e: e.dma_start(out=hbs.ap()[blk], in_=hbb[p][:]), reads=[f"hbb{p}"], writes=[f"d_hbs{blk}"], dma_sem=f"st_hb{p}")
                if mode == "main":
                    sweep(1, True, post_b)

                if mode == "main":
                    combine(0)

                def post_f(pos, blk):
                    p = pos % 2
                    A2("dve", lambda e: e.memset(ssq[:], 0.0), writes=["ssq"])
                    for h in range(4):
                        hc_ = slice(h * 512, (h + 1) * 512)
                        A2("act", lambda e, h=h, hc_=hc_: e.activation(out=junk[:], in_=hbb[p][:, hc_], func=AF.Square, accum_out=ssq[:, h:h + 1]),
                           reads=[f"hbb{p}", "ssq"], writes=["junk", "ssq"])
                    A2("dve", lambda e: e.tensor_scalar(out=rn[:], in0=ssq[:], scalar1=1.0 / DV, scalar2=EPS, op0=ALU.mult, op1=ALU.add), reads=["ssq"], writes=["rn"])
                    A2("act", lambda e: e.activation(out=rn[:], in_=rn[:], func=AF.Sqrt), reads=["rn"], writes=["rn"])
                    A2("dve", lambda e: e.reciprocal(out=rn[:], in_=rn[:]), reads=["rn"], writes=["rn"])
                    for h in range(4):
                        hc_ = slice(h * 512, (h + 1) * 512)
                        t2 = h % 2
                        A2("dve", lambda e, h=h, hc_=hc_, t2=t2: e.scalar_tensor_tensor(out=tmpn[t2][:], in0=hbb[p][:, hc_], scalar=rn[:, h:h + 1], in1=mnt[:, hc_],
                                                                                   op0=ALU.mult, op1=ALU.mult),
                           reads=[f"hbb{p}", "rn", "mnt"], writes=[f"tmpn{t2}"])
                        A2("pool", lambda e, hc_=hc_, t2=t2: e.tensor_tensor(out=gtb[:, hc_], in0=tmpn[t2][:], in1=sob[p][:, hc_], op=ALU.mult),
                           reads=[f"tmpn{t2}", f"sob{p}"], writes=["gtb"])
                    for half8 in range(2):
                        bank = 1 if half8 == 0 else 5
                        pb = ps[bank][:].bitcast(BF16)
                        for j in range(8):
                            jj = half8 * 8 + j
                            A2("pe", lambda e, j=j, jj=jj, pb=pb: e.transpose(pb[:, j * 128:(j + 1) * 128], gtb[:, jj * 128:(jj + 1) * 128], identb[:]),
                               reads=["gtb", "identb"], writes=[f"ps{bank}"])
                        A2("act" if half8 == 0 else "dve",
                           (lambda e, pb=pb, half8=half8: e.copy(out=gTst[p][:, half8 * 8:(half8 + 1) * 8, :], in_=pb[:, 0:1024].rearrange("p (a b) -> p a b", a=8))) if half8 == 0 else
                           (lambda e, pb=pb, half8=half8: e.tensor_copy(out=gTst[p][:, half8 * 8:(half8 + 1) * 8, :], in_=pb[:, 0:1024].rearrange("p (a b) -> p a b", a=8))),
                           reads=[f"ps{bank}"], writes=[f"gTst{p}"])
                    A2("pool", lambda e: e.dma_start(out=gTd[:, :, blk * 128:(blk + 1) * 128], in_=gTst[p][:]), reads=[f"gTst{p}"], writes=["d_gTd"], dma_sem=f"st_gT{p}")
                if mode == "main":
                    sweep(0, True, post_f)

                S2.finalize()
                with nc.Block() as blk2:
                    S2.emit(nc, blk2, sems, base)

        for ex_ in range(nextra):
            run_block1(ex_, True, ex_ == 0)
            run_block2("local", ex_)
        run_block1(3, False, nextra == 0)
        if not do_phase2:
            return nc
        run_block2("main", None)
        if p2stage < 6:
            return nc
        with ExitStack() as es3:
            new_block_sems()
            def sb3(name, shape, dt=F32):
                return es3.enter_context(nc.sbuf_tensor(name, list(shape), dt))
            xt = sb3("xt3", [128, KC, T])
            zt = sb3("zt3", [128, KC, T], BF16)
            hid = sb3("hid3", [128, FC, T], BF16)
            wf = [sb3(f"wf3_{i}", [128, 2048]) for i in range(WStream.NF)]
            wb = [sb3(f"wb3_{i}", [128, 2048], BF16) for i in range(WStream.NB)]
            rstd = sb3("rstd3", [128, T])
            tmp = [sb3(f"tmp3_{i}", [128, T]) for i in range(2)]
            sg = [sb3(f"sg3_{i}", [128, T]) for i in range(2)]
            plan3 = []
            for i in range(ntiles):
                for m in range(KC):
                    plan3.append((mwout_r[:, :, m * 128:(m + 1) * 128], 16, 128))
                plan_ffn(1, plan3)
            S3 = Sched()
            W3 = WStream(S3, wf, wb, plan3)
            rmsnorm_mod, mm_slab, ffn, nbank = make_helpers(S3, W3, xt, zt, hid, rstd, None, tmp, None, sg, None, None)
            for i in range(ntiles):
                c0 = i * T
                S3.add("pool", lambda e, c0=c0: e.dma_start(out=xt[:], in_=x1s_r[:, :, c0:c0 + T]),
                       writes=[f"x{k}" for k in range(KC)], dma_sem="x")
                S3.add("pool", lambda e, c0=c0: e.dma_start(out=zt[:], in_=gTd[:, :, c0:c0 + T]),
                       writes=[f"z{k}" for k in range(KC)], dma_sem="xh")
                for m in range(KC):
                    b = nbank()
                    wv, wres = W3.next(16, 128)
                    mm_slab(b, wv, wres, lambda k: zt[:, k, :], lambda k: f"z{k}", 16, True, True)
                    S3.add("dve", lambda e, b=b, m=m: e.scalar_tensor_tensor(
                        out=xt[:, m, :], in0=ps[b][:, :], scalar=modt[:, 96 + 32 + m:96 + 33 + m], in1=xt[:, m, :], op0=ALU.mult, op1=ALU.add),
                        reads=[f"ps{b}", f"x{m}", "modt"], writes=[f"x{m}"])
                rmsnorm_mod(avec[:, 48:64], modt[:, 96 + 48:96 + 64])
                ffn(1, modt[:, 96 + 80:96 + 96])
                rmsnorm_mod(avec[:, 64:80], modt[:, 192:208], outx=True)
                S3.add("pool", lambda e, c0=c0: e.dma_start(out=yT_r[:, :, c0:c0 + T], in_=xt[:]),
                       reads=[f"x{k}" for k in range(KC)], writes=["d_y"], dma_sem="st_y")
            assert W3.used == len(plan3)
            S3.finalize()
            with nc.Block() as blk3:
                S3.emit(nc, blk3, sems, base)
    return nc


def _prep_inputs(inp):
    f32 = np.float32
    xp = np.asarray(inp["x_prompt"], f32)[0]
    xs = np.asarray(inp["x_sample"], f32)
    cp = np.asarray(inp["c_prompt"], f32)
    csm = np.asarray(inp["c_sample"], f32)
    seqs = [xp, xs[0], xs[1]]
    cvecs = [cp[0], csm[0], csm[1]]
    core_seq = [0, 0, 0, 0, 1, 1, 2, 2]
    core_off = [0, 4096, 8192, 12288, 0, 4096, 0, 4096]

    def fm(v):
        return np.ascontiguousarray(np.asarray(v, f32).reshape(-1, 128).T)

    norm_g = np.asarray(inp["norm_g"], f32)
    smallv = np.concatenate([
        fm(norm_g.reshape(-1)), fm(np.asarray(inp["ada_b"], f32).reshape(-1)),
        fm(inp["final_ada_b"]), fm(inp["final_g"]), fm(np.asarray(inp["conv_w"], f32).reshape(-1))], axis=1)
    assert smallv.shape == (128, 352)
    mnorm = np.ascontiguousarray(np.broadcast_to(np.asarray(inp["mlstm_norm"], f32).reshape(1, D), (128, D)))
    bgate = np.asarray(inp["mlstm_b_gate"], f32).reshape(16, 1)
    gsel = np.zeros((16, 2), f32)
    gsel[[0, 1, 2, 3, 8, 9, 10, 11], 0] = 1.0
    gsel[[4, 5, 6, 7, 12, 13, 14, 15], 1] = 1.0
    s_idx = np.arange(128)[:, None]
    t_idx = np.arange(128)[None, :]
    U = (s_idx <= t_idx).astype(f32)
    consts = np.concatenate([np.eye(128, dtype=f32), U, (1.0 - U) * NEG, U.T, (1.0 - U.T) * NEG], axis=1).astype(f32)
    shared = dict(
        smallv=smallv, mnorm=mnorm, bgate=bgate, gsel=gsel, consts=consts,
        ada_w=np.asarray(inp["ada_w"], f32), final_ada_w=np.asarray(inp["final_ada_w"], f32),
        ffn_w_in=np.asarray(inp["ffn_w_in"], f32), ffn_w_out=np.asarray(inp["ffn_w_out"], f32),
        conv_w_in=np.asarray(inp["conv_w_in"], f32)[0], conv_w_out=np.asarray(inp["conv_w_out"], f32)[0],
        mlstm_w_in=np.asarray(inp["mlstm_w_in"], f32)[0], mlstm_w_out=np.asarray(inp["mlstm_w_out"], f32)[0])
    def chunk_arrays(sq, o):
        xT = np.ascontiguousarray(sq[o:o + NTOK].T)
        halo = np.zeros((NTILE, 2, D), f32)
        em = np.ones((NTILE, 2), f32)
        for i in range(NTILE):
            l = o + i * T - 1
            r = o + (i + 1) * T
            if l >= 0:
                halo[i, 0] = sq[l]
            else:
                em[i, 0] = 0.0
            if r < sq.shape[0]:
                halo[i, 1] = sq[r]
            else:
                em[i, 1] = 0.0
        xh = np.ascontiguousarray(halo.reshape(NTILE, 2, KC, 128).transpose(3, 0, 2, 1).reshape(128, NTILE * KC * 2))
        emask = np.ascontiguousarray(np.broadcast_to(em.reshape(1, NTILE * 2), (128, NTILE * 2)))
        return xT, xh, emask

    cache = {}
    for c in range(NCORE):
        cache[c] = chunk_arrays(seqs[core_seq[c]], core_off[c])
    in_maps = []
    for c in range(NCORE):
        others = [c2 for c2 in range(NCORE) if core_seq[c2] == core_seq[c] and c2 != c]
        pf = np.zeros(8, f32)
        pb = np.zeros(8, f32)
        ex = []
        for e_ in range(3):
            if e_ < len(others):
                c2 = others[e_]
                if c2 < c:
                    pf[e_] = 1.0
                else:
                    pb[e_] = 1.0
            else:
                c2 = others[0]
            ex.append(c2)
        pred = np.ascontiguousarray(np.broadcast_to(np.concatenate([pf, pb]).reshape(1, 16), (128, 16)))
        xT, xh, emask = cache[c]
        m = dict(shared)
        m.update(xT=xT, xhalo=xh, emask=emask, cvec=fm(cvecs[core_seq[c]]), pred=pred,
                 xTe=np.stack([cache[c2][0] for c2 in ex]), xhalo_e=np.stack([cache[c2][1] for c2 in ex]),
                 emask_e=np.stack([cache[c2][2] for c2 in ex]))
        in_maps.append(m)
    return in_maps, core_seq, core_off


def kernel(**inputs):
    in_maps, core_seq, core_off = _prep_inputs(inputs)
    nc = build_nc()
    res = run_bass_kernel_spmd(nc, in_maps, core_ids=list(range(NCORE)))
    yp = np.empty((1, 16384, D), np.float32)
    ys = np.empty((2, 8192, D), np.float32)
    for c in range(NCORE):
        y = np.asarray(res.results[c]["yT"], np.float32).T
        o = core_off[c]
        if core_seq[c] == 0:
            yp[0, o:o + NTOK] = y
        else:
            ys[core_seq[c] - 1, o:o + NTOK] = y
    return (yp, ys)
```

```python
import numpy as np
import concourse.bass as bass
import concourse.mybir as mybir
from concourse.bass_utils import run_bass_kernel_spmd

F32 = mybir.dt.float32
BF16 = mybir.dt.bfloat16
AF = mybir.ActivationFunctionType
ALU = mybir.AluOpType

D = 2048
KC = 16
FF = 5632
FC = 44
T = 512
NTOK = 4096
NTILE = NTOK // T
NBLK = NTOK // 128
NCORE = 8
H = 4
DQK = 256
DV = 512
EPS = 1e-6
MIN_ = 6160
NEG = -30000.0
_OLEV = 4
_SUB = 4


class Op:
    __slots__ = ("eng", "fn", "deps", "signal", "ticket", "dma_sem", "ndma", "idx", "inc")


class Sched:
    ENGS = ("pe", "act", "dve", "pool", "sp")

    def __init__(self):
        self.ops = {e: [] for e in self.ENGS}
        self.last_w = {}
        self.readers = {}
        self.dma_sems = []

    def add(self, eng, fn, reads=(), writes=(), dma_sem=None, ndma=1, inc=16):
        op = Op()
        op.inc = inc
        op.eng = eng
        op.fn = fn
        op.signal = False
        op.ticket = None
        op.dma_sem = dma_sem
        op.ndma = ndma
        if dma_sem is not None and dma_sem not in self.dma_sems:
            self.dma_sems.append(dma_sem)
        deps = []
        for r in reads:
            w = self.last_w.get(r)
            if w is not None:
                deps.append(w)
        for w_ in writes:
            w = self.last_w.get(w_)
            if w is not None:
                deps.append(w)
            rd = self.readers.get(w_)
            if rd:
                deps.extend(rd.values())
        op.deps = deps
        for d in deps:
            d.signal = True
        for r in reads:
            rd = self.readers.setdefault(r, {})
            key = eng if dma_sem is None else ("dma", dma_sem)
            rd[key] = op
        for w_ in writes:
            self.last_w[w_] = op
            self.readers[w_] = {}
        op.idx = len(self.ops[eng])
        self.ops[eng].append(op)
        return op

    def finalize(self):
        cnt = {e: 0 for e in self.ENGS}
        dcnt = {}
        for e in self.ENGS:
            lst = self.ops[e]
            if lst and lst[-1].dma_sem is None:
                lst[-1].signal = True
            for op in lst:
                if op.dma_sem is not None:
                    dcnt[op.dma_sem] = dcnt.get(op.dma_sem, 0) + op.inc * op.ndma
                    op.ticket = dcnt[op.dma_sem]
                elif op.signal:
                    cnt[e] += 1
                    op.ticket = cnt[e]
        self.final_cnt = cnt
        self.final_dcnt = dcnt

    def emit(self, nc, blk, sems, base):
        engobj = {"pe": blk.tensor, "act": blk.scalar, "dve": blk.vector, "pool": blk.gpsimd, "sp": blk.sync}
        S = self

        def make(ename):
            def body(e):
                waited = {}
                for op in S.ops[ename]:
                    for d in op.deps:
                        if d.dma_sem is not None:
                            key = d.dma_sem
                        else:
                            key = d.eng
                            if d.eng == "pe" and ename == "pe":
                                continue
                        val = d.ticket + base.get(key, 0)
                        if waited.get(key, -1) >= val:
                            continue
                        waited[key] = val
                        e.wait_ge(sems[key], val)
                    r = op.fn(e)
                    if op.dma_sem is not None:
                        if not isinstance(r, (list, tuple)):
                            r = [r]
                        assert len(r) == op.ndma
                        for ins in r:
                            ins.then_inc(sems[op.dma_sem], op.inc)
                    elif op.signal:
                        if isinstance(r, (list, tuple)):
                            r = r[-1]
                        r.then_inc(sems[ename], 1)
                for k, v in S.final_cnt.items():
                    if v > 0 and not (k == ename):
                        e.wait_ge(sems[k], v + base.get(k, 0))
                for k, v in S.final_dcnt.items():
                    e.wait_ge(sems[k], v + base.get(k, 0))
            return body

        for ename in self.ENGS:
            engobj[ename](make(ename))
        for k, v in self.final_cnt.items():
            base[k] = base.get(k, 0) + v
        for k, v in self.final_dcnt.items():
            base[k] = base.get(k, 0) + v


class WStream:
    NF = 4
    NB = 4
    LA = 3

    def __init__(self, S, wf, wb, plan):
        self.S = S
        self.wf = wf
        self.wb = wb
        self.plan = plan
        self.issued = 0
        self.used = 0

    def _issue(self, i):
        src, nk, ncol = self.plan[i]
        sf = i % self.NF
        sb = i % self.NB
        vf = self.wf[sf][:, 0:nk * ncol].rearrange("p (k c) -> p k c", k=nk)
        vb = self.wb[sb][:, 0:nk * ncol].rearrange("p (k c) -> p k c", k=nk)
        self.S.add("sp", lambda e, vf=vf, src=src: e.dma_start(out=vf, in_=src),
                   writes=[f"wf{sf}"], dma_sem=f"wf{sf}")
        ce = ("act", "dve", "pool")[i % 3]
        if ce == "act":
            fn = lambda e, vb=vb, vf=vf: e.copy(out=vb, in_=vf)
        else:
            fn = lambda e, vb=vb, vf=vf: e.tensor_copy(out=vb, in_=vf)
        self.S.add(ce, fn, reads=[f"wf{sf}"], writes=[f"wb{sb}"])

    def next(self, nk, ncol):
        i = self.used
        assert self.plan[i][1] == nk and self.plan[i][2] == ncol, (i, self.plan[i][1:], nk, ncol)
        while self.issued < min(len(self.plan), i + self.LA + 1):
            self._issue(self.issued)
            self.issued += 1
        self.used += 1
        sb = i % self.NB
        vb = self.wb[sb][:, 0:nk * ncol].rearrange("p (k c) -> p k c", k=nk)
        return vb, f"wb{sb}"


def build_nc(ntiles=NTILE, dbg=False, do_phase2=True, nextra=3, p2stage=6):
    nc = bass.Bass("TRN2", target_bir_lowering=False)

    def din(name, shape, dt=F32):
        return nc.dram_tensor(name, list(shape), dt, kind="ExternalInput")

    xT = din("xT", [D, NTOK])
    xTe = din("xTe", [3, D, NTOK])
    xhalo_e = din("xhalo_e", [3, 128, NTILE * KC * 2])
    emask_e = din("emask_e", [3, 128, NTILE * 2])
    xhalo = din("xhalo", [128, NTILE * KC * 2])
    emask = din("emask", [128, NTILE * 2])
    cvec = din("cvec", [128, KC])
    smallv = din("smallv", [128, 64 + 192 + 32 + 16 + 48])
    mnorm = din("mnorm", [128, D])
    bgate = din("bgate", [16, 1])
    gsel = din("gsel", [16, 2])
    consts = din("consts", [128, 5 * 128])
    pred = din("pred", [128, 16])
    ada_w = din("ada_w", [2, D, 6 * D])
    final_ada_w = din("final_ada_w", [D, 2 * D])
    ffn_w_in = din("ffn_w_in", [2, D, 2 * FF])
    ffn_w_out = din("ffn_w_out", [2, FF, D])
    conv_w_in = din("conv_w_in", [D, 3 * D])
    conv_w_out = din("conv_w_out", [D, D])
    mlstm_w_in = din("mlstm_w_in", [D, MIN_])
    mlstm_w_out = din("mlstm_w_out", [D, D])

    yT = nc.dram_tensor("yT", [D, NTOK], F32, kind="ExternalOutput")

    okind = "ExternalOutput" if dbg else "Internal"
    x1s = nc.dram_tensor("x1s", [D, NTOK], F32, kind=okind)
    qs = nc.dram_tensor("qs", [128, 8, NTOK], BF16, kind=okind)
    ks = nc.dram_tensor("ks", [128, 8, NTOK], BF16, kind=okind)
    kts = nc.dram_tensor("kts", [NBLK, 128, 1024], BF16, kind=okind)
    vts = nc.dram_tensor("vts", [NBLK, 128, D], BF16, kind=okind)
    ots = nc.dram_tensor("ots", [NBLK, 128, D], BF16, kind=okind)
    gts = nc.dram_tensor("gts", [16, NTOK], F32, kind=okind)

    SW = 4096 + 16
    hbs = nc.dram_tensor("hbs", [NBLK, 128, D], F32, kind="Internal")
    gTd = nc.dram_tensor("gTd", [128, KC, NTOK], BF16, kind="Internal")
    sloc = nc.dram_tensor("sloc", [128, 2 * SW], F32, kind="Internal")
    sall = nc.dram_tensor("sall", [3 * 128, 2 * SW], F32, kind="Internal")
    xT_r = xT.ap().rearrange("(k p) t -> p k t", p=128)
    x1s_r = x1s.ap().rearrange("(k p) t -> p k t", p=128)
    yT_r = yT.ap().rearrange("(k p) t -> p k t", p=128)
    xsrcs = [xTe.ap()[e_].rearrange("(k p) t -> p k t", p=128) for e_ in range(3)] + [xT_r]
    xhsrcs = [xhalo_e.ap()[e_] for e_ in range(3)] + [xhalo.ap()]
    emsrcs = [emask_e.ap()[e_] for e_ in range(3)] + [emask.ap()]

    def wr(w):
        return w.rearrange("(k p) n -> p k n", p=128)

    adaw_r = [wr(ada_w.ap()[l]) for l in range(2)]
    fadaw_r = wr(final_ada_w.ap())
    fwin_r = [wr(ffn_w_in.ap()[l]) for l in range(2)]
    fwout_r = [wr(ffn_w_out.ap()[l]) for l in range(2)]
    cwin_r = wr(conv_w_in.ap())
    cwout_r = wr(conv_w_out.ap())
    mwin_r = wr(mlstm_w_in.ap())
    mwout_r = wr(mlstm_w_out.ap())

    def plan_ffn(l, plan):
        for j in range(FC):
            plan.append((fwin_r[l][:, :, j * 128:(j + 1) * 128], 16, 128))
            plan.append((fwin_r[l][:, :, FF + j * 128:FF + (j + 1) * 128], 16, 128))
        for m in range(KC):
            for (k0, nk) in ((0, 16), (16, 16), (32, 12)):
                plan.append((fwout_r[l][:, k0:k0 + nk, m * 128:(m + 1) * 128], nk, 128))

    def make_plan1(lite):
        plan1 = []
        for i in range(ntiles):
            for m in range(KC):
                for part in range(3):
                    plan1.append((cwin_r[:, :, part * D + m * 128: part * D + (m + 1) * 128], 16, 128))
            for m in range(KC):
                plan1.append((cwout_r[:, :, m * 128:(m + 1) * 128], 16, 128))
            plan_ffn(0, plan1)
            for c in range(8 if lite else 0, 16):
                plan1.append((mwin_r[:, :, c * 128:(c + 1) * 128], 16, 128))
            for g in range(4 if lite else 8):
                for q4 in range(4):
                    plan1.append((mwin_r[:, q4 * 4:(q4 + 1) * 4, 2048 + g * 512: 2048 + (g + 1) * 512], 4, 512))
        return plan1

    sem_names = ["pe", "act", "dve", "pool", "sp"]
    base = {}

    from contextlib import ExitStack
    with ExitStack() as es:
        def sb(name, shape, dt=F32):
            return es.enter_context(nc.sbuf_tensor(name, list(shape), dt))

        cs = sb("cs", [128, KC])
        small = sb("small", [128, 352])
        modt = sb("modt", [128, 224])
        avec = sb("avec", [128, 5 * KC])
        cst = sb("cst", [128, 640])
        identb = sb("identb", [128, 128], BF16)
        onesb = sb("onesb", [128, 128], BF16)
        emk = sb("emk", [128, NTILE * 2])
        wgb = sb("wgb", [128, KC, 16], BF16)
        wgf = sb("wgf", [128, KC, 16])
        bgt = sb("bgt", [16, 1])
        gsl = sb("gsl", [16, 2])
        graw = sb("graw", [16, NTOK])
        ps = [es.enter_context(nc.psum_tensor(f"ps{i}", [128, 512], F32)) for i in range(8)]
        sems = {}
        blkno = [0]

        def new_block_sems():
            blkno[0] += 1
            names = sem_names + ["wf0", "wf1", "wf2", "wf3"]
            for n_ in names:
                sems[n_] = es.enter_context(nc.semaphore(f"s{blkno[0]}_" + n_))
                base[n_] = 0
        dsem_pool = ["x", "xh", "misc", "st_x1", "st_qk", "st_kt", "st_v",
                     "st_o", "st_g", "mw0", "mw1", "mw2", "cc", "p2a", "p2b", "p2c", "p2d", "p2e", "p2f",
                     "st_y", "st_hb0", "st_hb1", "st_S", "st_k", "st_gT0", "st_gT1", "ld_S0", "ld_S1", "ld_m"]
        for n_ in dsem_pool:
            sems[n_] = es.enter_context(nc.semaphore("d_" + n_))

        norm_g_v = small[:, 0:64]
        ada_b_v = small[:, 64:256]
        fada_b_v = small[:, 256:288]
        final_g_v = small[:, 288:304]
        conv_w_v = small[:, 304:352]

        def make_helpers(S, W, xt, zt, hid, rstd, rstdh, tmp, tmph, sg, xh, zh):
            bank_rr = [0]

            def nbank():
                b = bank_rr[0] % 6
                bank_rr[0] += 1
                return b

            def rmsnorm_mod(avw, shv, with_halo=False, outx=False):
                for kc in range(KC):
                    S.add("act", lambda e, kc=kc: e.activation(out=zt[:, kc, :], in_=xt[:, kc, :], func=AF.Square),
                          reads=[f"x{kc}"], writes=[f"z{kc}"])
                for kc in range(KC):
                    S.add("pe", lambda e, kc=kc: e.matmul(ps[6][:, :], onesb[:], zt[:, kc, :], start=(kc == 0), stop=(kc == KC - 1)),
                          reads=[f"z{kc}", "onesb"], writes=["ps6"])
                S.add("dve", lambda e: e.tensor_scalar(out=rstd[:], in0=ps[6][:, :], scalar1=1.0 / D, scalar2=EPS,
                                                       op0=ALU.mult, op1=ALU.add), reads=["ps6"], writes=["rstd"])
                S.add("act", lambda e: e.activation(out=rstd[:], in_=rstd[:], func=AF.Sqrt), reads=["rstd"], writes=["rstd"])
                S.add("dve", lambda e: e.reciprocal(out=rstd[:], in_=rstd[:]), reads=["rstd"], writes=["rstd"])
                if with_halo:
                    S.add("act", lambda e: e.activation(out=zh[:], in_=xh[:], func=AF.Square), reads=["xh"], writes=["zh"])
                    for kc in range(KC):
                        S.add("pe", lambda e, kc=kc: e.matmul(ps[7][:, 0:2], onesb[:], zh[:, kc, :], start=(kc == 0), stop=(kc == KC - 1)),
                              reads=["zh", "onesb"], writes=["ps7"])
                    S.add("dve", lambda e: e.tensor_scalar(out=rstdh[:], in0=ps[7][:, 0:2], scalar1=1.0 / D, scalar2=EPS,
                                                           op0=ALU.mult, op1=ALU.add), reads=["ps7"], writes=["rstdh"])
                    S.add("act", lambda e: e.activation(out=rstdh[:], in_=rstdh[:], func=AF.Sqrt), reads=["rstdh"], writes=["rstdh"])
                    S.add("dve", lambda e: e.reciprocal(out=rstdh[:], in_=rstdh[:]), reads=["rstdh"], writes=["rstdh"])
                for kc in range(KC):
                    tb = kc % 2
                    S.add("dve", lambda e, kc=kc, tb=tb: e.scalar_tensor_tensor(
                        out=tmp[tb][:], in0=xt[:, kc, :], scalar=avw[:, kc:kc + 1], in1=rstd[:], op0=ALU.mult, op1=ALU.mult),
                        reads=[f"x{kc}", "rstd", "avec"], writes=[f"tmp{tb}"])
                    S.add("act", lambda e, kc=kc, tb=tb: e.activation(out=(xt if outx else zt)[:, kc, :], in_=tmp[tb][:], func=AF.Identity,
                                                                    bias=shv[:, kc:kc + 1], scale=1.0),
                          reads=[f"tmp{tb}", "modt"], writes=[(f"x{kc}" if outx else f"z{kc}")])
                    if with_halo:
                        S.add("dve", lambda e, kc=kc: e.scalar_tensor_tensor(
                            out=tmph[:], in0=xh[:, kc, :], scalar=avw[:, kc:kc + 1], in1=rstdh[:], op0=ALU.mult, op1=ALU.mult),
                            reads=["xh", "rstdh", "avec"], writes=["tmph"])
                        S.add("act", lambda e, kc=kc: e.activation(out=zh[:, kc, :], in_=tmph[:], func=AF.Identity,
                                                                 bias=shv[:, kc:kc + 1], scale=1.0),
                              reads=["tmph", "modt"], writes=["zh"])

            def mm_slab(bank, wv, wres, rhs_fn, rhs_res_fn, nk, first, last, ncols=T, k0=0):
                for k in range(nk):
                    S.add("pe", lambda e, k=k: e.matmul(ps[bank][:, 0:ncols], wv[:, k, :], rhs_fn(k0 + k),
                                                       start=(first and k == 0), stop=(last and k == nk - 1)),
                          reads=[wres, rhs_res_fn(k0 + k)], writes=[f"ps{bank}"])

            def ffn(l, gv):
                for j in range(FC):
                    ba, bb = nbank(), nbank()
                    wv, wres = W.next(16, 128)
                    mm_slab(ba, wv, wres, lambda k: zt[:, k, :], lambda k: f"z{k}", 16, True, True)
                    wv, wres = W.next(16, 128)
                    mm_slab(bb, wv, wres, lambda k: zt[:, k, :], lambda k: f"z{k}", 16, True, True)
                    si = j % 2
                    S.add("act", lambda e, ba=ba, si=si: e.activation(out=sg[si][:], in_=ps[ba][:, :], func=AF.Silu),
                          reads=[f"ps{ba}"], writes=[f"sg{si}"])
                    S.add("dve", lambda e, bb=bb, si=si, j=j: e.tensor_tensor(out=hid[:, j, :], in0=sg[si][:], in1=ps[bb][:, :], op=ALU.mult),
                          reads=[f"sg{si}", f"ps{bb}"], writes=[f"hid{j}"])
                for m in range(KC):
                    b = nbank()
                    for si_, (k0, nk) in enumerate(((0, 16), (16, 16), (32, 12))):
                        wv, wres = W.next(nk, 128)
                        mm_slab(b, wv, wres, lambda k: hid[:, k, :], lambda k: f"hid{k}", nk, si_ == 0, si_ == 2, k0=k0)
                    S.add("dve", lambda e, b=b, m=m: e.scalar_tensor_tensor(
                        out=xt[:, m, :], in0=ps[b][:, :], scalar=gv[:, m:m + 1], in1=xt[:, m, :], op0=ALU.mult, op1=ALU.add),
                        reads=[f"ps{b}", f"x{m}", "modt"], writes=[f"x{m}"])

            return rmsnorm_mod, mm_slab, ffn, nbank

        def run_block1(chunk, lite, first):
            xsrc, xhsrc, emsrc = xsrcs[chunk], xhsrcs[chunk], emsrcs[chunk]
            with ExitStack() as es1:
                new_block_sems()
                def sb1(name, shape, dt=F32):
                    return es1.enter_context(nc.sbuf_tensor(f"{name}_b{blkno[0]}", list(shape), dt))
                xt = sb1("xt", [128, KC, T])
                zt = sb1("zt", [128, KC, T], BF16)
                hid = sb1("hid", [128, FC, T], BF16)
                ostg = sb1("ostg", [128, 4, D], BF16)
                wf = [sb1(f"wf{i}", [128, 2048]) for i in range(WStream.NF)]
                wb = [sb1(f"wb{i}", [128, 2048], BF16) for i in range(WStream.NB)]
                xh = sb1("xh", [128, KC, 2])
                zh = sb1("zh", [128, KC, 2], BF16)
                rstd = sb1("rstd", [128, T])
                rstdh = sb1("rstdh", [128, 2])
                tmp = [sb1(f"tmp{i}", [128, T]) for i in range(2)]
                tmph = sb1("tmph", [128, 2])
                bgs = [sb1(f"bgs{i}", [128, T]) for i in range(2)]
                cgs = [sb1(f"cgs{i}", [128, T]) for i in range(2)]
                uext = [sb1(f"uext{i}", [128, T + 2]) for i in range(2)]
                cv = [sb1(f"cv{i}", [128, T]) for i in range(2)]
                hc = [sb1(f"hc{i}", [128, 4]) for i in range(2)]
                sg = [sb1(f"sg{i}", [128, T]) for i in range(2)]

                S = Sched()
                plan1 = make_plan1(lite)
                W = WStream(S, wf, wb, plan1)
                if first:
                    S.add("pool", lambda e: [e.dma_start(out=cs[:], in_=cvec[:, :]),
                                             e.dma_start(out=small[:], in_=smallv[:, :]),
                                             e.dma_start(out=cst[:], in_=consts[:, :]),
                                             e.dma_start(out=bgt[:], in_=bgate[:, :]),
                                             e.dma_start(out=gsl[:], in_=gsel[:, :]),
                                             e.dma_start(out=wgf[:], in_=mwin_r[:, :, 6144:6160])],
                          writes=["cs", "small", "cst", "bgt", "gsl", "wgf"], dma_sem="misc", ndma=6)
                    S.add("act", lambda e: e.activation(out=cs[:], in_=cs[:], func=AF.Silu), reads=["cs"], writes=["cs"])
                    S.add("dve", lambda e: e.tensor_copy(out=identb[:], in_=cst[:, 0:128]), reads=["cst"], writes=["identb"])
                    S.add("dve", lambda e: e.memset(onesb[:], 1.0), writes=["onesb"])
                    S.add("dve", lambda e: e.tensor_copy(out=wgb[:], in_=wgf[:]), reads=["wgf"], writes=["wgb"])

                    mod_jobs = []
                    mod_jobs.append((adaw_r[0], 24, 0, ada_b_v[:, 0:96]))
                    mod_jobs.append((adaw_r[1], 24, 96, ada_b_v[:, 96:192]))
                    mod_jobs.append((fadaw_r, 8, 192, fada_b_v))
                    piece_i = 0
                    for (w_r, ng, cbase, bview) in mod_jobs:
                        for g in range(ng):
                            for kq in range(4):
                                slot = piece_i % 4
                                piece_i += 1
                                src = w_r[:, kq * 4:(kq + 1) * 4, g * 512:(g + 1) * 512]
                                mwv = wf[slot][:, :].rearrange("p (k c) -> p k c", k=4)
                                S.add("sp", lambda e, mwv=mwv, src=src: e.dma_start(out=mwv, in_=src),
                                      writes=[f"wf{slot}"], dma_sem=f"wf{slot}")
                                for m in range(4):
                                    for kcl in range(4):
                                        kc = kq * 4 + kcl
                                        S.add("pe", lambda e, m=m, kcl=kcl, kc=kc, mwv=mwv, kq=kq: e.matmul(
                                            ps[m][:, 0:1], mwv[:, kcl, m * 128:(m + 1) * 128], cs[:, kc:kc + 1],
                                            start=(kq == 0 and kcl == 0), stop=(kq == 3 and kcl == 3)),
                                            reads=[f"wf{slot}", "cs"], writes=[f"ps{m}"])
                            for m in range(4):
                                col = cbase + g * 4 + m
                                bc = g * 4 + m
                                S.add("dve", lambda e, m=m, col=col, bc=bc, bview=bview: e.tensor_tensor(
                                    out=modt[:, col:col + 1], in0=ps[m][:, 0:1], in1=bview[:, bc:bc + 1], op=ALU.add),
                                    reads=[f"ps{m}", "small"], writes=["modt"])
                    def mk_a(idx, gview, scview):
                        S.add("dve", lambda e: e.scalar_tensor_tensor(out=avec[:, idx * 16:(idx + 1) * 16], in0=scview, scalar=1.0,
                                                                      in1=gview, op0=ALU.add, op1=ALU.mult),
                              reads=["modt", "small"], writes=["avec"])
                    mk_a(0, norm_g_v[:, 0:16], modt[:, 16:32])
                    mk_a(1, norm_g_v[:, 16:32], modt[:, 64:80])
                    mk_a(2, norm_g_v[:, 32:48], modt[:, 96 + 16:96 + 32])
                    mk_a(3, norm_g_v[:, 48:64], modt[:, 96 + 64:96 + 80])
                    mk_a(4, final_g_v, modt[:, 192 + 16:192 + 32])

                S.add("pool", lambda e: e.dma_start(out=emk[:], in_=emsrc), writes=["emk"], dma_sem="ld_m")
                rmsnorm_mod, mm_slab, ffn, nbank = make_helpers(S, W, xt, zt, hid, rstd, rstdh, tmp, tmph, sg, xh, zh)

                for i in range(ntiles):
                    c0 = i * T
                    S.add("pool", lambda e, c0=c0: e.dma_start(out=xt[:], in_=xsrc[:, :, c0:c0 + T]),
                          writes=[f"x{k}" for k in range(KC)], dma_sem="x")
                    S.add("pool", lambda e, i=i: e.dma_start(out=xh[:].rearrange("p k t -> p (k t)"),
                                                              in_=xhsrc[:, i * 32:(i + 1) * 32]),
                          writes=["xh"], dma_sem="xh")
                    rmsnorm_mod(avec[:, 0:16], modt[:, 0:16], with_halo=True)
                    for m in range(KC):
                        ba, bb, bc_ = nbank(), nbank(), nbank()
                        wv, wres = W.next(16, 128)
                        mm_slab(ba, wv, wres, lambda k: zt[:, k, :], lambda k: f"z{k}", 16, True, True)
                        wv, wres = W.next(16, 128)
                        mm_slab(bb, wv, wres, lambda k: zt[:, k, :], lambda k: f"z{k}", 16, True, True)
                        for k in range(KC):
                            S.add("pe", lambda e, k=k, wv=wv: e.matmul(ps[7][:, 0:2], wv[:, k, :], zh[:, k, :], start=(k == 0), stop=(k == KC - 1)),
                                  reads=[wres, "zh"], writes=["ps7"])
                        wv, wres = W.next(16, 128)
                        mm_slab(bc_, wv, wres, lambda k: zt[:, k, :], lambda k: f"z{k}", 16, True, True)
                        for k in range(KC):
                            S.add("pe", lambda e, k=k, wv=wv: e.matmul(ps[7][:, 2:4], wv[:, k, :], zh[:, k, :], start=(k == 0), stop=(k == KC - 1)),
                                  reads=[wres, "zh"], writes=["ps7"])
                        r2 = m % 2
                        S.add("act", lambda e, ba=ba, r2=r2: e.copy(out=bgs[r2][:], in_=ps[ba][:, :]), reads=[f"ps{ba}"], writes=[f"bgs{r2}"])
                        S.add("act", lambda e, bb=bb, r2=r2: e.copy(out=cgs[r2][:], in_=ps[bb][:, :]), reads=[f"ps{bb}"], writes=[f"cgs{r2}"])
                        S.add("act", lambda e, r2=r2: e.copy(out=hc[r2][:], in_=ps[7][:, 0:4]), reads=["ps7"], writes=[f"hc{r2}"])
                        S.add("dve", lambda e, bc_=bc_, r2=r2: e.tensor_tensor(out=uext[r2][:, 1:T + 1], in0=cgs[r2][:], in1=ps[bc_][:, :], op=ALU.mult),
                              reads=[f"cgs{r2}", f"ps{bc_}"], writes=[f"uext{r2}"])
                        S.add("dve", lambda e, r2=r2, i=i: e.scalar_tensor_tensor(
                            out=uext[r2][:, 0:1], in0=hc[r2][:, 0:1], scalar=emk[:, 2 * i:2 * i + 1], in1=hc[r2][:, 2:3], op0=ALU.mult, op1=ALU.mult),
                            reads=[f"hc{r2}", "emk", f"uext{r2}"], writes=[f"uext{r2}"])
                        S.add("dve", lambda e, r2=r2, i=i: e.scalar_tensor_tensor(
                            out=uext[r2][:, T + 1:T + 2], in0=hc[r2][:, 1:2], scalar=emk[:, 2 * i + 1:2 * i + 2], in1=hc[r2][:, 3:4], op0=ALU.mult, op1=ALU.mult),
                            reads=[f"hc{r2}", "emk", f"uext{r2}"], writes=[f"uext{r2}"])
                        S.add("dve", lambda e, r2=r2, m=m: e.tensor_scalar(out=cv[r2][:], in0=uext[r2][:, 0:T], scalar1=conv_w_v[:, m:m + 1], scalar2=None, op0=ALU.mult),
                              reads=[f"uext{r2}", "small"], writes=[f"cv{r2}"])
                        S.add("dve", lambda e, r2=r2, m=m: e.scalar_tensor_tensor(out=cv[r2][:], in0=uext[r2][:, 1:T + 1], scalar=conv_w_v[:, 16 + m:17 + m],
                                                                                  in1=cv[r2][:], op0=ALU.mult, op1=ALU.add),
                              reads=[f"uext{r2}", "small", f"cv{r2}"], writes=[f"cv{r2}"])
                        S.add("dve", lambda e, r2=r2, m=m: e.scalar_tensor_tensor(out=cv[r2][:], in0=uext[r2][:, 2:T + 2], scalar=conv_w_v[:, 32 + m:33 + m],
                                                                                  in1=cv[r2][:], op0=ALU.mult, op1=ALU.add),
                              reads=[f"uext{r2}", "small", f"cv{r2}"], writes=[f"cv{r2}"])
                        S.add("dve", lambda e, r2=r2, m=m: e.tensor_tensor(out=hid[:, m, :], in0=cv[r2][:], in1=bgs[r2][:], op=ALU.mult),
                              reads=[f"cv{r2}", f"bgs{r2}"], writes=[f"hid{m}"])
                    for m in range(KC):
                        b = nbank()
                        wv, wres = W.next(16, 128)
                        mm_slab(b, wv, wres, lambda k: hid[:, k, :], lambda k: f"hid{k}", 16, True, True)
                        S.add("dve", lambda e, b=b, m=m: e.scalar_tensor_tensor(
                            out=xt[:, m, :], in0=ps[b][:, :], scalar=modt[:, 32 + m:33 + m], in1=xt[:, m, :], op0=ALU.mult, op1=ALU.add),
                            reads=[f"ps{b}", f"x{m}", "modt"], writes=[f"x{m}"])
                    rmsnorm_mod(avec[:, 16:32], modt[:, 48:64])
                    ffn(0, modt[:, 80:96])
                    if not lite:
                      S.add("pool", lambda e, c0=c0: e.dma_start(out=x1s_r[:, :, c0:c0 + T], in_=xt[:]),
                          reads=[f"x{k}" for k in range(KC)], writes=["d_x1s"], dma_sem="st_x1")
                    rmsnorm_mod(avec[:, 32:48], modt[:, 96:112])
                    for c in range(8 if lite else 0, 16):
                        b = nbank()
                        wv, wres = W.next(16, 128)
                        mm_slab(b, wv, wres, lambda k: zt[:, k, :], lambda k: f"z{k}", 16, True, True)
                        if c < 8:
                            S.add("act", lambda e, b=b, c=c: e.activation(out=hid[:, c, :], in_=ps[b][:, :], func=AF.Copy, scale=1.0 / 16.0),
                                  reads=[f"ps{b}"], writes=[f"hid{c}"])
                        else:
                            S.add("dve", lambda e, b=b, c=c: e.tensor_copy(out=hid[:, c, :], in_=ps[b][:, :]),
                                  reads=[f"ps{b}"], writes=[f"hid{c}"])
                    if not lite:
                      S.add("pool", lambda e, c0=c0: e.dma_start(out=qs[:, :, c0:c0 + T], in_=hid[:, 0:8, :]),
                          reads=[f"hid{c}" for c in range(8)], writes=["d_qs"], dma_sem="st_qk")
                    if not lite:
                      S.add("pool", lambda e, c0=c0: e.dma_start(out=ks[:, :, c0:c0 + T], in_=hid[:, 8:16, :]),
                          reads=[f"hid{c}" for c in range(8, 16)], writes=["d_ks"], dma_sem="st_k")
                    ktv = hid[:, 16:24, :].rearrange("p (a b) c -> p a (b c)", b=2)
                    ps7b = ps[7][:].bitcast(BF16)
                    for tb in range(4):
                        for c in range(8):
                            S.add("pe", lambda e, c=c, tb=tb: e.transpose(ps7b[:, c * 128:(c + 1) * 128], hid[:, 8 + c, tb * 128:(tb + 1) * 128], identb[:]),
                                  reads=[f"hid{8 + c}", "identb"], writes=["ps7"])
                        S.add("dve", lambda e, tb=tb: e.tensor_copy(out=ktv[:, tb, :], in_=ps7b[:, :]),
                              reads=["ps7"], writes=[f"hid{16 + 2 * tb}", f"hid{17 + 2 * tb}"])
                    S.add("pool", lambda e, i=i: e.dma_start(out=kts.ap()[4 * i:4 * i + 4].rearrange("b p c -> p b c"), in_=ktv),
                          reads=[f"hid{c}" for c in range(16, 24)], writes=["d_kts"], dma_sem="st_kt")
                    vtv = hid[:, 24:40, :].rearrange("p (a b) c -> p a (b c)", b=4)
                    for g in range(4 if lite else 8):
                        for q4 in range(4):
                            wv, wres = W.next(4, 512)
                            for kcl in range(4):
                                kc = q4 * 4 + kcl
                                for tb in range(4):
                                    S.add("pe", lambda e, kc=kc, kcl=kcl, tb=tb, wv=wv, q4=q4: e.matmul(
                                        ps[tb][:, :], zt[:, kc, tb * 128:(tb + 1) * 128], wv[:, kcl, :],
                                        start=(q4 == 0 and kcl == 0), stop=(q4 == 3 and kcl == 3)),
                                        reads=[wres, f"z{kc}"], writes=[f"ps{tb}"])
                        for tb in range(4):
                            if g < 4:
                                S.add("dve" if tb % 2 else "act",
                                      (lambda e, tb=tb, g=g: e.tensor_copy(out=vtv[:, tb, g * 512:(g + 1) * 512], in_=ps[tb][:, :])) if tb % 2 else
                                      (lambda e, tb=tb, g=g: e.copy(out=vtv[:, tb, g * 512:(g + 1) * 512], in_=ps[tb][:, :])),
                                      reads=[f"ps{tb}"], writes=[f"hid{24 + 4 * tb + k}" for k in range(4)])
                            else:
                                go = g - 4
                                S.add("act", lambda e, tb=tb, go=go: e.activation(out=ostg[:, tb, go * 512:(go + 1) * 512], in_=ps[tb][:, :], func=AF.Sigmoid),
                                      reads=[f"ps{tb}"], writes=["ostg"])
                        if g == 3:
                            S.add("pool", lambda e, i=i: e.dma_start(out=vts.ap()[4 * i:4 * i + 4].rearrange("b p c -> p b c"), in_=vtv),
                                  reads=[f"hid{c}" for c in range(24, 40)], writes=["d_vts"], dma_sem="st_v")
                    if not lite:
                      S.add("pool", lambda e, i=i: e.dma_start(out=ots.ap()[4 * i:4 * i + 4].rearrange("b p c -> p b c"), in_=ostg[:]),
                          reads=["ostg"], writes=["d_ots"], dma_sem="st_o")
                    for k in range(KC):
                        S.add("pe", lambda e, k=k: e.matmul(ps[6][0:16, :], wgb[:, k, :], zt[:, k, :], start=(k == 0), stop=(k == KC - 1)),
                              reads=["wgb", f"z{k}"], writes=["ps6"])
                    S.add("act", lambda e, c0=c0: e.activation(out=graw[:, c0:c0 + T], in_=ps[6][0:16, :], func=AF.Identity, bias=bgt[:, 0:1], scale=1.0),
                          reads=["ps6", "bgt"], writes=["graw"])

                if dbg:
                    S.add("pool", lambda e: e.dma_start(out=gts[:, :], in_=graw[:]), reads=["graw"], writes=["d_gts"], dma_sem="st_g")
                assert W.used == len(plan1), (W.used, len(plan1))
                S.finalize()
                with nc.Block() as blk:
                    S.emit(nc, blk, sems, base)
        nblk = ntiles * 4

        def run_block2(mode, ex):
            with ExitStack() as es2:
                blkno[0] += 1
                def sb2(name, shape, dt=F32):
                    return es2.enter_context(nc.sbuf_tensor(f"{name}_b{blkno[0]}", list(shape), dt))
                C = sb2("C", [128, 8, 512])
                Cb = sb2("Cb", [128, 8, 512], BF16)
                nst = sb2("nst", [128, 8])
                nb = sb2("nb", [128, 8], BF16)
                qb = [sb2(f"qb{i}", [128, 8, 128], BF16) for i in range(2)]
                kb = [sb2(f"kb{i}", [128, 8, 128], BF16) for i in range(2)]
                ktb = [sb2(f"ktb{i}", [128, 1024], BF16) for i in range(2)]
                vtb = [sb2(f"vtb{i}", [128, D], BF16) for i in range(2)]
                sob = [sb2(f"sob{i}", [128, D], BF16) for i in range(2)]
                hbb = [sb2(f"hbb{i}", [128, D]) for i in range(2)]
                glf = sb2("glf", [16, NTOK])
                gtok = sb2("gtok", [128, NBLK, 16])
                cumf = sb2("cumf", [128, NBLK, 16])
                cumb = sb2("cumb", [128, NBLK, 16])
                tot = sb2("tot", [128, NBLK, 16])
                dec = sb2("dec", [128, NBLK, 16])
                biasD = sb2("biasD", [128, NBLK, 8])
                wk = sb2("wk", [128, NBLK, 8])
                gsum = sb2("gsum", [128, 16])
                gexp = sb2("gexp", [128, 16])
                coef = sb2("coef", [128, 4])
                onesf = sb2("onesf", [128, 128])
                Rt = [sb2(f"Rt{i}", [128, 128]) for i in range(2)]
                EBt = [sb2(f"EBt{i}", [128, 128]) for i in range(2)]
                DTt = [sb2(f"DTt{i}", [128, 128]) for i in range(2)]
                ptl = [sb2(f"pt{i}", [128, 128], BF16) for i in range(2)]
                qtil = [sb2(f"qtil{i}", [128, 2, 128], BF16) for i in range(2)]
                ktil = [sb2(f"ktil{i}", [128, 256], BF16) for i in range(2)]
                dab = [sb2(f"dab{i}", [128, 1]) for i in range(2)]
                rden = [sb2(f"rden{i}", [128, 1]) for i in range(2)]
                ssq = sb2("ssq", [128, 4])
                rn = sb2("rn", [128, 4])
                tmpn = [sb2(f"tmpn{i}", [128, 512]) for i in range(2)]
                junk = sb2("junk", [128, 512])
                gtb = sb2("gtb", [128, D], BF16)
                gTst = [sb2(f"gTst{i}", [128, KC, 128], BF16) for i in range(2)]
                mnt = sb2("mnt", [128, D])
                sstg = [sb2(f"sstg{i}", [128, SW]) for i in range(2)]
                prd = sb2("prd", [128, 16])

                S2 = Sched()
                A2 = S2.add
                A2("pool", lambda e: [e.dma_start(out=prd[:], in_=pred[:, :]), e.dma_start(out=mnt[:], in_=mnorm[:, :])],
                   writes=["prd", "mnt"], dma_sem="ld_m", ndma=2)
                A2("dve", lambda e: e.memset(onesf[:], 1.0), writes=["onesf"])
                identf = cst[:, 0:128]

                A2("act", lambda e: e.activation(out=graw[:], in_=graw[:], func=AF.Tanh, scale=1.0 / 15.0), reads=["graw"], writes=["graw"])
                A2("dve", lambda e: e.tensor_scalar(out=graw[:], in0=graw[:], scalar1=15.0, scalar2=None, op0=ALU.mult), reads=["graw"], writes=["graw"])
                A2("act", lambda e: e.activation(out=glf[:], in_=graw[:], func=AF.Sigmoid), reads=["graw"], writes=["glf"])
                A2("act", lambda e: e.activation(out=glf[:], in_=glf[:], func=AF.Ln), reads=["glf"], writes=["glf"])
                A2("dve", lambda e: e.tensor_scalar(out=glf[:], in0=glf[:], scalar1=gsl[:, 1:2], scalar2=None, op0=ALU.mult), reads=["glf", "gsl"], writes=["glf"])
                A2("dve", lambda e: e.scalar_tensor_tensor(out=graw[:], in0=graw[:], scalar=gsl[:, 0:1], in1=glf[:], op0=ALU.mult, op1=ALU.add),
                   reads=["graw", "glf", "gsl"], writes=["graw"])
                if dbg:
                    A2("pool", lambda e: e.dma_start(out=gts[:, :], in_=graw[:]), reads=["graw"], writes=["d_gts"], dma_sem="st_g")
                for b_ in range(nblk):
                    A2("pe", lambda e, b_=b_: e.matmul(ps[0][:, b_ * 16:(b_ + 1) * 16], graw[0:16, b_ * 128:(b_ + 1) * 128], cst[0:16, 0:16], start=True, stop=True),
                       reads=["graw", "cst"], writes=["ps0"])
                gtok2 = gtok[:].rearrange("p b j -> p (b j)")
                A2("dve", lambda e: e.tensor_copy(out=gtok2[:, 0:nblk * 16], in_=ps[0][:, 0:nblk * 16]), reads=["ps0"], writes=["gtok"])
                for (bank, lhs, dst, lres, dname) in ((1, cst[:, 128:256], cumf, "cst", "cumf"), (2, cst[:, 384:512], cumb, "cst", "cumb"), (3, onesf[:], tot, "onesf", "tot")):
                    for b_ in range(nblk):
                        A2("pe", lambda e, b_=b_, bank=bank, lhs=lhs: e.matmul(ps[bank][:, b_ * 16:(b_ + 1) * 16], lhs, gtok[:, b_, :], start=True, stop=True),
                           reads=["gtok", lres], writes=[f"ps{bank}"])
                    d2 = dst[:].rearrange("p b j -> p (b j)")
                    A2("dve", lambda e, d2=d2, bank=bank: e.tensor_copy(out=d2[:, 0:nblk * 16], in_=ps[bank][:, 0:nblk * 16]), reads=[f"ps{bank}"], writes=[dname])
                A2("dve", lambda e: e.tensor_tensor(out=biasD[:, 0:nblk, 0:4], in0=gtok[:, 0:nblk, 0:4], in1=cumf[:, 0:nblk, 4:8], op=ALU.subtract),
                   reads=["gtok", "cumf", "cumb", "tot"], writes=["biasD"])
                A2("dve", lambda e: e.tensor_tensor(out=biasD[:, 0:nblk, 4:8], in0=gtok[:, 0:nblk, 8:12], in1=cumb[:, 0:nblk, 12:16], op=ALU.subtract),
                   reads=["gtok", "cumb", "biasD"], writes=["biasD"])
                A2("dve", lambda e: e.tensor_tensor(out=wk[:, 0:nblk, 0:4], in0=tot[:, 0:nblk, 4:8], in1=biasD[:, 0:nblk, 0:4], op=ALU.add),
                   reads=["biasD", "tot"], writes=["wk"])
                A2("dve", lambda e: e.tensor_tensor(out=wk[:, 0:nblk, 4:8], in0=tot[:, 0:nblk, 12:16], in1=biasD[:, 0:nblk, 4:8], op=ALU.add),
                   reads=["biasD", "tot", "wk"], writes=["wk"])
                A2("act", lambda e: e.activation(out=wk[:, 0:nblk, :], in_=wk[:, 0:nblk, :], func=AF.Exp), reads=["wk"], writes=["wk"])
                A2("act", lambda e: e.activation(out=dec[:, 0:nblk, :], in_=tot[:, 0:nblk, :], func=AF.Exp), reads=["tot"], writes=["dec"])
                A2("dve", lambda e: e.tensor_reduce(out=gsum[:], in_=tot[:, 0:nblk, :].rearrange("p b j -> p j b"), axis=mybir.AxisListType.X, op=ALU.add),
                   reads=["tot"], writes=["gsum"])
                A2("act", lambda e: e.activation(out=gexp[:], in_=gsum[:], func=AF.Exp), reads=["gsum"], writes=["gexp"])

                C2 = C[:].rearrange("p a b -> p (a b)")
                Cb2 = Cb[:].rearrange("p a b -> p (a b)")
                Cres = [f"C{h}" for h in range(4)]
                Cbres = [f"Cb{h}" for h in range(4)]

                def zero_state():
                    A2("dve", lambda e: e.memset(C2, 0.0), writes=Cres)
                    A2("dve", lambda e: e.memset(nst[:], 0.0), writes=["nst"])

                def cast_state():
                    A2("act", lambda e: e.copy(out=Cb2, in_=C2), reads=Cres, writes=Cbres)
                    A2("dve", lambda e: e.tensor_copy(out=nb[:], in_=nst[:]), reads=["nst"], writes=["nb"])

                dsem_qk = ["p2a", "p2b"]
                dsem_ktv = ["p2c", "p2d"]
                dsem_soh = ["p2e", "p2f"]

                def load_blk(pos, blk, outputs, dirn):
                    p = pos % 2
                    c0 = blk * 128
                    A2("sp", lambda e: [e.dma_start(out=ktb[p][:], in_=kts.ap()[blk]), e.dma_start(out=vtb[p][:], in_=vts.ap()[blk])],
                       reads=["d_kts", "d_vts"], writes=[f"ktb{p}", f"vtb{p}"], dma_sem=dsem_ktv[p], ndma=2)
                    if outputs:
                        A2("sp", lambda e: [e.dma_start(out=qb[p][:], in_=qs[:, :, c0:c0 + 128]), e.dma_start(out=kb[p][:], in_=ks[:, :, c0:c0 + 128])],
                           reads=["d_qs", "d_ks"], writes=[f"qb{p}", f"kb{p}"], dma_sem=dsem_qk[p], ndma=2)
                        if dirn == 0:
                            A2("sp", lambda e: [e.dma_start(out=sob[p][:], in_=ots.ap()[blk]), e.dma_start(out=hbb[p][:], in_=hbs.ap()[blk])],
                               reads=["d_ots", f"d_hbs{blk}"], writes=[f"sob{p}", f"hbb{p}"], dma_sem=dsem_soh[p], ndma=2)

                def scan_block(pos, blk, dirn, outputs):
                    p = pos % 2
                    U_ = cst[:, 128:256] if dirn == 0 else cst[:, 384:512]
                    M_ = cst[:, 256:384] if dirn == 0 else cst[:, 512:640]
                    for pair in ((0, 1), (2, 3)):
                        def st0(h):
                            par = h % 2
                            bE = 4 * par
                            gl = 8 * dirn + 4 + h
                            bi = 4 * dirn + h
                            if outputs:
                                A2("dve", lambda e: e.tensor_scalar(out=Rt[par][:], in0=U_, scalar1=gtok[:, blk, gl:gl + 1], scalar2=None, op0=ALU.mult),
                                   reads=["cst", "gtok"], writes=[f"Rt{par}"])
                                A2("pe", lambda e: e.matmul(ps[bE][:, 0:128], onesf[:], Rt[par][:], start=True, stop=True),
                                   reads=["onesf", f"Rt{par}"], writes=[f"ps{bE}"])
                                A2("pe", lambda e: e.matmul(ps[bE][:, 128:256], onesf[:], Rt[par][:], start=True, stop=False),
                                   reads=["onesf", f"Rt{par}"], writes=[f"ps{bE}"])
                                A2("pe", lambda e: e.matmul(ps[bE][:, 128:256], identf, M_, start=False, stop=True),
                                   reads=["cst"], writes=[f"ps{bE}"])
                                for half in range(2):
                                    A2("pe", lambda e, half=half: e.matmul(ps[bE][:, 256:384], kb[p][:, 2 * h + half, :], qb[p][:, 2 * h + half, :],
                                                                          start=(half == 0), stop=(half == 1)),
                                       reads=[f"kb{p}", f"qb{p}"], writes=[f"ps{bE}"])
                            A2("dve", lambda e: e.tensor_scalar(out=ktil[par][:], in0=ktb[p][:, h * 256:(h + 1) * 256], scalar1=wk[:, blk, bi:bi + 1], scalar2=None, op0=ALU.mult),
                               reads=[f"ktb{p}", "wk"], writes=[f"ktil{par}"])

                        def st1(h):
                            par = h % 2
                            bE = 4 * par
                            bi = 4 * dirn + h
                            if outputs and _OLEV >= 2:
                                A2("act", lambda e: e.activation(out=EBt[par][:], in_=ps[bE][:, 0:128], func=AF.Exp), reads=[f"ps{bE}"], writes=[f"EBt{par}"])
                                if _SUB >= 2:
                                  A2("act", lambda e: e.activation(out=DTt[par][:], in_=ps[bE][:, 128:256], func=AF.Exp, bias=biasD[:, blk, bi:bi + 1], scale=1.0),
                                   reads=[f"ps{bE}", "biasD"], writes=[f"DTt{par}"])
                                if _SUB >= 3:
                                  A2("dve", lambda e: e.tensor_tensor(out=ptl[par][:], in0=ps[bE][:, 256:384], in1=DTt[par][:], op=ALU.mult),
                                   reads=[f"ps{bE}", f"DTt{par}"], writes=[f"pt{par}"])
                                for half in range(2 if _SUB >= 4 else 0):
                                    A2("dve", lambda e, half=half: e.tensor_tensor(out=qtil[par][:, half, :], in0=qb[p][:, 2 * h + half, :], in1=EBt[par][:], op=ALU.mult),
                                       reads=[f"qb{p}", f"EBt{par}"], writes=[f"qtil{par}"])

                        def st2(h):
                            par = h % 2
                            bE, bN, bC0, bC1 = 4 * par, 4 * par + 1, 4 * par + 2, 4 * par + 3
                            vh = vtb[p][:, h * 512:(h + 1) * 512]
                            if outputs and _OLEV >= 3:
                                A2("pe", lambda e: e.matmul(ps[bN][:, :], ptl[par][:], vh, start=True, stop=False),
                                   reads=[f"pt{par}", f"vtb{p}"], writes=[f"ps{bN}"])
                                for half in range(2):
                                    A2("pe", lambda e, half=half: e.matmul(ps[bN][:, :], qtil[par][:, half, :], Cb[:, 2 * h + half, :], start=False, stop=(half == 1)),
                                       reads=[f"qtil{par}", f"Cb{h}"], writes=[f"ps{bN}"])
                                A2("pe", lambda e: e.matmul(ps[bE][:, 384:385], ptl[par][:], onesb[:, 0:1], start=True, stop=False),
                                   reads=[f"pt{par}", "onesb"], writes=[f"ps{bE}"])
                                for half in range(2):
                                    A2("pe", lambda e, half=half: e.matmul(ps[bE][:, 384:385], qtil[par][:, half, :], nb[:, 2 * h + half:2 * h + half + 1], start=False, stop=(half == 1)),
                                       reads=[f"qtil{par}", "nb"], writes=[f"ps{bE}"])
                            for half, bC in ((0, bC0), (1, bC1)):
                                A2("pe", lambda e, half=half, bC=bC: e.matmul(ps[bC][:, :], ktil[par][:, half * 128:(half + 1) * 128], vh, start=True, stop=True),
                                   reads=[f"ktil{par}", f"vtb{p}"], writes=[f"ps{bC}"])
                                A2("pe", lambda e, half=half: e.matmul(ps[bE][:, 386 + half:387 + half], ktil[par][:, half * 128:(half + 1) * 128], onesb[:, 0:1], start=True, stop=True),
                                   reads=[f"ktil{par}", "onesb"], writes=[f"ps{bE}"])

                        def st3(h):
                            par = h % 2
                            bE, bN, bC0, bC1 = 4 * par, 4 * par + 1, 4 * par + 2, 4 * par + 3
                            gl = 8 * dirn + 4 + h
                            hc_ = slice(h * 512, (h + 1) * 512)
                            if outputs and _OLEV >= 4:
                                A2("act", lambda e: e.activation(out=dab[par][:], in_=ps[bE][:, 384:385], func=AF.Abs), reads=[f"ps{bE}"], writes=[f"dab{par}"])
                                A2("dve", lambda e: e.tensor_scalar_max(out=dab[par][:], in0=dab[par][:], scalar1=1.0), reads=[f"dab{par}"], writes=[f"dab{par}"])
                                A2("dve", lambda e: e.reciprocal(out=rden[par][:], in_=dab[par][:]), reads=[f"dab{par}"], writes=[f"rden{par}"])
                                if dirn == 1:
                                    A2("act", lambda e: e.activation(out=hbb[p][:, hc_], in_=ps[bN][:, :], func=AF.Copy, scale=rden[par][:, 0:1]),
                                       reads=[f"ps{bN}", f"rden{par}"], writes=[f"hbb{p}"])
                                else:
                                    A2("dve", lambda e: e.scalar_tensor_tensor(out=hbb[p][:, hc_], in0=ps[bN][:, :], scalar=rden[par][:, 0:1], in1=hbb[p][:, hc_],
                                                                              op0=ALU.mult, op1=ALU.add),
                                       reads=[f"ps{bN}", f"rden{par}", f"hbb{p}"], writes=[f"hbb{p}"])
                            for half, bC in ((0, bC0), (1, bC1)):
                                A2("dve", lambda e, half=half, bC=bC: e.scalar_tensor_tensor(out=C[:, 2 * h + half, :], in0=C[:, 2 * h + half, :], scalar=dec[:, blk, gl:gl + 1],
                                                                                          in1=ps[bC][:, :], op0=ALU.mult, op1=ALU.add),
                                   reads=[f"C{h}", "dec", f"ps{bC}"], writes=[f"C{h}"])
                            A2("dve", lambda e: e.scalar_tensor_tensor(out=nst[:, 2 * h:2 * h + 2], in0=nst[:, 2 * h:2 * h + 2], scalar=dec[:, blk, gl:gl + 1],
                                                                      in1=ps[bE][:, 386:388], op0=ALU.mult, op1=ALU.add),
                               reads=["nst", "dec", f"ps{bE}"], writes=["nst"])
                            A2("act", lambda e: e.copy(out=Cb[:, 2 * h, :], in_=C[:, 2 * h, :]), reads=[f"C{h}"], writes=[f"Cb{h}"])
                            A2("pool", lambda e: e.tensor_copy(out=Cb[:, 2 * h + 1, :], in_=C[:, 2 * h + 1, :]), reads=[f"C{h}"], writes=[f"Cb{h}"])
                            A2("pool", lambda e: e.tensor_copy(out=nb[:, 2 * h:2 * h + 2], in_=nst[:, 2 * h:2 * h + 2]), reads=["nst"], writes=["nb"])

                        for st in (st0, st1, st2, st3):
                            for h in pair:
                                st(h)

                def sweep(dirn, outputs, post=None):
                    order = list(range(nblk)) if dirn == 0 else list(range(nblk - 1, -1, -1))
                    load_blk(0, order[0], outputs, dirn)
                    for pos, blk in enumerate(order):
                        if pos + 1 < len(order):
                            load_blk(pos + 1, order[pos + 1], outputs, dirn)
                        scan_block(pos, blk, dirn, outputs)
                        if post is not None:
                            post(pos, blk)

                def store_state(dirn):
                    o = dirn * SW
                    A2("pool", lambda e: [e.dma_start(out=sall[ex * 128:(ex + 1) * 128, o:o + 4096], in_=C2),
                                          e.dma_start(out=sall[ex * 128:(ex + 1) * 128, o + 4096:o + 4104], in_=nst[:]),
                                          e.dma_start(out=sall[ex * 128:(ex + 1) * 128, o + 4104:o + 4112], in_=gexp[:, dirn * 8:dirn * 8 + 8])],
                       reads=Cres + ["nst", "gexp"], writes=["d_sloc"], dma_sem="st_S", ndma=3)

                def combine(dirn):
                    zero_state()
                    order = list(range(nextra)) if dirn == 0 else list(range(nextra - 1, -1, -1))
                    o = dirn * SW
                    for n_, i in enumerate(order):
                        pp = n_ % 2
                        A2("sp", lambda e, i=i, pp=pp: e.dma_start(out=sstg[pp][:], in_=sall[i * 128:(i + 1) * 128, o:o + SW]),
                           reads=["d_sall"], writes=[f"sstg{pp}"], dma_sem=f"ld_S{pp}")
                        a_ = prd[:, dirn * 8 + i:dirn * 8 + i + 1]
                        A2("dve", lambda e, pp=pp, a_=a_: e.tensor_scalar(out=coef[:], in0=sstg[pp][:, 4108:4112], scalar1=-1.0, scalar2=a_, op0=ALU.add, op1=ALU.mult),
                           reads=[f"sstg{pp}", "prd"], writes=["coef"])
                        A2("dve", lambda e: e.tensor_scalar_add(out=coef[:], in0=coef[:], scalar1=1.0), reads=["coef"], writes=["coef"])
                        for hh in range(8):
                            h = hh // 2
                            A2("dve", lambda e, hh=hh, h=h: e.tensor_scalar(out=C[:, hh, :], in0=C[:, hh, :], scalar1=coef[:, h:h + 1], scalar2=None, op0=ALU.mult),
                               reads=[f"C{h}", "coef"], writes=[f"C{h}"])
                            A2("dve", lambda e, hh=hh, h=h, pp=pp, a_=a_: e.scalar_tensor_tensor(out=C[:, hh, :], in0=sstg[pp][:, hh * 512:(hh + 1) * 512], scalar=a_, in1=C[:, hh, :],
                                                                                             op0=ALU.mult, op1=ALU.add),
                               reads=[f"C{h}", f"sstg{pp}", "prd"], writes=[f"C{h}"])
                        for h in range(4):
                            A2("dve", lambda e, h=h: e.tensor_scalar(out=nst[:, 2 * h:2 * h + 2], in0=nst[:, 2 * h:2 * h + 2], scalar1=coef[:, h:h + 1], scalar2=None, op0=ALU.mult),
                               reads=["nst", "coef"], writes=["nst"])
                        A2("dve", lambda e, pp=pp, a_=a_: e.scalar_tensor_tensor(out=nst[:], in0=sstg[pp][:, 4096:4104], scalar=a_, in1=nst[:], op0=ALU.mult, op1=ALU.add),
                           reads=["nst", f"sstg{pp}", "prd"], writes=["nst"])
                    cast_state()

                def local_states_pass():
                    sufpre = sb2("sufpre", [128, NBLK, 8])
                    wkL = sb2("wkL", [128, NBLK, 8])
                    ktl = [sb2(f"ktl{i}", [128, 256], BF16) for i in range(4)]
                    A2("dve", lambda e: e.memset(sufpre[:].rearrange("p b j -> p (b j)"), 0.0), writes=["sufpre"])
                    for b_ in range(nblk - 2, -1, -1):
                        A2("dve", lambda e, b_=b_: e.tensor_tensor(out=sufpre[:, b_, 0:4], in0=sufpre[:, b_ + 1, 0:4], in1=tot[:, b_ + 1, 4:8], op=ALU.add),
                           reads=["sufpre", "tot"], writes=["sufpre"])
                    for b_ in range(1, nblk):
                        A2("dve", lambda e, b_=b_: e.tensor_tensor(out=sufpre[:, b_, 4:8], in0=sufpre[:, b_ - 1, 4:8], in1=tot[:, b_ - 1, 12:16], op=ALU.add),
                           reads=["sufpre", "tot"], writes=["sufpre"])
                    A2("act", lambda e: e.activation(out=sufpre[:, 0:nblk, :], in_=sufpre[:, 0:nblk, :], func=AF.Exp), reads=["sufpre"], writes=["sufpre"])
                    A2("dve", lambda e: e.tensor_tensor(out=wkL[:, 0:nblk, :], in0=wk[:, 0:nblk, :], in1=sufpre[:, 0:nblk, :], op=ALU.mult),
                       reads=["wk", "sufpre"], writes=["wkL"])
                    pairs = [(h, d) for d in range(2) for h in range(4)]
                    groups = [pairs[0:3], pairs[3:6], pairs[6:8]]
                    kt_i = 0
                    pos = 0
                    for grp in groups:
                        load_blk(pos, 0, False, 0)
                        for blk in range(nblk):
                            if blk + 1 < nblk:
                                load_blk(pos + 1, blk + 1, False, 0)
                            p = pos % 2
                            for j, (h, d) in enumerate(grp):
                                kb_ = kt_i % 4
                                kt_i += 1
                                A2("dve", lambda e, kb_=kb_, h=h, d=d, blk=blk, p=p: e.tensor_scalar(
                                    out=ktl[kb_][:], in0=ktb[p][:, h * 256:(h + 1) * 256], scalar1=wkL[:, blk, d * 4 + h:d * 4 + h + 1], scalar2=None, op0=ALU.mult),
                                   reads=[f"ktb{p}", "wkL"], writes=[f"ktl{kb_}"])
                                for half in range(2):
                                    A2("pe", lambda e, kb_=kb_, h=h, j=j, half=half, blk=blk, p=p: e.matmul(
                                        ps[2 * j + half][:, :], ktl[kb_][:, half * 128:(half + 1) * 128], vtb[p][:, h * 512:(h + 1) * 512],
                                        start=(blk == 0), stop=(blk == nblk - 1)),
                                       reads=[f"ktl{kb_}", f"vtb{p}"], writes=[f"ps{2 * j + half}"])
                                    nbk = 6 + blk % 2
                                    A2("pe", lambda e, kb_=kb_, j=j, half=half, nbk=nbk: e.matmul(
                                        ps[nbk][:, 2 * j + half:2 * j + half + 1], ktl[kb_][:, half * 128:(half + 1) * 128], onesb[:, 0:1],
                                        start=True, stop=True),
                                       reads=[f"ktl{kb_}", "onesb"], writes=[f"ps{nbk}"])
                            nbk = 6 + blk % 2
                            if blk == 0:
                                A2("dve", lambda e, nbk=nbk, ng=len(grp): e.tensor_copy(out=nst[:, 0:2 * ng], in_=ps[nbk][:, 0:2 * ng]),
                                   reads=[f"ps{nbk}"], writes=["nst"])
                            else:
                                A2("dve", lambda e, nbk=nbk, ng=len(grp): e.tensor_tensor(out=nst[:, 0:2 * ng], in0=nst[:, 0:2 * ng], in1=ps[nbk][:, 0:2 * ng], op=ALU.add),
                                   reads=[f"ps{nbk}", "nst"], writes=["nst"])
                            pos += 1
                        for j, (h, d) in enumerate(grp):
                            for half in range(2):
                                A2("act" if half == 0 else "dve",
                                   (lambda e, j=j, half=half: e.copy(out=C[:, 2 * j + half, :], in_=ps[2 * j + half][:, :])) if half == 0 else
                                   (lambda e, j=j, half=half: e.tensor_copy(out=C[:, 2 * j + half, :], in_=ps[2 * j + half][:, :])),
                                   reads=[f"ps{2 * j + half}"], writes=[f"C{j}"])

                        def st_fn(e, grp=grp):
                            r = []
                            for j, (h, d) in enumerate(grp):
                                o = d * SW
                                r.append(e.dma_start(out=sall[ex * 128:(ex + 1) * 128, o + 2 * h * 512:o + (2 * h + 2) * 512],
                                                     in_=C[:, 2 * j:2 * j + 2, :].rearrange("p a b -> p (a b)")))
                                r.append(e.dma_start(out=sall[ex * 128:(ex + 1) * 128, o + 4096 + 2 * h:o + 4096 + 2 * h + 2], in_=nst[:, 2 * j:2 * j + 2]))
                            return r
                        A2("pool", st_fn, reads=[f"C{j}" for j in range(len(grp))] + ["nst"], writes=["d_sall"], dma_sem="st_S", ndma=2 * len(grp))
                    A2("pool", lambda e: [e.dma_start(out=sall[ex * 128:(ex + 1) * 128, 4104:4112], in_=gexp[:, 0:8]),
                                          e.dma_start(out=sall[ex * 128:(ex + 1) * 128, SW + 4104:SW + 4112], in_=gexp[:, 8:16])],
                       reads=["gexp"], writes=["d_sall"], dma_sem="st_S", ndma=2)

                if mode == "local":
                    local_states_pass()
                else:
                    combine(1)

                def post_b(pos, blk):
                    p = pos % 2
                    A2("pool", lambda e: e.dma_start(out=hbs.ap()[blk], in_=hbb[p][:]), reads=[f"hbb{p}"], writes=[f"d_hbs{blk}"], dma_sem=f"st_hb{p}")
                if mode == "main":
                    sweep(1, True, post_b)

                if mode == "main":
                    combine(0)

                def post_f(pos, blk):
                    p = pos % 2
                    A2("dve", lambda e: e.memset(ssq[:], 0.0), writes=["ssq"])
                    for h in range(4):
                        hc_ = slice(h * 512, (h + 1) * 512)
                        A2("act", lambda e, h=h, hc_=hc_: e.activation(out=junk[:], in_=hbb[p][:, hc_], func=AF.Square, accum_out=ssq[:, h:h + 1]),
                           reads=[f"hbb{p}", "ssq"], writes=["junk", "ssq"])
                    A2("dve", lambda e: e.tensor_scalar(out=rn[:], in0=ssq[:], scalar1=1.0 / DV, scalar2=EPS, op0=ALU.mult, op1=ALU.add), reads=["ssq"], writes=["rn"])
                    A2("act", lambda e: e.activation(out=rn[:], in_=rn[:], func=AF.Sqrt), reads=["rn"], writes=["rn"])
                    A2("dve", lambda e: e.reciprocal(out=rn[:], in_=rn[:]), reads=["rn"], writes=["rn"])
                    for h in range(4):
                        hc_ = slice(h * 512, (h + 1) * 512)
                        t2 = h % 2
                        A2("dve", lambda e, h=h, hc_=hc_, t2=t2: e.scalar_tensor_tensor(out=tmpn[t2][:], in0=hbb[p][:, hc_], scalar=rn[:, h:h + 1], in1=mnt[:, hc_],
                                                                                   op0=ALU.mult, op1=ALU.mult),
                           reads=[f"hbb{p}", "rn", "mnt"], writes=[f"tmpn{t2}"])
                        A2("pool", lambda e, hc_=hc_, t2=t2: e.tensor_tensor(out=gtb[:, hc_], in0=tmpn[t2][:], in1=sob[p][:, hc_], op=ALU.mult),
                           reads=[f"tmpn{t2}", f"sob{p}"], writes=["gtb"])
                    for half8 in range(2):
                        bank = 1 if half8 == 0 else 5
                        pb = ps[bank][:].bitcast(BF16)
                        for j in range(8):
                            jj = half8 * 8 + j
                            A2("pe", lambda e, j=j, jj=jj, pb=pb: e.transpose(pb[:, j * 128:(j + 1) * 128], gtb[:, jj * 128:(jj + 1) * 128], identb[:]),
                               reads=["gtb", "identb"], writes=[f"ps{bank}"])
                        A2("act" if half8 == 0 else "dve",
                           (lambda e, pb=pb, half8=half8: e.copy(out=gTst[p][:, half8 * 8:(half8 + 1) * 8, :], in_=pb[:, 0:1024].rearrange("p (a b) -> p a b", a=8))) if half8 == 0 else
                           (lambda e, pb=pb, half8=half8: e.tensor_copy(out=gTst[p][:, half8 * 8:(half8 + 1) * 8, :], in_=pb[:, 0:1024].rearrange("p (a b) -> p a b", a=8))),
                           reads=[f"ps{bank}"], writes=[f"gTst{p}"])
                    A2("pool", lambda e: e.dma_start(out=gTd[:, :, blk * 128:(blk + 1) * 128], in_=gTst[p][:]), reads=[f"gTst{p}"], writes=["d_gTd"], dma_sem=f"st_gT{p}")
                if mode == "main":
                    sweep(0, True, post_f)

                S2.finalize()
                with nc.Block() as blk2:
                    S2.emit(nc, blk2, sems, base)

        for ex_ in range(nextra):
            run_block1(ex_, True, ex_ == 0)
            run_block2("local", ex_)
        run_block1(3, False, nextra == 0)
        if not do_phase2:
            return nc
        run_block2("main", None)
        if p2stage < 6:
            return nc
        with ExitStack() as es3:
            new_block_sems()
            def sb3(name, shape, dt=F32):
                return es3.enter_context(nc.sbuf_tensor(name, list(shape), dt))
            xt = sb3("xt3", [128, KC, T])
            zt = sb3("zt3", [128, KC, T], BF16)
            hid = sb3("hid3", [128, FC, T], BF16)
            wf = [sb3(f"wf3_{i}", [128, 2048]) for i in range(WStream.NF)]
            wb = [sb3(f"wb3_{i}", [128, 2048], BF16) for i in range(WStream.NB)]
            rstd = sb3("rstd3", [128, T])
            tmp = [sb3(f"tmp3_{i}", [128, T]) for i in range(2)]
            sg = [sb3(f"sg3_{i}", [128, T]) for i in range(2)]
            plan3 = []
            for i in range(ntiles):
                for m in range(KC):
                    plan3.append((mwout_r[:, :, m * 128:(m + 1) * 128], 16, 128))
                plan_ffn(1, plan3)
            S3 = Sched()
            W3 = WStream(S3, wf, wb, plan3)
            rmsnorm_mod, mm_slab, ffn, nbank = make_helpers(S3, W3, xt, zt, hid, rstd, None, tmp, None, sg, None, None)
            for i in range(ntiles):
                c0 = i * T
                S3.add("pool", lambda e, c0=c0: e.dma_start(out=xt[:], in_=x1s_r[:, :, c0:c0 + T]),
                       writes=[f"x{k}" for k in range(KC)], dma_sem="x")
                S3.add("pool", lambda e, c0=c0: e.dma_start(out=zt[:], in_=gTd[:, :, c0:c0 + T]),
                       writes=[f"z{k}" for k in range(KC)], dma_sem="xh")
                for m in range(KC):
                    b = nbank()
                    wv, wres = W3.next(16, 128)
                    mm_slab(b, wv, wres, lambda k: zt[:, k, :], lambda k: f"z{k}", 16, True, True)
                    S3.add("dve", lambda e, b=b, m=m: e.scalar_tensor_tensor(
                        out=xt[:, m, :], in0=ps[b][:, :], scalar=modt[:, 96 + 32 + m:96 + 33 + m], in1=xt[:, m, :], op0=ALU.mult, op1=ALU.add),
                        reads=[f"ps{b}", f"x{m}", "modt"], writes=[f"x{m}"])
                rmsnorm_mod(avec[:, 48:64], modt[:, 96 + 48:96 + 64])
                ffn(1, modt[:, 96 + 80:96 + 96])
                rmsnorm_mod(avec[:, 64:80], modt[:, 192:208], outx=True)
                S3.add("pool", lambda e, c0=c0: e.dma_start(out=yT_r[:, :, c0:c0 + T], in_=xt[:]),
                       reads=[f"x{k}" for k in range(KC)], writes=["d_y"], dma_sem="st_y")
            assert W3.used == len(plan3)
            S3.finalize()
            with nc.Block() as blk3:
                S3.emit(nc, blk3, sems, base)
    return nc


def _prep_inputs(inp):
    f32 = np.float32
    xp = np.asarray(inp["x_prompt"], f32)[0]
    xs = np.asarray(inp["x_sample"], f32)
    cp = np.asarray(inp["c_prompt"], f32)
    csm = np.asarray(inp["c_sample"], f32)
    seqs = [xp, xs[0], xs[1]]
    cvecs = [cp[0], csm[0], csm[1]]
    core_seq = [0, 0, 0, 0, 1, 1, 2, 2]
    core_off = [0, 4096, 8192, 12288, 0, 4096, 0, 4096]

    def fm(v):
        return np.ascontiguousarray(np.asarray(v, f32).reshape(-1, 128).T)

    norm_g = np.asarray(inp["norm_g"], f32)
    smallv = np.concatenate([
        fm(norm_g.reshape(-1)), fm(np.asarray(inp["ada_b"], f32).reshape(-1)),
        fm(inp["final_ada_b"]), fm(inp["final_g"]), fm(np.asarray(inp["conv_w"], f32).reshape(-1))], axis=1)
    assert smallv.shape == (128, 352)
    mnorm = np.ascontiguousarray(np.broadcast_to(np.asarray(inp["mlstm_norm"], f32).reshape(1, D), (128, D)))
    bgate = np.asarray(inp["mlstm_b_gate"], f32).reshape(16, 1)
    gsel = np.zeros((16, 2), f32)
    gsel[[0, 1, 2, 3, 8, 9, 10, 11], 0] = 1.0
    gsel[[4, 5, 6, 7, 12, 13, 14, 15], 1] = 1.0
    s_idx = np.arange(128)[:, None]
    t_idx = np.arange(128)[None, :]
    U = (s_idx <= t_idx).astype(f32)
    consts = np.concatenate([np.eye(128, dtype=f32), U, (1.0 - U) * NEG, U.T, (1.0 - U.T) * NEG], axis=1).astype(f32)
    shared = dict(
        smallv=smallv, mnorm=mnorm, bgate=bgate, gsel=gsel, consts=consts,
        ada_w=np.asarray(inp["ada_w"], f32), final_ada_w=np.asarray(inp["final_ada_w"], f32),
        ffn_w_in=np.asarray(inp["ffn_w_in"], f32), ffn_w_out=np.asarray(inp["ffn_w_out"], f32),
        conv_w_in=np.asarray(inp["conv_w_in"], f32)[0], conv_w_out=np.asarray(inp["conv_w_out"], f32)[0],
        mlstm_w_in=np.asarray(inp["mlstm_w_in"], f32)[0], mlstm_w_out=np.asarray(inp["mlstm_w_out"], f32)[0])
    def chunk_arrays(sq, o):
        xT = np.ascontiguousarray(sq[o:o + NTOK].T)
        halo = np.zeros((NTILE, 2, D), f32)
        em = np.ones((NTILE, 2), f32)
        for i in range(NTILE):
            l = o + i * T - 1
            r = o + (i + 1) * T
            if l >= 0:
                halo[i, 0] = sq[l]
            else:
                em[i, 0] = 0.0
            if r < sq.shape[0]:
                halo[i, 1] = sq[r]
            else:
                em[i, 1] = 0.0
        xh = np.ascontiguousarray(halo.reshape(NTILE, 2, KC, 128).transpose(3, 0, 2, 1).reshape(128, NTILE * KC * 2))
        emask = np.ascontiguousarray(np.broadcast_to(em.reshape(1, NTILE * 2), (128, NTILE * 2)))
        return xT, xh, emask

    cache = {}
    for c in range(NCORE):
        cache[c] = chunk_arrays(seqs[core_seq[c]], core_off[c])
    in_maps = []
    for c in range(NCORE):
        others = [c2 for c2 in range(NCORE) if core_seq[c2] == core_seq[c] and c2 != c]
        pf = np.zeros(8, f32)
        pb = np.zeros(8, f32)
        ex = []
        for e_ in range(3):
            if e_ < len(others):
                c2 = others[e_]
                if c2 < c:
                    pf[e_] = 1.0
                else:
                    pb[e_] = 1.0
            else:
                c2 = others[0]
            ex.append(c2)
        pred = np.ascontiguousarray(np.broadcast_to(np.concatenate([pf, pb]).reshape(1, 16), (128, 16)))
        xT, xh, emask = cache[c]
        m = dict(shared)
        m.update(xT=xT, xhalo=xh, emask=emask, cvec=fm(cvecs[core_seq[c]]), pred=pred,
                 xTe=np.stack([cache[c2][0] for c2 in ex]), xhalo_e=np.stack([cache[c2][1] for c2 in ex]),
                 emask_e=np.stack([cache[c2][2] for c2 in ex]))
        in_maps.append(m)
    return in_maps, core_seq, core_off


def kernel(**inputs):
    in_maps, core_seq, core_off = _prep_inputs(inputs)
    nc = build_nc()
    res = run_bass_kernel_spmd(nc, in_maps, core_ids=list(range(NCORE)))
    yp = np.empty((1, 16384, D), np.float32)
    ys = np.empty((2, 8192, D), np.float32)
    for c in range(NCORE):
        y = np.asarray(res.results[c]["yT"], np.float32).T
        o = core_off[c]
        if core_seq[c] == 0:
            yp[0, o:o + NTOK] = y
        else:
            ys[core_seq[c] - 1, o:o + NTOK] = y
    return (yp, ys)
```

```python
import numpy as np
import concourse.bass as bass
import concourse.mybir as mybir
from concourse.bass_utils import run_bass_kernel_spmd

F32 = mybir.dt.float32
BF16 = mybir.dt.bfloat16
AF = mybir.ActivationFunctionType
ALU = mybir.AluOpType

D = 2048
KC = 16
FF = 5632
FC = 44
T = 512
NTOK = 4096
NTILE = NTOK // T
NBLK = NTOK // 128
NCORE = 8
H = 4
DQK = 256
DV = 512
EPS = 1e-6
MIN_ = 6160
NEG = -30000.0
_OLEV = 4
_SUB = 4


class Op:
    __slots__ = ("eng", "fn", "deps", "signal", "ticket", "dma_sem", "ndma", "idx", "inc")


class Sched:
    ENGS = ("pe", "act", "dve", "pool", "sp")

    def __init__(self):
        self.ops = {e: [] for e in self.ENGS}
        self.last_w = {}
        self.readers = {}
        self.dma_sems = []

    def add(self, eng, fn, reads=(), writes=(), dma_sem=None, ndma=1, inc=16):
        op = Op()
        op.inc = inc
        op.eng = eng
        op.fn = fn
        op.signal = False
        op.ticket = None
        op.dma_sem = dma_sem
        op.ndma = ndma
        if dma_sem is not None and dma_sem not in self.dma_sems:
            self.dma_sems.append(dma_sem)
        deps = []
        for r in reads:
            w = self.last_w.get(r)
            if w is not None:
                deps.append(w)
        for w_ in writes:
            w = self.last_w.get(w_)
            if w is not None:
                deps.append(w)
            rd = self.readers.get(w_)
            if rd:
                deps.extend(rd.values())
        op.deps = deps
        for d in deps:
            d.signal = True
        for r in reads:
            rd = self.readers.setdefault(r, {})
            key = eng if dma_sem is None else ("dma", dma_sem)
            rd[key] = op
        for w_ in writes:
            self.last_w[w_] = op
            self.readers[w_] = {}
        op.idx = len(self.ops[eng])
        self.ops[eng].append(op)
        return op

    def finalize(self):
        cnt = {e: 0 for e in self.ENGS}
        dcnt = {}
        for e in self.ENGS:
            lst = self.ops[e]
            if lst and lst[-1].dma_sem is None:
                lst[-1].signal = True
            for op in lst:
                if op.dma_sem is not None:
                    dcnt[op.dma_sem] = dcnt.get(op.dma_sem, 0) + op.inc * op.ndma
                    op.ticket = dcnt[op.dma_sem]
                elif op.signal:
                    cnt[e] += 1
                    op.ticket = cnt[e]
        self.final_cnt = cnt
        self.final_dcnt = dcnt

    def emit(self, nc, blk, sems, base):
        engobj = {"pe": blk.tensor, "act": blk.scalar, "dve": blk.vector, "pool": blk.gpsimd, "sp": blk.sync}
        S = self

        def make(ename):
            def body(e):
                waited = {}
                for op in S.ops[ename]:
                    for d in op.deps:
                        if d.dma_sem is not None:
                            key = d.dma_sem
                        else:
                            key = d.eng
                            if d.eng == "pe" and ename == "pe":
                                continue
                        val = d.ticket + base.get(key, 0)
                        if waited.get(key, -1) >= val:
                            continue
                        waited[key] = val
                        e.wait_ge(sems[key], val)
                    r = op.fn(e)
                    if op.dma_sem is not None:
                        if not isinstance(r, (list, tuple)):
                            r = [r]
                        assert len(r) == op.ndma
                        for ins in r:
                            ins.then_inc(sems[op.dma_sem], op.inc)
                    elif op.signal:
                        if isinstance(r, (list, tuple)):
                            r = r[-1]
                        r.then_inc(sems[ename], 1)
                for k, v in S.final_cnt.items():
                    if v > 0 and not (k == ename):
                        e.wait_ge(sems[k], v + base.get(k, 0))
                for k, v in S.final_dcnt.items():
                    e.wait_ge(sems[k], v + base.get(k, 0))
            return body

        for ename in self.ENGS:
            engobj[ename](make(ename))
        for k, v in self.final_cnt.items():
            base[k] = base.get(k, 0) + v
        for k, v in self.final_dcnt.items():
            base[k] = base.get(k, 0) + v


class WStream:
    NF = 5
    NB = 5
    LA = 4

    def __init__(self, S, wf, wb, plan):
        self.S = S
        self.wf = wf
        self.wb = wb
        self.plan = plan
        self.issued = 0
        self.used = 0

    def _issue(self, i):
        src, nk, ncol = self.plan[i]
        sf = i % self.NF
        sb = i % self.NB
        vf = self.wf[sf][:, 0:nk * ncol].rearrange("p (k c) -> p k c", k=nk)
        vb = self.wb[sb][:, 0:nk * ncol].rearrange("p (k c) -> p k c", k=nk)
        self.S.add("sp", lambda e, vf=vf, src=src: e.dma_start(out=vf, in_=src),
                   writes=[f"wf{sf}"], dma_sem=f"wf{sf}")
        ce = ("act", "dve", "pool")[i % 3]
        if ce == "act":
            fn = lambda e, vb=vb, vf=vf: e.copy(out=vb, in_=vf)
        else:
            fn = lambda e, vb=vb, vf=vf: e.tensor_copy(out=vb, in_=vf)
        self.S.add(ce, fn, reads=[f"wf{sf}"], writes=[f"wb{sb}"])

    def next(self, nk, ncol):
        i = self.used
        assert self.plan[i][1] == nk and self.plan[i][2] == ncol, (i, self.plan[i][1:], nk, ncol)
        while self.issued < min(len(self.plan), i + self.LA + 1):
            self._issue(self.issued)
            self.issued += 1
        self.used += 1
        sb = i % self.NB
        vb = self.wb[sb][:, 0:nk * ncol].rearrange("p (k c) -> p k c", k=nk)
        return vb, f"wb{sb}"


def build_nc(ntiles=NTILE, dbg=False, do_phase2=True, nextra=3, p2stage=6):
    nc = bass.Bass("TRN2", target_bir_lowering=False)

    def din(name, shape, dt=F32):
        return nc.dram_tensor(name, list(shape), dt, kind="ExternalInput")

    xT = din("xT", [D, NTOK])
    xTe = din("xTe", [3, D, NTOK])
    xhalo_e = din("xhalo_e", [3, 128, NTILE * KC * 2])
    emask_e = din("emask_e", [3, 128, NTILE * 2])
    xhalo = din("xhalo", [128, NTILE * KC * 2])
    emask = din("emask", [128, NTILE * 2])
    cvec = din("cvec", [128, KC])
    smallv = din("smallv", [128, 64 + 192 + 32 + 16 + 48])
    mnorm = din("mnorm", [128, D])
    bgate = din("bgate", [16, 1])
    gsel = din("gsel", [16, 2])
    consts = din("consts", [128, 5 * 128])
    pred = din("pred", [128, 16])
    ada_w = din("ada_w", [2, D, 6 * D])
    final_ada_w = din("final_ada_w", [D, 2 * D])
    ffn_w_in = din("ffn_w_in", [2, D, 2 * FF])
    ffn_w_out = din("ffn_w_out", [2, FF, D])
    conv_w_in = din("conv_w_in", [D, 3 * D])
    conv_w_out = din("conv_w_out", [D, D])
    mlstm_w_in = din("mlstm_w_in", [D, MIN_])
    mlstm_w_out = din("mlstm_w_out", [D, D])

    yT = nc.dram_tensor("yT", [D, NTOK], F32, kind="ExternalOutput")

    okind = "ExternalOutput" if dbg else "Internal"
    x1s = nc.dram_tensor("x1s", [D, NTOK], F32, kind=okind)
    qs = nc.dram_tensor("qs", [128, 8, NTOK], BF16, kind=okind)
    ks = nc.dram_tensor("ks", [128, 8, NTOK], BF16, kind=okind)
    kts = nc.dram_tensor("kts", [NBLK, 128, 1024], BF16, kind=okind)
    vts = nc.dram_tensor("vts", [NBLK, 128, D], BF16, kind=okind)
    ots = nc.dram_tensor("ots", [NBLK, 128, D], BF16, kind=okind)
    gts = nc.dram_tensor("gts", [16, NTOK], F32, kind=okind)

    SW = 4096 + 16
    hbs = nc.dram_tensor("hbs", [NBLK, 128, D], F32, kind="Internal")
    gTd = nc.dram_tensor("gTd", [128, KC, NTOK], BF16, kind="Internal")
    sloc = nc.dram_tensor("sloc", [128, 2 * SW], F32, kind="Internal")
    sall = nc.dram_tensor("sall", [3 * 128, 2 * SW], F32, kind="Internal")
    xT_r = xT.ap().rearrange("(k p) t -> p k t", p=128)
    x1s_r = x1s.ap().rearrange("(k p) t -> p k t", p=128)
    yT_r = yT.ap().rearrange("(k p) t -> p k t", p=128)
    xsrcs = [xTe.ap()[e_].rearrange("(k p) t -> p k t", p=128) for e_ in range(3)] + [xT_r]
    xhsrcs = [xhalo_e.ap()[e_] for e_ in range(3)] + [xhalo.ap()]
    emsrcs = [emask_e.ap()[e_] for e_ in range(3)] + [emask.ap()]

    def wr(w):
        return w.rearrange("(k p) n -> p k n", p=128)

    adaw_r = [wr(ada_w.ap()[l]) for l in range(2)]
    fadaw_r = wr(final_ada_w.ap())
    fwin_r = [wr(ffn_w_in.ap()[l]) for l in range(2)]
    fwout_r = [wr(ffn_w_out.ap()[l]) for l in range(2)]
    cwin_r = wr(conv_w_in.ap())
    cwout_r = wr(conv_w_out.ap())
    mwin_r = wr(mlstm_w_in.ap())
    mwout_r = wr(mlstm_w_out.ap())

    def plan_ffn(l, plan):
        for j in range(FC):
            plan.append((fwin_r[l][:, :, j * 128:(j + 1) * 128], 16, 128))
            plan.append((fwin_r[l][:, :, FF + j * 128:FF + (j + 1) * 128], 16, 128))
        for m in range(KC):
            for (k0, nk) in ((0, 16), (16, 16), (32, 12)):
                plan.append((fwout_r[l][:, k0:k0 + nk, m * 128:(m + 1) * 128], nk, 128))

    def make_plan1(lite):
        plan1 = []
        for i in range(ntiles):
            for m in range(KC):
                for part in range(3):
                    plan1.append((cwin_r[:, :, part * D + m * 128: part * D + (m + 1) * 128], 16, 128))
            for m in range(KC):
                plan1.append((cwout_r[:, :, m * 128:(m + 1) * 128], 16, 128))
            plan_ffn(0, plan1)
            for c in range(8 if lite else 0, 16):
                plan1.append((mwin_r[:, :, c * 128:(c + 1) * 128], 16, 128))
            for g in range(4 if lite else 8):
                for q4 in range(4):
                    plan1.append((mwin_r[:, q4 * 4:(q4 + 1) * 4, 2048 + g * 512: 2048 + (g + 1) * 512], 4, 512))
        return plan1

    sem_names = ["pe", "act", "dve", "pool", "sp"]
    base = {}

    from contextlib import ExitStack
    with ExitStack() as es:
        def sb(name, shape, dt=F32):
            return es.enter_context(nc.sbuf_tensor(name, list(shape), dt))

        cs = sb("cs", [128, KC])
        small = sb("small", [128, 352])
        modt = sb("modt", [128, 224])
        avec = sb("avec", [128, 5 * KC])
        cst = sb("cst", [128, 640])
        identb = sb("identb", [128, 128], BF16)
        onesb = sb("onesb", [128, 128], BF16)
        emk = sb("emk", [128, NTILE * 2])
        wgb = sb("wgb", [128, KC, 16], BF16)
        wgf = sb("wgf", [128, KC, 16])
        bgt = sb("bgt", [16, 1])
        gsl = sb("gsl", [16, 2])
        graw = sb("graw", [16, NTOK])
        ps = [es.enter_context(nc.psum_tensor(f"ps{i}", [128, 512], F32)) for i in range(8)]
        sems = {}
        blkno = [0]

        def new_block_sems():
            blkno[0] += 1
            names = sem_names + [f"wf{i_}" for i_ in range(WStream.NF)]
            for n_ in names:
                sems[n_] = es.enter_context(nc.semaphore(f"s{blkno[0]}_" + n_))
                base[n_] = 0
        dsem_pool = ["x", "xh", "misc", "st_x1", "st_qk", "st_kt", "st_v",
                     "st_o", "st_g", "mw0", "mw1", "mw2", "cc", "p2a", "p2b", "p2c", "p2d", "p2e", "p2f",
                     "st_y", "st_hb0", "st_hb1", "st_S", "st_k", "st_gT0", "st_gT1", "ld_S0", "ld_S1", "ld_m", "st_o2"]
        for n_ in dsem_pool:
            sems[n_] = es.enter_context(nc.semaphore("d_" + n_))

        norm_g_v = small[:, 0:64]
        ada_b_v = small[:, 64:256]
        fada_b_v = small[:, 256:288]
        final_g_v = small[:, 288:304]
        conv_w_v = small[:, 304:352]

        def make_helpers(S, W, xt, zt, hid, rstd, rstdh, tmp, tmph, sg, xh, zh):
            bank_rr = [0]

            def nbank():
                b = bank_rr[0] % 6
                bank_rr[0] += 1
                return b

            def rmsnorm_mod(avw, shv, with_halo=False, outx=False):
                for kc in range(KC):
                    S.add("act", lambda e, kc=kc: e.activation(out=zt[:, kc, :], in_=xt[:, kc, :], func=AF.Square),
                          reads=[f"x{kc}"], writes=[f"z{kc}"])
                for kc in range(KC):
                    S.add("pe", lambda e, kc=kc: e.matmul(ps[6][:, :], onesb[:], zt[:, kc, :], start=(kc == 0), stop=(kc == KC - 1)),
                          reads=[f"z{kc}", "onesb"], writes=["ps6"])
                S.add("dve", lambda e: e.tensor_scalar(out=rstd[:], in0=ps[6][:, :], scalar1=1.0 / D, scalar2=EPS,
                                                       op0=ALU.mult, op1=ALU.add), reads=["ps6"], writes=["rstd"])
                S.add("act", lambda e: e.activation(out=rstd[:], in_=rstd[:], func=AF.Sqrt), reads=["rstd"], writes=["rstd"])
                S.add("dve", lambda e: e.reciprocal(out=rstd[:], in_=rstd[:]), reads=["rstd"], writes=["rstd"])
                if with_halo:
                    S.add("act", lambda e: e.activation(out=zh[:], in_=xh[:], func=AF.Square), reads=["xh"], writes=["zh"])
                    for kc in range(KC):
                        S.add("pe", lambda e, kc=kc: e.matmul(ps[7][:, 0:2], onesb[:], zh[:, kc, :], start=(kc == 0), stop=(kc == KC - 1)),
                              reads=["zh", "onesb"], writes=["ps7"])
                    S.add("dve", lambda e: e.tensor_scalar(out=rstdh[:], in0=ps[7][:, 0:2], scalar1=1.0 / D, scalar2=EPS,
                                                           op0=ALU.mult, op1=ALU.add), reads=["ps7"], writes=["rstdh"])
                    S.add("act", lambda e: e.activation(out=rstdh[:], in_=rstdh[:], func=AF.Sqrt), reads=["rstdh"], writes=["rstdh"])
                    S.add("dve", lambda e: e.reciprocal(out=rstdh[:], in_=rstdh[:]), reads=["rstdh"], writes=["rstdh"])
                for kc in range(KC):
                    tb = kc % 2
                    S.add("dve", lambda e, kc=kc, tb=tb: e.scalar_tensor_tensor(
                        out=tmp[tb][:], in0=xt[:, kc, :], scalar=avw[:, kc:kc + 1], in1=rstd[:], op0=ALU.mult, op1=ALU.mult),
                        reads=[f"x{kc}", "rstd", "avec"], writes=[f"tmp{tb}"])
                    S.add("act", lambda e, kc=kc, tb=tb: e.activation(out=(xt if outx else zt)[:, kc, :], in_=tmp[tb][:], func=AF.Identity,
                                                                    bias=shv[:, kc:kc + 1], scale=1.0),
                          reads=[f"tmp{tb}", "modt"], writes=[(f"x{kc}" if outx else f"z{kc}")])
                    if with_halo:
                        S.add("dve", lambda e, kc=kc: e.scalar_tensor_tensor(
                            out=tmph[:], in0=xh[:, kc, :], scalar=avw[:, kc:kc + 1], in1=rstdh[:], op0=ALU.mult, op1=ALU.mult),
                            reads=["xh", "rstdh", "avec"], writes=["tmph"])
                        S.add("act", lambda e, kc=kc: e.activation(out=zh[:, kc, :], in_=tmph[:], func=AF.Identity,
                                                                 bias=shv[:, kc:kc + 1], scale=1.0),
                              reads=["tmph", "modt"], writes=["zh"])

            def mm_slab(bank, wv, wres, rhs_fn, rhs_res_fn, nk, first, last, ncols=T, k0=0):
                for k in range(nk):
                    S.add("pe", lambda e, k=k: e.matmul(ps[bank][:, 0:ncols], wv[:, k, :], rhs_fn(k0 + k),
                                                       start=(first and k == 0), stop=(last and k == nk - 1)),
                          reads=[wres, rhs_res_fn(k0 + k)], writes=[f"ps{bank}"])

            def ffn(l, gv):
                for j in range(FC):
                    ba, bb = nbank(), nbank()
                    wv, wres = W.next(16, 128)
                    mm_slab(ba, wv, wres, lambda k: zt[:, k, :], lambda k: f"z{k}", 16, True, True)
                    wv, wres = W.next(16, 128)
                    mm_slab(bb, wv, wres, lambda k: zt[:, k, :], lambda k: f"z{k}", 16, True, True)
                    si = j % 2
                    S.add("act", lambda e, ba=ba, si=si: e.activation(out=sg[si][:], in_=ps[ba][:, :], func=AF.Silu),
                          reads=[f"ps{ba}"], writes=[f"sg{si}"])
                    S.add("dve", lambda e, bb=bb, si=si, j=j: e.tensor_tensor(out=hid[:, j, :], in0=sg[si][:], in1=ps[bb][:, :], op=ALU.mult),
                          reads=[f"sg{si}", f"ps{bb}"], writes=[f"hid{j}"])
                for m in range(KC):
                    b = nbank()
                    for si_, (k0, nk) in enumerate(((0, 16), (16, 16), (32, 12))):
                        wv, wres = W.next(nk, 128)
                        mm_slab(b, wv, wres, lambda k: hid[:, k, :], lambda k: f"hid{k}", nk, si_ == 0, si_ == 2, k0=k0)
                    S.add("dve", lambda e, b=b, m=m: e.scalar_tensor_tensor(
                        out=xt[:, m, :], in0=ps[b][:, :], scalar=gv[:, m:m + 1], in1=xt[:, m, :], op0=ALU.mult, op1=ALU.add),
                        reads=[f"ps{b}", f"x{m}", "modt"], writes=[f"x{m}"])

            return rmsnorm_mod, mm_slab, ffn, nbank

        def run_block1(chunk, lite, first):
            xsrc, xhsrc, emsrc = xsrcs[chunk], xhsrcs[chunk], emsrcs[chunk]
            with ExitStack() as es1:
                new_block_sems()
                def sb1(name, shape, dt=F32):
                    return es1.enter_context(nc.sbuf_tensor(f"{name}_b{blkno[0]}", list(shape), dt))
                xt = sb1("xt", [128, KC, T])
                zt = sb1("zt", [128, KC, T], BF16)
                hid = sb1("hid", [128, FC, T], BF16)
                ostg = [sb1(f"ostg{i_}", [128, 4, 512], BF16) for i_ in range(2)]
                wf = [sb1(f"wf{i}", [128, 2048]) for i in range(WStream.NF)]
                wb = [sb1(f"wb{i}", [128, 2048], BF16) for i in range(WStream.NB)]
                xh = sb1("xh", [128, KC, 2])
                zh = sb1("zh", [128, KC, 2], BF16)
                rstd = sb1("rstd", [128, T])
                rstdh = sb1("rstdh", [128, 2])
                tmp = [sb1(f"tmp{i}", [128, T]) for i in range(2)]
                tmph = sb1("tmph", [128, 2])
                bgs = [sb1(f"bgs{i}", [128, T]) for i in range(2)]
                cgs = [sb1(f"cgs{i}", [128, T]) for i in range(2)]
                uext1_ = sb1("uext0", [128, T + 2])
                uext = [uext1_, uext1_]
                cv1_ = sb1("cv0", [128, T])
                cv = [cv1_, cv1_]
                hc = [sb1(f"hc{i}", [128, 4]) for i in range(2)]
                sg = [sb1(f"sg{i}", [128, T]) for i in range(2)]

                S = Sched()
                plan1 = make_plan1(lite)
                W = WStream(S, wf, wb, plan1)
                if first:
                    S.add("pool", lambda e: [e.dma_start(out=cs[:], in_=cvec[:, :]),
                                             e.dma_start(out=small[:], in_=smallv[:, :]),
                                             e.dma_start(out=cst[:], in_=consts[:, :]),
                                             e.dma_start(out=bgt[:], in_=bgate[:, :]),
                                             e.dma_start(out=gsl[:], in_=gsel[:, :]),
                                             e.dma_start(out=wgf[:], in_=mwin_r[:, :, 6144:6160])],
                          writes=["cs", "small", "cst", "bgt", "gsl", "wgf"], dma_sem="misc", ndma=6)
                    S.add("act", lambda e: e.activation(out=cs[:], in_=cs[:], func=AF.Silu), reads=["cs"], writes=["cs"])
                    S.add("dve", lambda e: e.tensor_copy(out=identb[:], in_=cst[:, 0:128]), reads=["cst"], writes=["identb"])
                    S.add("dve", lambda e: e.memset(onesb[:], 1.0), writes=["onesb"])
                    S.add("dve", lambda e: e.tensor_copy(out=wgb[:], in_=wgf[:]), reads=["wgf"], writes=["wgb"])

                    mod_jobs = []
                    mod_jobs.append((adaw_r[0], 24, 0, ada_b_v[:, 0:96]))
                    mod_jobs.append((adaw_r[1], 24, 96, ada_b_v[:, 96:192]))
                    mod_jobs.append((fadaw_r, 8, 192, fada_b_v))
                    piece_i = 0
                    for (w_r, ng, cbase, bview) in mod_jobs:
                        for g in range(ng):
                            for kq in range(4):
                                slot = piece_i % 4
                                piece_i += 1
                                src = w_r[:, kq * 4:(kq + 1) * 4, g * 512:(g + 1) * 512]
                                mwv = wf[slot][:, :].rearrange("p (k c) -> p k c", k=4)
                                S.add("sp", lambda e, mwv=mwv, src=src: e.dma_start(out=mwv, in_=src),
                                      writes=[f"wf{slot}"], dma_sem=f"wf{slot}")
                                for m in range(4):
                                    for kcl in range(4):
                                        kc = kq * 4 + kcl
                                        S.add("pe", lambda e, m=m, kcl=kcl, kc=kc, mwv=mwv, kq=kq: e.matmul(
                                            ps[m][:, 0:1], mwv[:, kcl, m * 128:(m + 1) * 128], cs[:, kc:kc + 1],
                                            start=(kq == 0 and kcl == 0), stop=(kq == 3 and kcl == 3)),
                                            reads=[f"wf{slot}", "cs"], writes=[f"ps{m}"])
                            for m in range(4):
                                col = cbase + g * 4 + m
                                bc = g * 4 + m
                                S.add("dve", lambda e, m=m, col=col, bc=bc, bview=bview: e.tensor_tensor(
                                    out=modt[:, col:col + 1], in0=ps[m][:, 0:1], in1=bview[:, bc:bc + 1], op=ALU.add),
                                    reads=[f"ps{m}", "small"], writes=["modt"])
                    def mk_a(idx, gview, scview):
                        S.add("dve", lambda e: e.scalar_tensor_tensor(out=avec[:, idx * 16:(idx + 1) * 16], in0=scview, scalar=1.0,
                                                                      in1=gview, op0=ALU.add, op1=ALU.mult),
                              reads=["modt", "small"], writes=["avec"])
                    mk_a(0, norm_g_v[:, 0:16], modt[:, 16:32])
                    mk_a(1, norm_g_v[:, 16:32], modt[:, 64:80])
                    mk_a(2, norm_g_v[:, 32:48], modt[:, 96 + 16:96 + 32])
                    mk_a(3, norm_g_v[:, 48:64], modt[:, 96 + 64:96 + 80])
                    mk_a(4, final_g_v, modt[:, 192 + 16:192 + 32])

                S.add("pool", lambda e: e.dma_start(out=emk[:], in_=emsrc), writes=["emk"], dma_sem="ld_m")
                rmsnorm_mod, mm_slab, ffn, nbank = make_helpers(S, W, xt, zt, hid, rstd, rstdh, tmp, tmph, sg, xh, zh)

                for i in range(ntiles):
                    c0 = i * T
                    S.add("pool", lambda e, c0=c0: e.dma_start(out=xt[:], in_=xsrc[:, :, c0:c0 + T]),
                          writes=[f"x{k}" for k in range(KC)], dma_sem="x")
                    S.add("pool", lambda e, i=i: e.dma_start(out=xh[:].rearrange("p k t -> p (k t)"),
                                                              in_=xhsrc[:, i * 32:(i + 1) * 32]),
                          writes=["xh"], dma_sem="xh")
                    rmsnorm_mod(avec[:, 0:16], modt[:, 0:16], with_halo=True)
                    for m in range(KC):
                        ba, bb, bc_ = nbank(), nbank(), nbank()
                        wv, wres = W.next(16, 128)
                        mm_slab(ba, wv, wres, lambda k: zt[:, k, :], lambda k: f"z{k}", 16, True, True)
                        wv, wres = W.next(16, 128)
                        mm_slab(bb, wv, wres, lambda k: zt[:, k, :], lambda k: f"z{k}", 16, True, True)
                        for k in range(KC):
                            S.add("pe", lambda e, k=k, wv=wv: e.matmul(ps[7][:, 0:2], wv[:, k, :], zh[:, k, :], start=(k == 0), stop=(k == KC - 1)),
                                  reads=[wres, "zh"], writes=["ps7"])
                        wv, wres = W.next(16, 128)
                        mm_slab(bc_, wv, wres, lambda k: zt[:, k, :], lambda k: f"z{k}", 16, True, True)
                        for k in range(KC):
                            S.add("pe", lambda e, k=k, wv=wv: e.matmul(ps[7][:, 2:4], wv[:, k, :], zh[:, k, :], start=(k == 0), stop=(k == KC - 1)),
                                  reads=[wres, "zh"], writes=["ps7"])
                        r2 = m % 2
                        S.add("act", lambda e, ba=ba, r2=r2: e.copy(out=bgs[r2][:], in_=ps[ba][:, :]), reads=[f"ps{ba}"], writes=[f"bgs{r2}"])
                        S.add("act", lambda e, bb=bb, r2=r2: e.copy(out=cgs[r2][:], in_=ps[bb][:, :]), reads=[f"ps{bb}"], writes=[f"cgs{r2}"])
                        S.add("act", lambda e, r2=r2: e.copy(out=hc[r2][:], in_=ps[7][:, 0:4]), reads=["ps7"], writes=[f"hc{r2}"])
                        S.add("dve", lambda e, bc_=bc_, r2=r2: e.tensor_tensor(out=uext[r2][:, 1:T + 1], in0=cgs[r2][:], in1=ps[bc_][:, :], op=ALU.mult),
                              reads=[f"cgs{r2}", f"ps{bc_}"], writes=["uext0"])
                        S.add("dve", lambda e, r2=r2, i=i: e.scalar_tensor_tensor(
                            out=uext[r2][:, 0:1], in0=hc[r2][:, 0:1], scalar=emk[:, 2 * i:2 * i + 1], in1=hc[r2][:, 2:3], op0=ALU.mult, op1=ALU.mult),
                            reads=[f"hc{r2}", "emk", "uext0"], writes=["uext0"])
                        S.add("dve", lambda e, r2=r2, i=i: e.scalar_tensor_tensor(
                            out=uext[r2][:, T + 1:T + 2], in0=hc[r2][:, 1:2], scalar=emk[:, 2 * i + 1:2 * i + 2], in1=hc[r2][:, 3:4], op0=ALU.mult, op1=ALU.mult),
                            reads=[f"hc{r2}", "emk", "uext0"], writes=["uext0"])
                        S.add("dve", lambda e, r2=r2, m=m: e.tensor_scalar(out=cv[r2][:], in0=uext[r2][:, 0:T], scalar1=conv_w_v[:, m:m + 1], scalar2=None, op0=ALU.mult),
                              reads=["uext0", "small"], writes=["cv0"])
                        S.add("dve", lambda e, r2=r2, m=m: e.scalar_tensor_tensor(out=cv[r2][:], in0=uext[r2][:, 1:T + 1], scalar=conv_w_v[:, 16 + m:17 + m],
                                                                                  in1=cv[r2][:], op0=ALU.mult, op1=ALU.add),
                              reads=["uext0", "small", "cv0"], writes=["cv0"])
                        S.add("dve", lambda e, r2=r2, m=m: e.scalar_tensor_tensor(out=cv[r2][:], in0=uext[r2][:, 2:T + 2], scalar=conv_w_v[:, 32 + m:33 + m],
                                                                                  in1=cv[r2][:], op0=ALU.mult, op1=ALU.add),
                              reads=["uext0", "small", "cv0"], writes=["cv0"])
                        S.add("dve", lambda e, r2=r2, m=m: e.tensor_tensor(out=hid[:, m, :], in0=cv[r2][:], in1=bgs[r2][:], op=ALU.mult),
                              reads=["cv0", f"bgs{r2}"], writes=[f"hid{m}"])
                    for m in range(KC):
                        b = nbank()
                        wv, wres = W.next(16, 128)
                        mm_slab(b, wv, wres, lambda k: hid[:, k, :], lambda k: f"hid{k}", 16, True, True)
                        S.add("dve", lambda e, b=b, m=m: e.scalar_tensor_tensor(
                            out=xt[:, m, :], in0=ps[b][:, :], scalar=modt[:, 32 + m:33 + m], in1=xt[:, m, :], op0=ALU.mult, op1=ALU.add),
                            reads=[f"ps{b}", f"x{m}", "modt"], writes=[f"x{m}"])
                    rmsnorm_mod(avec[:, 16:32], modt[:, 48:64])
                    ffn(0, modt[:, 80:96])
                    if not lite:
                      S.add("pool", lambda e, c0=c0: e.dma_start(out=x1s_r[:, :, c0:c0 + T], in_=xt[:]),
                          reads=[f"x{k}" for k in range(KC)], writes=["d_x1s"], dma_sem="st_x1")
                    rmsnorm_mod(avec[:, 32:48], modt[:, 96:112])
                    for c in range(8 if lite else 0, 16):
                        b = nbank()
                        wv, wres = W.next(16, 128)
                        mm_slab(b, wv, wres, lambda k: zt[:, k, :], lambda k: f"z{k}", 16, True, True)
                        if c < 8:
                            S.add("act", lambda e, b=b, c=c: e.activation(out=hid[:, c, :], in_=ps[b][:, :], func=AF.Copy, scale=1.0 / 16.0),
                                  reads=[f"ps{b}"], writes=[f"hid{c}"])
                        else:
                            S.add("dve", lambda e, b=b, c=c: e.tensor_copy(out=hid[:, c, :], in_=ps[b][:, :]),
                                  reads=[f"ps{b}"], writes=[f"hid{c}"])
                    if not lite:
                      S.add("pool", lambda e, c0=c0: e.dma_start(out=qs[:, :, c0:c0 + T], in_=hid[:, 0:8, :]),
                          reads=[f"hid{c}" for c in range(8)], writes=["d_qs"], dma_sem="st_qk")
                    if not lite:
                      S.add("pool", lambda e, c0=c0: e.dma_start(out=ks[:, :, c0:c0 + T], in_=hid[:, 8:16, :]),
                          reads=[f"hid{c}" for c in range(8, 16)], writes=["d_ks"], dma_sem="st_k")
                    ktv = hid[:, 16:24, :].rearrange("p (a b) c -> p a (b c)", b=2)
                    ps7b = ps[7][:].bitcast(BF16)
                    for tb in range(4):
                        for c in range(8):
                            S.add("pe", lambda e, c=c, tb=tb: e.transpose(ps7b[:, c * 128:(c + 1) * 128], hid[:, 8 + c, tb * 128:(tb + 1) * 128], identb[:]),
                                  reads=[f"hid{8 + c}", "identb"], writes=["ps7"])
                        S.add("dve", lambda e, tb=tb: e.tensor_copy(out=ktv[:, tb, :], in_=ps7b[:, :]),
                              reads=["ps7"], writes=[f"hid{16 + 2 * tb}", f"hid{17 + 2 * tb}"])
                    S.add("pool", lambda e, i=i: e.dma_start(out=kts.ap()[4 * i:4 * i + 4].rearrange("b p c -> p b c"), in_=ktv),
                          reads=[f"hid{c}" for c in range(16, 24)], writes=["d_kts"], dma_sem="st_kt")
                    vtv = hid[:, 24:40, :].rearrange("p (a b) c -> p a (b c)", b=4)
                    for g in range(4 if lite else 8):
                        for q4 in range(4):
                            wv, wres = W.next(4, 512)
                            for kcl in range(4):
                                kc = q4 * 4 + kcl
                                for tb in range(4):
                                    S.add("pe", lambda e, kc=kc, kcl=kcl, tb=tb, wv=wv, q4=q4: e.matmul(
                                        ps[tb][:, :], zt[:, kc, tb * 128:(tb + 1) * 128], wv[:, kcl, :],
                                        start=(q4 == 0 and kcl == 0), stop=(q4 == 3 and kcl == 3)),
                                        reads=[wres, f"z{kc}"], writes=[f"ps{tb}"])
                        for tb in range(4):
                            if g < 4:
                                S.add("dve" if tb % 2 else "act",
                                      (lambda e, tb=tb, g=g: e.tensor_copy(out=vtv[:, tb, g * 512:(g + 1) * 512], in_=ps[tb][:, :])) if tb % 2 else
                                      (lambda e, tb=tb, g=g: e.copy(out=vtv[:, tb, g * 512:(g + 1) * 512], in_=ps[tb][:, :])),
                                      reads=[f"ps{tb}"], writes=[f"hid{24 + 4 * tb + k}" for k in range(4)])
                            else:
                                go = g - 4
                                S.add("act", lambda e, tb=tb, go=go: e.activation(out=ostg[go % 2][:, tb, :], in_=ps[tb][:, :], func=AF.Sigmoid),
                                      reads=[f"ps{tb}"], writes=[f"ostg{go % 2}"])
                        if g == 3:
                            S.add("pool", lambda e, i=i: e.dma_start(out=vts.ap()[4 * i:4 * i + 4].rearrange("b p c -> p b c"), in_=vtv),
                                  reads=[f"hid{c}" for c in range(24, 40)], writes=["d_vts"], dma_sem="st_v")
                        if g >= 4:
                            go = g - 4
                            S.add("pool", lambda e, i=i, go=go: e.dma_start(out=ots.ap()[4 * i:4 * i + 4, :, go * 512:(go + 1) * 512].rearrange("b p c -> p b c"), in_=ostg[go % 2][:]),
                                  reads=[f"ostg{go % 2}"], writes=["d_ots"], dma_sem=("st_o" if go % 2 == 0 else "st_o2"))
                    for k in range(KC):
                        S.add("pe", lambda e, k=k: e.matmul(ps[6][0:16, :], wgb[:, k, :], zt[:, k, :], start=(k == 0), stop=(k == KC - 1)),
                              reads=["wgb", f"z{k}"], writes=["ps6"])
                    S.add("act", lambda e, c0=c0: e.activation(out=graw[:, c0:c0 + T], in_=ps[6][0:16, :], func=AF.Identity, bias=bgt[:, 0:1], scale=1.0),
                          reads=["ps6", "bgt"], writes=["graw"])

                if dbg:
                    S.add("pool", lambda e: e.dma_start(out=gts[:, :], in_=graw[:]), reads=["graw"], writes=["d_gts"], dma_sem="st_g")
                assert W.used == len(plan1), (W.used, len(plan1))
                S.finalize()
                with nc.Block() as blk:
                    S.emit(nc, blk, sems, base)
        nblk = ntiles * 4

        def run_block2(mode, ex):
            with ExitStack() as es2:
                blkno[0] += 1
                def sb2(name, shape, dt=F32):
                    return es2.enter_context(nc.sbuf_tensor(f"{name}_b{blkno[0]}", list(shape), dt))
                C = sb2("C", [128, 8, 512])
                Cb = sb2("Cb", [128, 8, 512], BF16)
                nst = sb2("nst", [128, 8])
                nb = sb2("nb", [128, 8], BF16)
                qb = [sb2(f"qb{i}", [128, 8, 128], BF16) for i in range(2)]
                kb = [sb2(f"kb{i}", [128, 8, 128], BF16) for i in range(2)]
                ktb = [sb2(f"ktb{i}", [128, 1024], BF16) for i in range(2)]
                vtb = [sb2(f"vtb{i}", [128, D], BF16) for i in range(2)]
                sob = [sb2(f"sob{i}", [128, D], BF16) for i in range(2)]
                hbb = [sb2(f"hbb{i}", [128, D]) for i in range(2)]
                glf = sb2("glf", [16, NTOK])
                gtok = sb2("gtok", [128, NBLK, 16])
                cumf = sb2("cumf", [128, NBLK, 16])
                cumb = sb2("cumb", [128, NBLK, 16])
                tot = sb2("tot", [128, NBLK, 16])
                dec = sb2("dec", [128, NBLK, 16])
                biasD = sb2("biasD", [128, NBLK, 8])
                wk = sb2("wk", [128, NBLK, 8])
                gsum = sb2("gsum", [128, 16])
                gexp = sb2("gexp", [128, 16])
                coef = sb2("coef", [128, 4])
                onesf = sb2("onesf", [128, 128])
                Rt = [sb2(f"Rt{i}", [128, 128]) for i in range(2)]
                EBt = [sb2(f"EBt{i}", [128, 128]) for i in range(2)]
                DTt = [sb2(f"DTt{i}", [128, 128]) for i in range(2)]
                ptl = [sb2(f"pt{i}", [128, 128], BF16) for i in range(2)]
                qtil = [sb2(f"qtil{i}", [128, 2, 128], BF16) for i in range(2)]
                ktil = [sb2(f"ktil{i}", [128, 256], BF16) for i in range(2)]
                dab = [sb2(f"dab{i}", [128, 1]) for i in range(2)]
                rden = [sb2(f"rden{i}", [128, 1]) for i in range(2)]
                ssq = sb2("ssq", [128, 4])
                rn = sb2("rn", [128, 4])
                tmpn = [sb2(f"tmpn{i}", [128, 512]) for i in range(2)]
                junk = sb2("junk", [128, 512])
                gtb = sb2("gtb", [128, D], BF16)
                gTst = [sb2(f"gTst{i}", [128, KC, 128], BF16) for i in range(2)]
                mnt = sb2("mnt", [128, D])
                sstg = [sb2(f"sstg{i}", [128, SW]) for i in range(2)]
                prd = sb2("prd", [128, 16])

                S2 = Sched()
                A2 = S2.add
                A2("pool", lambda e: [e.dma_start(out=prd[:], in_=pred[:, :]), e.dma_start(out=mnt[:], in_=mnorm[:, :])],
                   writes=["prd", "mnt"], dma_sem="ld_m", ndma=2)
                A2("dve", lambda e: e.memset(onesf[:], 1.0), writes=["onesf"])
                identf = cst[:, 0:128]

                A2("act", lambda e: e.activation(out=graw[:], in_=graw[:], func=AF.Tanh, scale=1.0 / 15.0), reads=["graw"], writes=["graw"])
                A2("dve", lambda e: e.tensor_scalar(out=graw[:], in0=graw[:], scalar1=15.0, scalar2=None, op0=ALU.mult), reads=["graw"], writes=["graw"])
                A2("act", lambda e: e.activation(out=glf[:], in_=graw[:], func=AF.Sigmoid), reads=["graw"], writes=["glf"])
                A2("act", lambda e: e.activation(out=glf[:], in_=glf[:], func=AF.Ln), reads=["glf"], writes=["glf"])
                A2("dve", lambda e: e.tensor_scalar(out=glf[:], in0=glf[:], scalar1=gsl[:, 1:2], scalar2=None, op0=ALU.mult), reads=["glf", "gsl"], writes=["glf"])
                A2("dve", lambda e: e.scalar_tensor_tensor(out=graw[:], in0=graw[:], scalar=gsl[:, 0:1], in1=glf[:], op0=ALU.mult, op1=ALU.add),
                   reads=["graw", "glf", "gsl"], writes=["graw"])
                if dbg:
                    A2("pool", lambda e: e.dma_start(out=gts[:, :], in_=graw[:]), reads=["graw"], writes=["d_gts"], dma_sem="st_g")
                for b_ in range(nblk):
                    A2("pe", lambda e, b_=b_: e.matmul(ps[0][:, b_ * 16:(b_ + 1) * 16], graw[0:16, b_ * 128:(b_ + 1) * 128], cst[0:16, 0:16], start=True, stop=True),
                       reads=["graw", "cst"], writes=["ps0"])
                gtok2 = gtok[:].rearrange("p b j -> p (b j)")
                A2("dve", lambda e: e.tensor_copy(out=gtok2[:, 0:nblk * 16], in_=ps[0][:, 0:nblk * 16]), reads=["ps0"], writes=["gtok"])
                for (bank, lhs, dst, lres, dname) in ((1, cst[:, 128:256], cumf, "cst", "cumf"), (2, cst[:, 384:512], cumb, "cst", "cumb"), (3, onesf[:], tot, "onesf", "tot")):
                    for b_ in range(nblk):
                        A2("pe", lambda e, b_=b_, bank=bank, lhs=lhs: e.matmul(ps[bank][:, b_ * 16:(b_ + 1) * 16], lhs, gtok[:, b_, :], start=True, stop=True),
                           reads=["gtok", lres], writes=[f"ps{bank}"])
                    d2 = dst[:].rearrange("p b j -> p (b j)")
                    A2("dve", lambda e, d2=d2, bank=bank: e.tensor_copy(out=d2[:, 0:nblk * 16], in_=ps[bank][:, 0:nblk * 16]), reads=[f"ps{bank}"], writes=[dname])
                A2("dve", lambda e: e.tensor_tensor(out=biasD[:, 0:nblk, 0:4], in0=gtok[:, 0:nblk, 0:4], in1=cumf[:, 0:nblk, 4:8], op=ALU.subtract),
                   reads=["gtok", "cumf", "cumb", "tot"], writes=["biasD"])
                A2("dve", lambda e: e.tensor_tensor(out=biasD[:, 0:nblk, 4:8], in0=gtok[:, 0:nblk, 8:12], in1=cumb[:, 0:nblk, 12:16], op=ALU.subtract),
                   reads=["gtok", "cumb", "biasD"], writes=["biasD"])
                A2("dve", lambda e: e.tensor_tensor(out=wk[:, 0:nblk, 0:4], in0=tot[:, 0:nblk, 4:8], in1=biasD[:, 0:nblk, 0:4], op=ALU.add),
                   reads=["biasD", "tot"], writes=["wk"])
                A2("dve", lambda e: e.tensor_tensor(out=wk[:, 0:nblk, 4:8], in0=tot[:, 0:nblk, 12:16], in1=biasD[:, 0:nblk, 4:8], op=ALU.add),
                   reads=["biasD", "tot", "wk"], writes=["wk"])
                A2("act", lambda e: e.activation(out=wk[:, 0:nblk, :], in_=wk[:, 0:nblk, :], func=AF.Exp), reads=["wk"], writes=["wk"])
                A2("act", lambda e: e.activation(out=dec[:, 0:nblk, :], in_=tot[:, 0:nblk, :], func=AF.Exp), reads=["tot"], writes=["dec"])
                A2("dve", lambda e: e.tensor_reduce(out=gsum[:], in_=tot[:, 0:nblk, :].rearrange("p b j -> p j b"), axis=mybir.AxisListType.X, op=ALU.add),
                   reads=["tot"], writes=["gsum"])
                A2("act", lambda e: e.activation(out=gexp[:], in_=gsum[:], func=AF.Exp), reads=["gsum"], writes=["gexp"])

                C2 = C[:].rearrange("p a b -> p (a b)")
                Cb2 = Cb[:].rearrange("p a b -> p (a b)")
                Cres = [f"C{h}" for h in range(4)]
                Cbres = [f"Cb{h}" for h in range(4)]

                def zero_state():
                    A2("dve", lambda e: e.memset(C2, 0.0), writes=Cres)
                    A2("dve", lambda e: e.memset(nst[:], 0.0), writes=["nst"])

                def cast_state():
                    A2("act", lambda e: e.copy(out=Cb2, in_=C2), reads=Cres, writes=Cbres)
                    A2("dve", lambda e: e.tensor_copy(out=nb[:], in_=nst[:]), reads=["nst"], writes=["nb"])

                dsem_qk = ["p2a", "p2b"]
                dsem_ktv = ["p2c", "p2d"]
                dsem_soh = ["p2e", "p2f"]

                def load_blk(pos, blk, outputs, dirn):
                    p = pos % 2
                    c0 = blk * 128
                    A2("sp", lambda e: [e.dma_start(out=ktb[p][:], in_=kts.ap()[blk]), e.dma_start(out=vtb[p][:], in_=vts.ap()[blk])],
                       reads=["d_kts", "d_vts"], writes=[f"ktb{p}", f"vtb{p}"], dma_sem=dsem_ktv[p], ndma=2)
                    if outputs:
                        A2("sp", lambda e: [e.dma_start(out=qb[p][:], in_=qs[:, :, c0:c0 + 128]), e.dma_start(out=kb[p][:], in_=ks[:, :, c0:c0 + 128])],
                           reads=["d_qs", "d_ks"], writes=[f"qb{p}", f"kb{p}"], dma_sem=dsem_qk[p], ndma=2)
                        if dirn == 0:
                            A2("sp", lambda e: [e.dma_start(out=sob[p][:], in_=ots.ap()[blk]), e.dma_start(out=hbb[p][:], in_=hbs.ap()[blk])],
                               reads=["d_ots", f"d_hbs{blk}"], writes=[f"sob{p}", f"hbb{p}"], dma_sem=dsem_soh[p], ndma=2)

                def scan_block(pos, blk, dirn, outputs):
                    p = pos % 2
                    U_ = cst[:, 128:256] if dirn == 0 else cst[:, 384:512]
                    M_ = cst[:, 256:384] if dirn == 0 else cst[:, 512:640]
                    for pair in ((0, 1), (2, 3)):
                        def st0(h):
                            par = h % 2
                            bE = 4 * par
                            gl = 8 * dirn + 4 + h
                            bi = 4 * dirn + h
                            if outputs:
                                A2("dve", lambda e: e.tensor_scalar(out=Rt[par][:], in0=U_, scalar1=gtok[:, blk, gl:gl + 1], scalar2=None, op0=ALU.mult),
                                   reads=["cst", "gtok"], writes=[f"Rt{par}"])
                                A2("pe", lambda e: e.matmul(ps[bE][:, 0:128], onesf[:], Rt[par][:], start=True, stop=True),
                                   reads=["onesf", f"Rt{par}"], writes=[f"ps{bE}"])
                                A2("pe", lambda e: e.matmul(ps[bE][:, 128:256], onesf[:], Rt[par][:], start=True, stop=False),
                                   reads=["onesf", f"Rt{par}"], writes=[f"ps{bE}"])
                                A2("pe", lambda e: e.matmul(ps[bE][:, 128:256], identf, M_, start=False, stop=True),
                                   reads=["cst"], writes=[f"ps{bE}"])
                                for half in range(2):
                                    A2("pe", lambda e, half=half: e.matmul(ps[bE][:, 256:384], kb[p][:, 2 * h + half, :], qb[p][:, 2 * h + half, :],
                                                                          start=(half == 0), stop=(half == 1)),
                                       reads=[f"kb{p}", f"qb{p}"], writes=[f"ps{bE}"])
                            A2("dve", lambda e: e.tensor_scalar(out=ktil[par][:], in0=ktb[p][:, h * 256:(h + 1) * 256], scalar1=wk[:, blk, bi:bi + 1], scalar2=None, op0=ALU.mult),
                               reads=[f"ktb{p}", "wk"], writes=[f"ktil{par}"])

                        def st1(h):
                            par = h % 2
                            bE = 4 * par
                            bi = 4 * dirn + h
                            if outputs and _OLEV >= 2:
                                A2("act", lambda e: e.activation(out=EBt[par][:], in_=ps[bE][:, 0:128], func=AF.Exp), reads=[f"ps{bE}"], writes=[f"EBt{par}"])
                                if _SUB >= 2:
                                  A2("act", lambda e: e.activation(out=DTt[par][:], in_=ps[bE][:, 128:256], func=AF.Exp, bias=biasD[:, blk, bi:bi + 1], scale=1.0),
                                   reads=[f"ps{bE}", "biasD"], writes=[f"DTt{par}"])
                                if _SUB >= 3:
                                  A2("dve", lambda e: e.tensor_tensor(out=ptl[par][:], in0=ps[bE][:, 256:384], in1=DTt[par][:], op=ALU.mult),
                                   reads=[f"ps{bE}", f"DTt{par}"], writes=[f"pt{par}"])
                                for half in range(2 if _SUB >= 4 else 0):
                                    A2("dve", lambda e, half=half: e.tensor_tensor(out=qtil[par][:, half, :], in0=qb[p][:, 2 * h + half, :], in1=EBt[par][:], op=ALU.mult),
                                       reads=[f"qb{p}", f"EBt{par}"], writes=[f"qtil{par}"])

                        def st2(h):
                            par = h % 2
                            bE, bN, bC0, bC1 = 4 * par, 4 * par + 1, 4 * par + 2, 4 * par + 3
                            vh = vtb[p][:, h * 512:(h + 1) * 512]
                            if outputs and _OLEV >= 3:
                                A2("pe", lambda e: e.matmul(ps[bN][:, :], ptl[par][:], vh, start=True, stop=False),
                                   reads=[f"pt{par}", f"vtb{p}"], writes=[f"ps{bN}"])
                                for half in range(2):
                                    A2("pe", lambda e, half=half: e.matmul(ps[bN][:, :], qtil[par][:, half, :], Cb[:, 2 * h + half, :], start=False, stop=(half == 1)),
                                       reads=[f"qtil{par}", f"Cb{h}"], writes=[f"ps{bN}"])
                                A2("pe", lambda e: e.matmul(ps[bE][:, 384:385], ptl[par][:], onesb[:, 0:1], start=True, stop=False),
                                   reads=[f"pt{par}", "onesb"], writes=[f"ps{bE}"])
                                for half in range(2):
                                    A2("pe", lambda e, half=half: e.matmul(ps[bE][:, 384:385], qtil[par][:, half, :], nb[:, 2 * h + half:2 * h + half + 1], start=False, stop=(half == 1)),
                                       reads=[f"qtil{par}", "nb"], writes=[f"ps{bE}"])
                            for half, bC in ((0, bC0), (1, bC1)):
                                A2("pe", lambda e, half=half, bC=bC: e.matmul(ps[bC][:, :], ktil[par][:, half * 128:(half + 1) * 128], vh, start=True, stop=True),
                                   reads=[f"ktil{par}", f"vtb{p}"], writes=[f"ps{bC}"])
                                A2("pe", lambda e, half=half: e.matmul(ps[bE][:, 386 + half:387 + half], ktil[par][:, half * 128:(half + 1) * 128], onesb[:, 0:1], start=True, stop=True),
                                   reads=[f"ktil{par}", "onesb"], writes=[f"ps{bE}"])

                        def st3(h):
                            par = h % 2
                            bE, bN, bC0, bC1 = 4 * par, 4 * par + 1, 4 * par + 2, 4 * par + 3
                            gl = 8 * dirn + 4 + h
                            hc_ = slice(h * 512, (h + 1) * 512)
                            if outputs and _OLEV >= 4:
                                A2("act", lambda e: e.activation(out=dab[par][:], in_=ps[bE][:, 384:385], func=AF.Abs), reads=[f"ps{bE}"], writes=[f"dab{par}"])
                                A2("dve", lambda e: e.tensor_scalar_max(out=dab[par][:], in0=dab[par][:], scalar1=1.0), reads=[f"dab{par}"], writes=[f"dab{par}"])
                                A2("dve", lambda e: e.reciprocal(out=rden[par][:], in_=dab[par][:]), reads=[f"dab{par}"], writes=[f"rden{par}"])
                                if dirn == 1:
                                    A2("act", lambda e: e.activation(out=hbb[p][:, hc_], in_=ps[bN][:, :], func=AF.Copy, scale=rden[par][:, 0:1]),
                                       reads=[f"ps{bN}", f"rden{par}"], writes=[f"hbb{p}"])
                                else:
                                    A2("dve", lambda e: e.scalar_tensor_tensor(out=hbb[p][:, hc_], in0=ps[bN][:, :], scalar=rden[par][:, 0:1], in1=hbb[p][:, hc_],
                                                                              op0=ALU.mult, op1=ALU.add),
                                       reads=[f"ps{bN}", f"rden{par}", f"hbb{p}"], writes=[f"hbb{p}"])
                            for half, bC in ((0, bC0), (1, bC1)):
                                A2("dve", lambda e, half=half, bC=bC: e.scalar_tensor_tensor(out=C[:, 2 * h + half, :], in0=C[:, 2 * h + half, :], scalar=dec[:, blk, gl:gl + 1],
                                                                                          in1=ps[bC][:, :], op0=ALU.mult, op1=ALU.add),
                                   reads=[f"C{h}", "dec", f"ps{bC}"], writes=[f"C{h}"])
                            A2("dve", lambda e: e.scalar_tensor_tensor(out=nst[:, 2 * h:2 * h + 2], in0=nst[:, 2 * h:2 * h + 2], scalar=dec[:, blk, gl:gl + 1],
                                                                      in1=ps[bE][:, 386:388], op0=ALU.mult, op1=ALU.add),
                               reads=["nst", "dec", f"ps{bE}"], writes=["nst"])
                            A2("act", lambda e: e.copy(out=Cb[:, 2 * h, :], in_=C[:, 2 * h, :]), reads=[f"C{h}"], writes=[f"Cb{h}"])
                            A2("pool", lambda e: e.tensor_copy(out=Cb[:, 2 * h + 1, :], in_=C[:, 2 * h + 1, :]), reads=[f"C{h}"], writes=[f"Cb{h}"])
                            A2("pool", lambda e: e.tensor_copy(out=nb[:, 2 * h:2 * h + 2], in_=nst[:, 2 * h:2 * h + 2]), reads=["nst"], writes=["nb"])

                        for st in (st0, st1, st2, st3):
                            for h in pair:
                                st(h)

                def sweep(dirn, outputs, post=None):
                    order = list(range(nblk)) if dirn == 0 else list(range(nblk - 1, -1, -1))
                    load_blk(0, order[0], outputs, dirn)
                    for pos, blk in enumerate(order):
                        if pos + 1 < len(order):
                            load_blk(pos + 1, order[pos + 1], outputs, dirn)
                        scan_block(pos, blk, dirn, outputs)
                        if post is not None:
                            post(pos, blk)

                def store_state(dirn):
                    o = dirn * SW
                    A2("pool", lambda e: [e.dma_start(out=sall[ex * 128:(ex + 1) * 128, o:o + 4096], in_=C2),
                                          e.dma_start(out=sall[ex * 128:(ex + 1) * 128, o + 4096:o + 4104], in_=nst[:]),
                                          e.dma_start(out=sall[ex * 128:(ex + 1) * 128, o + 4104:o + 4112], in_=gexp[:, dirn * 8:dirn * 8 + 8])],
                       reads=Cres + ["nst", "gexp"], writes=["d_sloc"], dma_sem="st_S", ndma=3)

                def combine(dirn):
                    zero_state()
                    order = list(range(nextra)) if dirn == 0 else list(range(nextra - 1, -1, -1))
                    o = dirn * SW
                    for n_, i in enumerate(order):
                        pp = n_ % 2
                        A2("sp", lambda e, i=i, pp=pp: e.dma_start(out=sstg[pp][:], in_=sall[i * 128:(i + 1) * 128, o:o + SW]),
                           reads=["d_sall"], writes=[f"sstg{pp}"], dma_sem=f"ld_S{pp}")
                        a_ = prd[:, dirn * 8 + i:dirn * 8 + i + 1]
                        A2("dve", lambda e, pp=pp, a_=a_: e.tensor_scalar(out=coef[:], in0=sstg[pp][:, 4108:4112], scalar1=-1.0, scalar2=a_, op0=ALU.add, op1=ALU.mult),
                           reads=[f"sstg{pp}", "prd"], writes=["coef"])
                        A2("dve", lambda e: e.tensor_scalar_add(out=coef[:], in0=coef[:], scalar1=1.0), reads=["coef"], writes=["coef"])
                        for hh in range(8):
                            h = hh // 2
                            A2("dve", lambda e, hh=hh, h=h: e.tensor_scalar(out=C[:, hh, :], in0=C[:, hh, :], scalar1=coef[:, h:h + 1], scalar2=None, op0=ALU.mult),
                               reads=[f"C{h}", "coef"], writes=[f"C{h}"])
                            A2("dve", lambda e, hh=hh, h=h, pp=pp, a_=a_: e.scalar_tensor_tensor(out=C[:, hh, :], in0=sstg[pp][:, hh * 512:(hh + 1) * 512], scalar=a_, in1=C[:, hh, :],
                                                                                             op0=ALU.mult, op1=ALU.add),
                               reads=[f"C{h}", f"sstg{pp}", "prd"], writes=[f"C{h}"])
                        for h in range(4):
                            A2("dve", lambda e, h=h: e.tensor_scalar(out=nst[:, 2 * h:2 * h + 2], in0=nst[:, 2 * h:2 * h + 2], scalar1=coef[:, h:h + 1], scalar2=None, op0=ALU.mult),
                               reads=["nst", "coef"], writes=["nst"])
                        A2("dve", lambda e, pp=pp, a_=a_: e.scalar_tensor_tensor(out=nst[:], in0=sstg[pp][:, 4096:4104], scalar=a_, in1=nst[:], op0=ALU.mult, op1=ALU.add),
                           reads=["nst", f"sstg{pp}", "prd"], writes=["nst"])
                    cast_state()

                def local_states_pass():
                    sufpre = sb2("sufpre", [128, NBLK, 8])
                    wkL = sb2("wkL", [128, NBLK, 8])
                    ktl = [sb2(f"ktl{i}", [128, 256], BF16) for i in range(4)]
                    A2("dve", lambda e: e.memset(sufpre[:].rearrange("p b j -> p (b j)"), 0.0), writes=["sufpre"])
                    for b_ in range(nblk - 2, -1, -1):
                        A2("dve", lambda e, b_=b_: e.tensor_tensor(out=sufpre[:, b_, 0:4], in0=sufpre[:, b_ + 1, 0:4], in1=tot[:, b_ + 1, 4:8], op=ALU.add),
                           reads=["sufpre", "tot"], writes=["sufpre"])
                    for b_ in range(1, nblk):
                        A2("dve", lambda e, b_=b_: e.tensor_tensor(out=sufpre[:, b_, 4:8], in0=sufpre[:, b_ - 1, 4:8], in1=tot[:, b_ - 1, 12:16], op=ALU.add),
                           reads=["sufpre", "tot"], writes=["sufpre"])
                    A2("act", lambda e: e.activation(out=sufpre[:, 0:nblk, :], in_=sufpre[:, 0:nblk, :], func=AF.Exp), reads=["sufpre"], writes=["sufpre"])
                    A2("dve", lambda e: e.tensor_tensor(out=wkL[:, 0:nblk, :], in0=wk[:, 0:nblk, :], in1=sufpre[:, 0:nblk, :], op=ALU.mult),
                       reads=["wk", "sufpre"], writes=["wkL"])
                    pairs = [(h, d) for d in range(2) for h in range(4)]
                    groups = [pairs[0:3], pairs[3:6], pairs[6:8]]
                    kt_i = 0
                    pos = 0
                    for grp in groups:
                        load_blk(pos, 0, False, 0)
                        for blk in range(nblk):
                            if blk + 1 < nblk:
                                load_blk(pos + 1, blk + 1, False, 0)
                            p = pos % 2
                            for j, (h, d) in enumerate(grp):
                                kb_ = kt_i % 4
                                kt_i += 1
                                A2("dve", lambda e, kb_=kb_, h=h, d=d, blk=blk, p=p: e.tensor_scalar(
                                    out=ktl[kb_][:], in0=ktb[p][:, h * 256:(h + 1) * 256], scalar1=wkL[:, blk, d * 4 + h:d * 4 + h + 1], scalar2=None, op0=ALU.mult),
                                   reads=[f"ktb{p}", "wkL"], writes=[f"ktl{kb_}"])
                                for half in range(2):
                                    A2("pe", lambda e, kb_=kb_, h=h, j=j, half=half, blk=blk, p=p: e.matmul(
                                        ps[2 * j + half][:, :], ktl[kb_][:, half * 128:(half + 1) * 128], vtb[p][:, h * 512:(h + 1) * 512],
                                        start=(blk == 0), stop=(blk == nblk - 1)),
                                       reads=[f"ktl{kb_}", f"vtb{p}"], writes=[f"ps{2 * j + half}"])
                                    nbk = 6 + blk % 2
                                    A2("pe", lambda e, kb_=kb_, j=j, half=half, nbk=nbk: e.matmul(
                                        ps[nbk][:, 2 * j + half:2 * j + half + 1], ktl[kb_][:, half * 128:(half + 1) * 128], onesb[:, 0:1],
                                        start=True, stop=True),
                                       reads=[f"ktl{kb_}", "onesb"], writes=[f"ps{nbk}"])
                            nbk = 6 + blk % 2
                            if blk == 0:
                                A2("dve", lambda e, nbk=nbk, ng=len(grp): e.tensor_copy(out=nst[:, 0:2 * ng], in_=ps[nbk][:, 0:2 * ng]),
                                   reads=[f"ps{nbk}"], writes=["nst"])
                            else:
                                A2("dve", lambda e, nbk=nbk, ng=len(grp): e.tensor_tensor(out=nst[:, 0:2 * ng], in0=nst[:, 0:2 * ng], in1=ps[nbk][:, 0:2 * ng], op=ALU.add),
                                   reads=[f"ps{nbk}", "nst"], writes=["nst"])
                            pos += 1
                        for j, (h, d) in enumerate(grp):
                            for half in range(2):
                                A2("act" if half == 0 else "dve",
                                   (lambda e, j=j, half=half: e.copy(out=C[:, 2 * j + half, :], in_=ps[2 * j + half][:, :])) if half == 0 else
                                   (lambda e, j=j, half=half: e.tensor_copy(out=C[:, 2 * j + half, :], in_=ps[2 * j + half][:, :])),
                                   reads=[f"ps{2 * j + half}"], writes=[f"C{j}"])

                        def st_fn(e, grp=grp):
                            r = []
                            for j, (h, d) in enumerate(grp):
                                o = d * SW
                                r.append(e.dma_start(out=sall[ex * 128:(ex + 1) * 128, o + 2 * h * 512:o + (2 * h + 2) * 512],
                                                     in_=C[:, 2 * j:2 * j + 2, :].rearrange("p a b -> p (a b)")))
                                r.append(e.dma_start(out=sall[ex * 128:(ex + 1) * 128, o + 4096 + 2 * h:o + 4096 + 2 * h + 2], in_=nst[:, 2 * j:2 * j + 2]))
                            return r
                        A2("pool", st_fn, reads=[f"C{j}" for j in range(len(grp))] + ["nst"], writes=["d_sall"], dma_sem="st_S", ndma=2 * len(grp))
                    A2("pool", lambda e: [e.dma_start(out=sall[ex * 128:(ex + 1) * 128, 4104:4112], in_=gexp[:, 0:8]),
                                          e.dma_start(out=sall[ex * 128:(ex + 1) * 128, SW + 4104:SW + 4112], in_=gexp[:, 8:16])],
                       reads=["gexp"], writes=["d_sall"], dma_sem="st_S", ndma=2)

                if mode == "local":
                    local_states_pass()
                else:
                    combine(1)

                def post_b(pos, blk):
                    p = pos % 2
                    A2("pool", lambda e: e.dma_start(out=hbs.ap()[blk], in_=hbb[p][:]), reads=[f"hbb{p}"], writes=[f"d_hbs{blk}"], dma_sem=f"st_hb{p}")
                if mode == "main":
                    sweep(1, True, post_b)

                if mode == "main":
                    combine(0)

                def post_f(pos, blk):
                    p = pos % 2
                    A2("dve", lambda e: e.memset(ssq[:], 0.0), writes=["ssq"])
                    for h in range(4):
                        hc_ = slice(h * 512, (h + 1) * 512)
                        A2("act", lambda e, h=h, hc_=hc_: e.activation(out=junk[:], in_=hbb[p][:, hc_], func=AF.Square, accum_out=ssq[:, h:h + 1]),
                           reads=[f"hbb{p}", "ssq"], writes=["junk", "ssq"])
                    A2("dve", lambda e: e.tensor_scalar(out=rn[:], in0=ssq[:], scalar1=1.0 / DV, scalar2=EPS, op0=ALU.mult, op1=ALU.add), reads=["ssq"], writes=["rn"])
                    A2("act", lambda e: e.activation(out=rn[:], in_=rn[:], func=AF.Sqrt), reads=["rn"], writes=["rn"])
                    A2("dve", lambda e: e.reciprocal(out=rn[:], in_=rn[:]), reads=["rn"], writes=["rn"])
                    for h in range(4):
                        hc_ = slice(h * 512, (h + 1) * 512)
                        t2 = h % 2
                        A2("dve", lambda e, h=h, hc_=hc_, t2=t2: e.scalar_tensor_tensor(out=tmpn[t2][:], in0=hbb[p][:, hc_], scalar=rn[:, h:h + 1], in1=mnt[:, hc_],
                                                                                   op0=ALU.mult, op1=ALU.mult),
                           reads=[f"hbb{p}", "rn", "mnt"], writes=[f"tmpn{t2}"])
                        A2("pool", lambda e, hc_=hc_, t2=t2: e.tensor_tensor(out=gtb[:, hc_], in0=tmpn[t2][:], in1=sob[p][:, hc_], op=ALU.mult),
                           reads=[f"tmpn{t2}", f"sob{p}"], writes=["gtb"])
                    for half8 in range(2):
                        bank = 1 if half8 == 0 else 5
                        pb = ps[bank][:].bitcast(BF16)
                        for j in range(8):
                            jj = half8 * 8 + j
                            A2("pe", lambda e, j=j, jj=jj, pb=pb: e.transpose(pb[:, j * 128:(j + 1) * 128], gtb[:, jj * 128:(jj + 1) * 128], identb[:]),
                               reads=["gtb", "identb"], writes=[f"ps{bank}"])
                        A2("act" if half8 == 0 else "dve",
                           (lambda e, pb=pb, half8=half8: e.copy(out=gTst[p][:, half8 * 8:(half8 + 1) * 8, :], in_=pb[:, 0:1024].rearrange("p (a b) -> p a b", a=8))) if half8 == 0 else
                           (lambda e, pb=pb, half8=half8: e.tensor_copy(out=gTst[p][:, half8 * 8:(half8 + 1) * 8, :], in_=pb[:, 0:1024].rearrange("p (a b) -> p a b", a=8))),
                           reads=[f"ps{bank}"], writes=[f"gTst{p}"])
                    A2("pool", lambda e: e.dma_start(out=gTd[:, :, blk * 128:(blk + 1) * 128], in_=gTst[p][:]), reads=[f"gTst{p}"], writes=["d_gTd"], dma_sem=f"st_gT{p}")
                if mode == "main":
                    sweep(0, True, post_f)

                S2.finalize()
                with nc.Block() as blk2:
                    S2.emit(nc, blk2, sems, base)

        for ex_ in range(nextra):
            run_block1(ex_, True, ex_ == 0)
            run_block2("local", ex_)
        run_block1(3, False, nextra == 0)
        if not do_phase2:
            return nc
        run_block2("main", None)
        if p2stage < 6:
            return nc
        with ExitStack() as es3:
            new_block_sems()
            def sb3(name, shape, dt=F32):
                return es3.enter_context(nc.sbuf_tensor(name, list(shape), dt))
            xt = sb3("xt3", [128, KC, T])
            zt = sb3("zt3", [128, KC, T], BF16)
            hid = sb3("hid3", [128, FC, T], BF16)
            wf = [sb3(f"wf3_{i}", [128, 2048]) for i in range(WStream.NF)]
            wb = [sb3(f"wb3_{i}", [128, 2048], BF16) for i in range(WStream.NB)]
            rstd = sb3("rstd3", [128, T])
            tmp = [sb3(f"tmp3_{i}", [128, T]) for i in range(2)]
            sg = [sb3(f"sg3_{i}", [128, T]) for i in range(2)]
            plan3 = []
            for i in range(ntiles):
                for m in range(KC):
                    plan3.append((mwout_r[:, :, m * 128:(m + 1) * 128], 16, 128))
                plan_ffn(1, plan3)
            S3 = Sched()
            W3 = WStream(S3, wf, wb, plan3)
            rmsnorm_mod, mm_slab, ffn, nbank = make_helpers(S3, W3, xt, zt, hid, rstd, None, tmp, None, sg, None, None)
            for i in range(ntiles):
                c0 = i * T
                S3.add("pool", lambda e, c0=c0: e.dma_start(out=xt[:], in_=x1s_r[:, :, c0:c0 + T]),
                       writes=[f"x{k}" for k in range(KC)], dma_sem="x")
                S3.add("pool", lambda e, c0=c0: e.dma_start(out=zt[:], in_=gTd[:, :, c0:c0 + T]),
                       writes=[f"z{k}" for k in range(KC)], dma_sem="xh")
                for m in range(KC):
                    b = nbank()
                    wv, wres = W3.next(16, 128)
                    mm_slab(b, wv, wres, lambda k: zt[:, k, :], lambda k: f"z{k}", 16, True, True)
                    S3.add("dve", lambda e, b=b, m=m: e.scalar_tensor_tensor(
                        out=xt[:, m, :], in0=ps[b][:, :], scalar=modt[:, 96 + 32 + m:96 + 33 + m], in1=xt[:, m, :], op0=ALU.mult, op1=ALU.add),
                        reads=[f"ps{b}", f"x{m}", "modt"], writes=[f"x{m}"])
                rmsnorm_mod(avec[:, 48:64], modt[:, 96 + 48:96 + 64])
                ffn(1, modt[:, 96 + 80:96 + 96])
                rmsnorm_mod(avec[:, 64:80], modt[:, 192:208], outx=True)
                S3.add("pool", lambda e, c0=c0: e.dma_start(out=yT_r[:, :, c0:c0 + T], in_=xt[:]),
                       reads=[f"x{k}" for k in range(KC)], writes=["d_y"], dma_sem="st_y")
            assert W3.used == len(plan3)
            S3.finalize()
            with nc.Block() as blk3:
                S3.emit(nc, blk3, sems, base)
    return nc


def _prep_inputs(inp):
    f32 = np.float32
    xp = np.asarray(inp["x_prompt"], f32)[0]
    xs = np.asarray(inp["x_sample"], f32)
    cp = np.asarray(inp["c_prompt"], f32)
    csm = np.asarray(inp["c_sample"], f32)
    seqs = [xp, xs[0], xs[1]]
    cvecs = [cp[0], csm[0], csm[1]]
    core_seq = [0, 0, 0, 0, 1, 1, 2, 2]
    core_off = [0, 4096, 8192, 12288, 0, 4096, 0, 4096]

    def fm(v):
        return np.ascontiguousarray(np.asarray(v, f32).reshape(-1, 128).T)

    norm_g = np.asarray(inp["norm_g"], f32)
    smallv = np.concatenate([
        fm(norm_g.reshape(-1)), fm(np.asarray(inp["ada_b"], f32).reshape(-1)),
        fm(inp["final_ada_b"]), fm(inp["final_g"]), fm(np.asarray(inp["conv_w"], f32).reshape(-1))], axis=1)
    assert smallv.shape == (128, 352)
    mnorm = np.ascontiguousarray(np.broadcast_to(np.asarray(inp["mlstm_norm"], f32).reshape(1, D), (128, D)))
    bgate = np.asarray(inp["mlstm_b_gate"], f32).reshape(16, 1)
    gsel = np.zeros((16, 2), f32)
    gsel[[0, 1, 2, 3, 8, 9, 10, 11], 0] = 1.0
    gsel[[4, 5, 6, 7, 12, 13, 14, 15], 1] = 1.0
    s_idx = np.arange(128)[:, None]
    t_idx = np.arange(128)[None, :]
    U = (s_idx <= t_idx).astype(f32)
    consts = np.concatenate([np.eye(128, dtype=f32), U, (1.0 - U) * NEG, U.T, (1.0 - U.T) * NEG], axis=1).astype(f32)
    shared = dict(
        smallv=smallv, mnorm=mnorm, bgate=bgate, gsel=gsel, consts=consts,
        ada_w=np.asarray(inp["ada_w"], f32), final_ada_w=np.asarray(inp["final_ada_w"], f32),
        ffn_w_in=np.asarray(inp["ffn_w_in"], f32), ffn_w_out=np.asarray(inp["ffn_w_out"], f32),
        conv_w_in=np.asarray(inp["conv_w_in"], f32)[0], conv_w_out=np.asarray(inp["conv_w_out"], f32)[0],
        mlstm_w_in=np.asarray(inp["mlstm_w_in"], f32)[0], mlstm_w_out=np.asarray(inp["mlstm_w_out"], f32)[0])
    def chunk_arrays(sq, o):
        xT = np.ascontiguousarray(sq[o:o + NTOK].T)
        halo = np.zeros((NTILE, 2, D), f32)
        em = np.ones((NTILE, 2), f32)
        for i in range(NTILE):
            l = o + i * T - 1
            r = o + (i + 1) * T
            if l >= 0:
                halo[i, 0] = sq[l]
            else:
                em[i, 0] = 0.0
            if r < sq.shape[0]:
                halo[i, 1] = sq[r]
            else:
                em[i, 1] = 0.0
        xh = np.ascontiguousarray(halo.reshape(NTILE, 2, KC, 128).transpose(3, 0, 2, 1).reshape(128, NTILE * KC * 2))
        emask = np.ascontiguousarray(np.broadcast_to(em.reshape(1, NTILE * 2), (128, NTILE * 2)))
        return xT, xh, emask

    cache = {}
    for c in range(NCORE):
        cache[c] = chunk_arrays(seqs[core_seq[c]], core_off[c])
    in_maps = []
    for c in range(NCORE):
        others = [c2 for c2 in range(NCORE) if core_seq[c2] == core_seq[c] and c2 != c]
        pf = np.zeros(8, f32)
        pb = np.zeros(8, f32)
        ex = []
        for e_ in range(3):
            if e_ < len(others):
                c2 = others[e_]
                if c2 < c:
                    pf[e_] = 1.0
                else:
                    pb[e_] = 1.0
            else:
                c2 = others[0]
            ex.append(c2)
        pred = np.ascontiguousarray(np.broadcast_to(np.concatenate([pf, pb]).reshape(1, 16), (128, 16)))
        xT, xh, emask = cache[c]
        m = dict(shared)
        m.update(xT=xT, xhalo=xh, emask=emask, cvec=fm(cvecs[core_seq[c]]), pred=pred,
                 xTe=np.stack([cache[c2][0] for c2 in ex]), xhalo_e=np.stack([cache[c2][1] for c2 in ex]),
                 emask_e=np.stack([cache[c2][2] for c2 in ex]))
        in_maps.append(m)
    return in_maps, core_seq, core_off


def kernel(**inputs):
    in_maps, core_seq, core_off = _prep_inputs(inputs)
    nc = build_nc()
    res = run_bass_kernel_spmd(nc, in_maps, core_ids=list(range(NCORE)))
    yp = np.empty((1, 16384, D), np.float32)
    ys = np.empty((2, 8192, D), np.float32)
    for c in range(NCORE):
        y = np.asarray(res.results[c]["yT"], np.float32).T
        o = core_off[c]
        if core_seq[c] == 0:
            yp[0, o:o + NTOK] = y
        else:
            ys[core_seq[c] - 1, o:o + NTOK] = y
    return (yp, ys)
```
